# Optimizing a Trainium2 kernel written in Bass

```python
import math
import jax, jax.numpy as jnp
from jax import lax
import numpy as np

D_MODEL = 4096
BATCH = 1
SEQ = 8192
DEPTH = 2

HEAD_DIM = 128
MIX_WIDTH = D_MODEL
MEM_WIDTH = MIX_WIDTH // 4
SELF_WIDTH = MIX_WIDTH - MEM_WIDTH
MOBA_HEADS = SELF_WIDTH // HEAD_DIM
DIFF_HEADS = SELF_WIDTH // (2 * HEAD_DIM)
MEM_HEADS = 4
MEM_HEAD_DIM = MEM_WIDTH // MEM_HEADS
MEM_LEN = 256
PROJ_WIDTH = 3 * SELF_WIDTH + MEM_WIDTH
ROT_DIM = HEAD_DIM // 4
ROPE_THETA = 500000.0
MOBA_BLOCK = 256
MOBA_TOPK = 3
MOBA_Q_CHUNK = 32
ATTN_Q_BLOCK = 128
_FF_RAW = (8 * D_MODEL + 2) // 3
D_FF = ((_FF_RAW + 255) // 256) * 256
N_DIFF = DEPTH // 2
NORM_EPS = 1e-6
SUBLN_EPS = 1e-5

kernel_name = 'hybrid_moba_diffattn_memory_block'

f32 = jnp.float32


def _rms(x, g, eps=NORM_EPS):
    xf = x.astype(f32)
    y = xf * lax.rsqrt(jnp.mean(xf * xf, axis=-1, keepdims=True) + eps)
    return (y * g.astype(f32)).astype(x.dtype)


def _rope_tables(positions):
    inv = ROPE_THETA ** (-jnp.arange(0, ROT_DIM, 2, dtype=f32) / ROT_DIM)
    ang = positions.astype(f32)[..., None] * inv
    return jnp.cos(ang), jnp.sin(ang)


def _partial_rope(x, cos, sin):
    c = cos[:, :, None, :].astype(x.dtype)
    s = sin[:, :, None, :].astype(x.dtype)
    half = ROT_DIM // 2
    x1 = x[..., :half]
    x2 = x[..., half:ROT_DIM]
    return jnp.concatenate([x1 * c - x2 * s, x2 * c + x1 * s, x[..., ROT_DIM:]], axis=-1)


def _moba_attention(q, k, v):
    B, S, H, D = q.shape
    n_blk = -(-S // MOBA_BLOCK)
    pad = n_blk * MOBA_BLOCK - S
    n_sel = min(MOBA_TOPK, max(n_blk - 1, 1))
    scale = HEAD_DIM ** -0.5
    widths = ((0, 0), (0, pad), (0, 0), (0, 0))
    kb = jnp.pad(k, widths).reshape(B, n_blk, MOBA_BLOCK, H, D).transpose(0, 3, 1, 2, 4)
    vb = jnp.pad(v, widths).reshape(B, n_blk, MOBA_BLOCK, H, D).transpose(0, 3, 1, 2, 4)
    counts = jnp.clip(S - jnp.arange(n_blk) * MOBA_BLOCK, 1, MOBA_BLOCK).astype(f32)
    k_mean = (kb.astype(f32).sum(axis=3) / counts[None, None, :, None]).astype(q.dtype)
    n_chunk = S // MOBA_Q_CHUNK
    q_chunks = jnp.moveaxis(q.transpose(0, 2, 1, 3).reshape(B, H, n_chunk, MOBA_Q_CHUNK, D), 2, 0)
    b_idx = jnp.arange(B)[:, None, None, None]
    h_idx = jnp.arange(H)[None, :, None, None]
    blk_ids = jnp.arange(n_blk)

    def chunk(args):
        qc, c = args
        start = c * MOBA_Q_CHUNK
        b0 = start // MOBA_BLOCK
        t = start + jnp.arange(MOBA_Q_CHUNK)
        gate = jnp.einsum('bhqd,bhnd->bhqn', qc, k_mean).astype(f32)
        gate = jnp.where(blk_ids < b0, gate, -jnp.inf)
        _, sel = lax.top_k(gate, n_sel)
        valid = sel < b0
        k_sel = kb[b_idx, h_idx, sel]
        v_sel = vb[b_idx, h_idx, sel]
        s_sel = jnp.einsum('bhqd,bhqnkd->bhqnk', qc, k_sel).astype(f32) * scale
        s_sel = jnp.where(valid[..., None], s_sel, -jnp.inf).reshape(B, H, MOBA_Q_CHUNK, n_sel * MOBA_BLOCK)
        k_own = lax.dynamic_index_in_dim(kb, b0, axis=2, keepdims=False)
        v_own = lax.dynamic_index_in_dim(vb, b0, axis=2, keepdims=False)
        s_own = jnp.einsum('bhqd,bhkd->bhqk', qc, k_own).astype(f32) * scale
        key_pos = b0 * MOBA_BLOCK + jnp.arange(MOBA_BLOCK)
        s_own = jnp.where(key_pos[None, :] <= t[:, None], s_own, -jnp.inf)
        p = jax.nn.softmax(jnp.concatenate([s_sel, s_own], axis=-1), axis=-1).astype(v.dtype)
        p_sel = p[..., :n_sel * MOBA_BLOCK].reshape(B, H, MOBA_Q_CHUNK, n_sel, MOBA_BLOCK)
        p_own = p[..., n_sel * MOBA_BLOCK:]
        return (jnp.einsum('bhqnk,bhqnkd->bhqd', p_sel, v_sel)
                + jnp.einsum('bhqk,bhkd->bhqd', p_own, v_own))

    out = lax.map(chunk, (q_chunks, jnp.arange(n_chunk)))
    out = jnp.moveaxis(out, 0, 2).reshape(B, H, S, D).transpose(0, 2, 1, 3)
    return out.reshape(B, S, H * D)


def _diff_attention(q, k, v, lam):
    B, S, H, _, D = q.shape
    scale = D ** -0.5
    n_blk = S // ATTN_Q_BLOCK
    q_blocks = jnp.moveaxis(q.reshape(B, n_blk, ATTN_Q_BLOCK, H, 2, D), 1, 0)
    key_pos = jnp.arange(S)

    def block(args):
        qb, c = args
        t = c * ATTN_Q_BLOCK + jnp.arange(ATTN_Q_BLOCK)
        s = jnp.einsum('bqhcd,bkhcd->bhcqk', qb, k).astype(f32) * scale
        s = jnp.where(key_pos[None, :] <= t[:, None], s, -jnp.inf)
        p = jax.nn.softmax(s, axis=-1)
        a = p[:, :, 0] - lam * p[:, :, 1]
        return jnp.einsum('bhqk,bkhe->bqhe', a.astype(v.dtype), v)

    out = lax.map(block, (q_blocks, jnp.arange(n_blk)))
    return jnp.moveaxis(out, 0, 1).reshape(B, S, H, 2 * D)


def _mem_attention(qm, km, vm):
    B, S = qm.shape[:2]
    s = jnp.einsum('bqhd,bmhd->bhqm', qm, km).astype(f32) * (MEM_HEAD_DIM ** -0.5)
    p = jax.nn.softmax(s, axis=-1).astype(vm.dtype)
    return jnp.einsum('bhqm,bmhd->bqhd', p, vm).reshape(B, S, MEM_WIDTH)


def setup_inputs(seed: int = 0) -> dict:
    key = jax.random.key(seed)
    ks = jax.random.split(key, 24)

    def nrm(k, shape, scale):
        return jax.random.normal(k, shape, f32) * scale

    def gain(k, shape):
        return 1.0 + 0.02 * jax.random.normal(k, shape, f32)

    return {
        'x': nrm(ks[0], (BATCH, SEQ, D_MODEL), 1.0),
        'mem': nrm(ks[1], (BATCH, MEM_LEN, D_MODEL), 1.0),
        'positions': jnp.broadcast_to(jnp.arange(SEQ, dtype=jnp.int32)[None, :], (BATCH, SEQ)),
        'g_attn_norm': gain(ks[2], (DEPTH, D_MODEL)),
        'w_in': nrm(ks[3], (DEPTH, D_MODEL, PROJ_WIDTH), D_MODEL ** -0.5),
        'w_out': nrm(ks[4], (DEPTH, MIX_WIDTH, D_MODEL), MIX_WIDTH ** -0.5),
        'g_qnorm': gain(ks[5], (DEPTH, HEAD_DIM)),
        'g_knorm': gain(ks[6], (DEPTH, HEAD_DIM)),
        'g_mem_qnorm': gain(ks[7], (DEPTH, MEM_HEAD_DIM)),
        'g_mem_knorm': gain(ks[8], (DEPTH, MEM_HEAD_DIM)),
        'g_mem_norm': gain(ks[9], (D_MODEL,)),
        'w_mem_kv': nrm(ks[10], (D_MODEL, 2 * MEM_WIDTH), D_MODEL ** -0.5),
        'lambda_q1': nrm(ks[11], (N_DIFF, HEAD_DIM), 0.1),
        'lambda_k1': nrm(ks[12], (N_DIFF, HEAD_DIM), 0.1),
        'lambda_q2': nrm(ks[13], (N_DIFF, HEAD_DIM), 0.1),
        'lambda_k2': nrm(ks[14], (N_DIFF, HEAD_DIM), 0.1),
        'g_subln': gain(ks[15], (N_DIFF, 2 * HEAD_DIM)),
        'g_ffn_norm': gain(ks[16], (DEPTH, D_MODEL)),
        'w_gate': nrm(ks[17], (DEPTH, D_MODEL, D_FF), D_MODEL ** -0.5),
        'w_up': nrm(ks[18], (DEPTH, D_MODEL, D_FF), D_MODEL ** -0.5),
        'w_down': nrm(ks[19], (DEPTH, D_FF, D_MODEL), D_FF ** -0.5),
    }


def reference(x, mem, positions, g_attn_norm, w_in, w_out, g_qnorm, g_knorm,
              g_mem_qnorm, g_mem_knorm, g_mem_norm, w_mem_kv, lambda_q1, lambda_k1,
              lambda_q2, lambda_k2, g_subln, g_ffn_norm, w_gate, w_up, w_down):
    B, S, _ = x.shape
    M = mem.shape[1]
    cos, sin = _rope_tables(positions)
    mkv = _rms(mem, g_mem_norm) @ w_mem_kv
    mk_raw = mkv[..., :MEM_WIDTH].reshape(B, M, MEM_HEADS, MEM_HEAD_DIM)
    mv = mkv[..., MEM_WIDTH:].reshape(B, M, MEM_HEADS, MEM_HEAD_DIM)

    for i in range(DEPTH):
        h = _rms(x, g_attn_norm[i])
        proj = h @ w_in[i]
        qs = proj[..., :SELF_WIDTH]
        ks_ = proj[..., SELF_WIDTH:2 * SELF_WIDTH]
        vs = proj[..., 2 * SELF_WIDTH:3 * SELF_WIDTH]
        qm = proj[..., 3 * SELF_WIDTH:]
        if i % 2 == 0:
            q = _partial_rope(_rms(qs.reshape(B, S, MOBA_HEADS, HEAD_DIM), g_qnorm[i]), cos, sin)
            k = _partial_rope(_rms(ks_.reshape(B, S, MOBA_HEADS, HEAD_DIM), g_knorm[i]), cos, sin)
            v = vs.reshape(B, S, MOBA_HEADS, HEAD_DIM)
            self_out = _moba_attention(q, k, v)
        else:
            j = i // 2
            q = _partial_rope(_rms(qs.reshape(B, S, 2 * DIFF_HEADS, HEAD_DIM), g_qnorm[i]), cos, sin)
            k = _partial_rope(_rms(ks_.reshape(B, S, 2 * DIFF_HEADS, HEAD_DIM), g_knorm[i]), cos, sin)
            q = q.reshape(B, S, DIFF_HEADS, 2, HEAD_DIM)
            k = k.reshape(B, S, DIFF_HEADS, 2, HEAD_DIM)
            v = vs.reshape(B, S, DIFF_HEADS, 2 * HEAD_DIM)
            lam_init = 0.8 - 0.6 * math.exp(-0.3 * i)
            lam = (jnp.exp(jnp.sum(lambda_q1[j].astype(f32) * lambda_k1[j].astype(f32)))
                   - jnp.exp(jnp.sum(lambda_q2[j].astype(f32) * lambda_k2[j].astype(f32)))
                   + lam_init)
            o = _diff_attention(q, k, v, lam)
            o = _rms(o, g_subln[j], SUBLN_EPS) * (1.0 - lam_init)
            self_out = o.reshape(B, S, SELF_WIDTH)
        qmh = _rms(qm.reshape(B, S, MEM_HEADS, MEM_HEAD_DIM), g_mem_qnorm[i])
        kmh = _rms(mk_raw, g_mem_knorm[i])
        mem_out = _mem_attention(qmh, kmh, mv)
        x = x + jnp.concatenate([self_out, mem_out], axis=-1) @ w_out[i]
        f = _rms(x, g_ffn_norm[i])
        x = x + (jax.nn.silu(f @ w_gate[i]) * (f @ w_up[i])) @ w_down[i]
    return x
```

```python
import math
from contextlib import ExitStack

import numpy as np
import ml_dtypes

import concourse.bass as bass
import concourse.mybir as mybir
from concourse.bass_utils import run_bass_kernel_spmd

F32 = mybir.dt.float32
BF16 = mybir.dt.bfloat16
I32 = mybir.dt.int32
ALU = mybir.AluOpType
AF = mybir.ActivationFunctionType
AX = mybir.AxisListType

NCORES = 8
D = 4096
SEQ = 8192
NLOC = SEQ // NCORES
DC = D // 128
SELF_W = 3072
MEM_W = 1024
PROJ_W = 10240
DFF = 11008
FC = DFF // 128
MEM_LEN = 256
NORM_EPS = 1e-6
SUBLN_EPS = 1e-5
ROPE_THETA = 500000.0
LAM_INIT1 = 0.8 - 0.6 * math.exp(-0.3 * 1)
TWO_PI = 2.0 * math.pi
NDMASEM = 40


class Buf:
    def __init__(self, t, dsem=None):
        self.t = t
        self.w = None
        self.r = []
        self.dsem = dsem

    def __getitem__(self, k):
        return self.t[k]


class Sched:
    CE = ("pe", "act", "dve", "pool")

    def __init__(self, nc, stack):
        self.nc = nc
        self.sem = {}
        self.cnt = {}
        for e in self.CE:
            self.sem[e] = stack.enter_context(nc.semaphore("s_" + e))
            self.cnt[e] = 0
        for i in range(NDMASEM):
            self.sem[("d", i)] = stack.enter_context(nc.semaphore("d%d" % i))
            self.cnt[("d", i)] = 0
        self.sem["cc"] = stack.enter_context(nc.semaphore("ccs"))
        self.cnt["cc"] = 0
        self.free_d = [i for i in range(NDMASEM) if getattr(self.sem[("d", i)], "num", 0) != 192]
        self.phase_d = []
        self.ops = {e: [] for e in ("pe", "act", "dve", "pool", "sp")}
        self.waited = {e: {} for e in ("pe", "act", "dve", "pool", "sp")}
        self.pstack = None
        self.uid = 0

    def begin(self):
        self.pstack = ExitStack()
        self.pstack.__enter__()
        self.ops = {e: [] for e in self.ops}
        self.phase_d = []

    def sb(self, shape, dt, dma=False, name=None):
        self.uid += 1
        t = self.pstack.enter_context(self.nc.sbuf_tensor("%s_%d" % (name or "sb", self.uid), list(shape), dt))
        ds = None
        if dma:
            ds = self.free_d.pop()
            self.phase_d.append(ds)
        return Buf(t, ds)

    def ps(self, shape, dt=F32, name=None):
        self.uid += 1
        t = self.pstack.enter_context(self.nc.psum_tensor("%s_%d" % (name or "ps", self.uid), list(shape), dt))
        return Buf(t)

    def dr(self, t):
        return Buf(t)

    def end(self):
        nc = self.nc
        finals = [(k, v) for k, v in self.cnt.items() if v > 0]
        with nc.Block() as block:
            def emit(ename, eng):
                waited = self.waited[ename]
                for deps, fn, inc in self.ops[ename]:
                    for (k, v) in deps:
                        if k == ename and ename == "pe":
                            continue
                        if waited.get(k, 0) >= v:
                            continue
                        eng.wait_ge(self.sem[k], v)
                        waited[k] = v
                    ins = fn(eng)
                    if inc is not None:
                        ins.then_inc(self.sem[inc[0]], inc[1])
                for (k, v) in finals:
                    if waited.get(k, 0) >= v:
                        continue
                    eng.wait_ge(self.sem[k], v)
                    waited[k] = v

            @block.tensor
            def _(e):
                emit("pe", e)

            @block.scalar
            def _(e):
                emit("act", e)

            @block.vector
            def _(e):
                emit("dve", e)

            @block.gpsimd
            def _(e):
                emit("pool", e)

            @block.sync
            def _(e):
                emit("sp", e)
        self.free_d.extend(self.phase_d)
        self.phase_d = []
        self.pstack.__exit__(None, None, None)
        self.pstack = None

    def _deps(self, reads, writes):
        deps = []
        for b in reads:
            if b.w is not None:
                deps.append(b.w)
        for b in writes:
            if b.w is not None:
                deps.append(b.w)
            deps.extend(b.r)
        return deps

    def _commit(self, tok, reads, writes):
        for b in writes:
            b.w = tok
            b.r = []
        for b in reads:
            if b not in writes:
                b.r.append(tok)

    def op(self, eng, fn, reads=(), writes=()):
        deps = self._deps(reads, writes)
        self.cnt[eng] += 1
        tok = (eng, self.cnt[eng])
        self.ops[eng].append((deps, fn, (eng, 1)))
        self._commit(tok, reads, writes)
        return tok

    def dma(self, q, out_ap, in_ap, semb, reads=(), writes=()):
        import os
        if os.environ.get("KNOSTORE") == "1" and len(writes) == 0:
            return None
        deps = self._deps(reads, writes)
        k = ("d", semb.dsem)
        self.cnt[k] += 16
        tok = (k, self.cnt[k])
        self.ops[q].append((deps, lambda e: e.dma_start(out=out_ap, in_=in_ap), (k, 16)))
        self._commit(tok, reads, writes)
        return tok

    def allgather(self, in_t, out_t, reads, writes):
        deps = self._deps(reads, writes)
        self.cnt["cc"] += 1
        tok = ("cc", self.cnt["cc"])

        def fn(e):
            return e.collective_compute("AllGather", ALU.bypass, replica_groups=[list(range(NCORES))],
                                        ins=[in_t.ap().opt()], outs=[out_t.ap().opt()])
        self.ops["pool"].append((deps, fn, ("cc", 1)))
        self._commit(tok, reads, writes)
        return tok


def mm_group(S, out_ap, pairs, reads, psb):
    def fn(e):
        ins = None
        n = len(pairs)
        for i, (l, r) in enumerate(pairs):
            ins = e.matmul(out_ap, lhsT=l, rhs=r, start=(i == 0), stop=(i == n - 1))
        return ins
    return S.op("pe", fn, reads=reads, writes=[psb])


def load_consts(S, C):
    k = {}
    k["ones"] = S.sb([128, 128], BF16, dma=True, name="ones")
    S.dma("sp", k["ones"][:, :], C["ones"][:, :], k["ones"], writes=[k["ones"]])
    return k


def rstd_from_ss(S, ob, o_ap, sb_, s_ap, scale, eps):
    S.op("act", lambda e: e.activation(out=o_ap, in_=s_ap, func=AF.Sqrt, bias=float(eps), scale=float(scale)),
         reads=[sb_], writes=[ob])
    S.op("dve", lambda e: e.reciprocal(out=o_ap, in_=o_ap), reads=[ob], writes=[ob])


def phase_norm(S, xT_d, gcols_d, hT, ones, ss):
    xs = [S.sb([128, NLOC], F32, dma=True, name="xs") for _ in range(2)]
    sq = [S.sb([128, NLOC], BF16, name="sq") for _ in range(2)]
    gc = S.sb([128, DC], F32, dma=True, name="gc")
    rstd = S.sb([128, NLOC], F32, name="rstd")
    S.dma("sp", gc[:, :], gcols_d[:, :], gc, writes=[gc])
    for c in range(DC):
        x = xs[c % 2]
        q = sq[c % 2]
        S.dma("sp", x[:, :], xT_d[c * 128:(c + 1) * 128, :], x, writes=[x])
        S.op("act", lambda e, x=x, q=q: e.activation(out=q[:, :], in_=x[:, :], func=AF.Square), reads=[x], writes=[q])
        S.op("dve", lambda e, x=x, c=c: e.tensor_scalar(out=hT[:, c, :], in0=x[:, :], scalar1=gc[:, c:c + 1],
                                                        scalar2=None, op0=ALU.mult), reads=[x, gc], writes=[hT])
        for h in range(2):
            def fn(e, q=q, h=h, c=c):
                return e.matmul(ss[h][:, :], lhsT=ones[:, :], rhs=q[:, h * 512:(h + 1) * 512],
                                start=(c == 0), stop=(c == DC - 1))
            S.op("pe", fn, reads=[q, ones], writes=[ss[h]])
    for h in range(2):
        sl = slice(h * 512, (h + 1) * 512)
        rstd_from_ss(S, rstd, rstd[:, sl], ss[h], ss[h][:, :], 1.0 / D, NORM_EPS)
    for c in range(DC):
        S.op("dve", lambda e, c=c: e.tensor_tensor(out=hT[:, c, :], in0=hT[:, c, :], in1=rstd[:, :], op=ALU.mult),
             reads=[hT, rstd], writes=[hT])


def load_w(S, wb, w_view, c0, c1, f0, f1, nsplit=4):
    g = f0 // 256
    n = c1 - c0
    step = (n + nsplit - 1) // nsplit
    for s in range(0, n, step):
        e = min(n, s + step)
        S.dma("pool", wb[:, s:e, 0:f1 - f0], w_view[g, :, c0 + s:c0 + e, :], wb, writes=[wb])


def phase_inproj(S, T, C, lay):
    S.begin()
    K = load_consts(S, C)
    ones = K["ones"]
    hT = S.sb([128, DC, NLOC], BF16, name="hT")
    ssb = [S.ps([128, 512], name="ssq") for _ in range(2)]
    import os
    if os.environ.get("KNORM", "1") == "1":
        phase_norm(S, T["xT_in%d" % lay], C["g_attn%d" % lay], hT, ones, ssb)
    perm = S.sb([32, 32], BF16, dma=True, name="perm")
    S.dma("sp", perm[:, :], C["perm"][:, :], perm, writes=[perm])
    gq = S.sb([128, 6], F32, dma=True, name="gq")
    S.dma("sp", gq[:, :], C["gq%d" % lay][:, :], gq, writes=[gq])
    posi = S.sb([32, NLOC], I32, dma=True, name="posi")
    S.dma("sp", posi[:, :], C["pos32"][:, :], posi, writes=[posi])
    ang = S.sb([32, NLOC], F32, name="ang")
    tmp = S.sb([32, NLOC], F32, name="tmpa")
    cosT = S.sb([32, NLOC], F32, name="cosT")
    sinT = S.sb([32, NLOC], F32, name="sinT")
    S.op("dve", lambda e: e.tensor_copy(out=ang[:, :], in_=posi[:, :]), reads=[posi], writes=[ang])
    S.op("dve", lambda e: e.tensor_scalar(out=ang[:, :], in0=ang[:, :], scalar1=gq[0:32, 4:5], scalar2=None,
                                          op0=ALU.mult), reads=[ang, gq], writes=[ang])
    ki = S.sb([32, NLOC], I32, name="ki")
    kf = S.sb([32, NLOC], F32, name="kf")

    def sin_of(dst, offset):
        if offset != 0.0:
            S.op("dve", lambda e: e.tensor_scalar(out=tmp[:, :], in0=ang[:, :], scalar1=float(offset), scalar2=None,
                                                  op0=ALU.add), reads=[ang], writes=[tmp])
        else:
            S.op("dve", lambda e: e.tensor_copy(out=tmp[:, :], in_=ang[:, :]), reads=[ang], writes=[tmp])
        S.op("dve", lambda e: e.tensor_scalar(out=kf[:, :], in0=tmp[:, :], scalar1=1.0 / TWO_PI, scalar2=None,
                                              op0=ALU.mult), reads=[tmp], writes=[kf])
        S.op("dve", lambda e: e.tensor_copy(out=ki[:, :], in_=kf[:, :]), reads=[kf], writes=[ki])
        S.op("dve", lambda e: e.tensor_copy(out=kf[:, :], in_=ki[:, :]), reads=[ki], writes=[kf])
        S.op("dve", lambda e: e.scalar_tensor_tensor(out=tmp[:, :], in0=kf[:, :], scalar=-TWO_PI, in1=tmp[:, :],
                                                     op0=ALU.mult, op1=ALU.add), reads=[kf, tmp], writes=[tmp])
        S.op("dve", lambda e: e.tensor_scalar(out=kf[:, :], in0=tmp[:, :], scalar1=math.pi, scalar2=-TWO_PI,
                                              op0=ALU.is_gt, op1=ALU.mult), reads=[tmp], writes=[kf])
        S.op("dve", lambda e: e.tensor_tensor(out=tmp[:, :], in0=tmp[:, :], in1=kf[:, :], op=ALU.add),
             reads=[tmp, kf], writes=[tmp])
        S.op("dve", lambda e: e.tensor_scalar(out=kf[:, :], in0=tmp[:, :], scalar1=-math.pi, scalar2=TWO_PI,
                                              op0=ALU.is_lt, op1=ALU.mult), reads=[tmp], writes=[kf])
        S.op("dve", lambda e: e.tensor_tensor(out=tmp[:, :], in0=tmp[:, :], in1=kf[:, :], op=ALU.add),
             reads=[tmp, kf], writes=[tmp])
        S.op("dve", lambda e: e.tensor_scalar(out=kf[:, :], in0=tmp[:, :], scalar1=-1.0, scalar2=math.pi,
                                              op0=ALU.mult, op1=ALU.add), reads=[tmp], writes=[kf])
        S.op("dve", lambda e: e.tensor_tensor(out=kf[:, :], in0=kf[:, :], in1=tmp[:, :], op=ALU.min),
             reads=[tmp, kf], writes=[kf])
        S.op("dve", lambda e: e.tensor_scalar(out=tmp[:, :], in0=tmp[:, :], scalar1=-1.0, scalar2=-math.pi,
                                              op0=ALU.mult, op1=ALU.add), reads=[tmp], writes=[tmp])
        S.op("dve", lambda e: e.tensor_tensor(out=tmp[:, :], in0=kf[:, :], in1=tmp[:, :], op=ALU.max),
             reads=[tmp, kf], writes=[tmp])
        S.op("dve", lambda e: e.tensor_tensor(out=kf[:, :], in0=tmp[:, :], in1=tmp[:, :], op=ALU.mult),
             reads=[tmp], writes=[kf])
        cs = [-1.0 / 39916800.0, 1.0 / 362880.0, -1.0 / 5040.0, 1.0 / 120.0, -1.0 / 6.0]
        S.op("dve", lambda e: e.tensor_scalar(out=dst[:, :], in0=kf[:, :], scalar1=cs[0], scalar2=None, op0=ALU.mult),
             reads=[kf], writes=[dst])
        for cc in cs[1:]:
            S.op("dve", lambda e, cc=cc: e.scalar_tensor_tensor(out=dst[:, :], in0=dst[:, :], scalar=cc, in1=kf[:, :],
                                                                op0=ALU.add, op1=ALU.mult), reads=[dst, kf], writes=[dst])
        S.op("dve", lambda e: e.scalar_tensor_tensor(out=dst[:, :], in0=dst[:, :], scalar=1.0, in1=tmp[:, :],
                                                     op0=ALU.add, op1=ALU.mult), reads=[dst, tmp], writes=[dst])

    if os.environ.get("KTAB", "1") == "1":
        sin_of(sinT, 0.0)
        sin_of(cosT, 0.5 * math.pi)
    S.op("dve", lambda e: e.tensor_scalar(out=sinT[:, :], in0=sinT[:, :], scalar1=gq[0:32, 5:6], scalar2=None,
                                          op0=ALU.mult), reads=[sinT, gq], writes=[sinT])

    wv = T["w_in%d" % lay]
    wb = [S.sb([128, DC, 256], BF16, dma=True, name="wb") for _ in range(2)]
    pq = [S.ps([128, 512], name="pq") for _ in range(4)]
    prb = [S.ps([128, 512], name="prp") for _ in range(2)]
    sqb = [S.sb([128, 512], BF16, name="sqb") for _ in range(4)]
    qgb = [S.sb([128, 512], BF16, name="qgb") for _ in range(4)]
    rsb = [S.sb([128, 512], F32, name="rsb") for _ in range(2)]
    t1b = [S.sb([32, 512], F32, name="t1b") for _ in range(2)]
    t2b = [S.sb([32, 512], F32, name="t2b") for _ in range(2)]
    ob = [S.sb([128, 512], BF16, dma=True, name="ob") for _ in range(4)]
    vst = [S.sb([128, 8, 256], BF16, dma=True, name="vst") for _ in range(2)]
    vf32 = S.sb([128, 256], F32, name="vf32")
    praw = [S.sb([128, 512], F32, name="praw") for _ in range(2)]
    qT_d, kT_d, v_d, qmT_d = T["qT%d" % lay], T["kT_loc%d" % lay], T["v_loc%d" % lay], T["qmT%d" % lay]
    ctr = {"pq": 0, "t": 0, "ob": 0}

    def proj_tile(w, ch, half):
        p = pq[ctr["pq"] % 4]
        ctr["pq"] += 1
        pairs = [(w[:, c, ch * 128:(ch + 1) * 128], hT[:, c, half * 512:(half + 1) * 512]) for c in range(DC)]
        mm_group(S, p[:, :], pairs, [w, hT], p)
        return p

    import os
    for g in [int(t) for t in os.environ.get('KGROUPS', ','.join(str(i) for i in range(40))).split(',') if t != '']:
        w = wb[g % 2]
        load_w(S, w, wv, 0, DC, g * 256, (g + 1) * 256)
        if os.environ.get('KONLYLOAD') == '1':
            continue
        if g < 24:
            isq = g < 12
            gcol = 0 if isq else 1
            for ch in range(2):
                head = (g % 12) * 2 + ch
                for half in range(2):
                    hs = slice(half * 512, (half + 1) * 512)
                    p = proj_tile(w, ch, half)
                    i = ctr["t"] % 4
                    i2 = ctr["t"] % 2
                    ctr["t"] += 1
                    sqt, qg, rs, t1, t2, ssp, prp = sqb[i], qgb[i], rsb[i2], t1b[i2], t2b[i2], ssb[i2], prb[i2]
                    o = ob[ctr["ob"] % 4]
                    ctr["ob"] += 1
                    pr_ = praw[i2]
                    S.op("act", lambda e, p=p, pr_=pr_: e.activation(out=pr_[:, :], in_=p[:, :], func=AF.Copy),
                         reads=[p], writes=[pr_])
                    S.op("act", lambda e, pr_=pr_, sqt=sqt: e.activation(out=sqt[:, :], in_=pr_[:, :], func=AF.Square),
                         reads=[pr_], writes=[sqt])
                    S.op("dve", lambda e, pr_=pr_, qg=qg, gcol=gcol: e.tensor_scalar(
                        out=qg[:, :], in0=pr_[:, :], scalar1=gq[:, gcol:gcol + 1], scalar2=None, op0=ALU.mult),
                        reads=[pr_, gq], writes=[qg])
                    mm_group(S, ssp[:, :], [(ones[:, :], sqt[:, :])], [ones, sqt], ssp)
                    mm_group(S, prp[0:32, :], [(perm[:, :], qg[0:32, :])], [perm, qg], prp)
                    rstd_from_ss(S, rs, rs[:, :], ssp, ssp[:, :], 1.0 / 128.0, NORM_EPS)
                    S.op("dve", lambda e, t1=t1, qg=qg, hs=hs: e.tensor_tensor(
                        out=t1[:, :], in0=qg[0:32, :], in1=cosT[:, hs], op=ALU.mult), reads=[qg, cosT], writes=[t1])
                    S.op("dve", lambda e, t2=t2, prp=prp, hs=hs: e.tensor_tensor(
                        out=t2[:, :], in0=prp[0:32, :], in1=sinT[:, hs], op=ALU.mult), reads=[prp, sinT], writes=[t2])
                    S.op("dve", lambda e, t1=t1, t2=t2, qg=qg: e.tensor_tensor(
                        out=qg[0:32, :], in0=t1[:, :], in1=t2[:, :], op=ALU.add), reads=[t1, t2], writes=[qg])
                    S.op("dve", lambda e, o=o, qg=qg, rs=rs: e.scalar_tensor_tensor(
                        out=o[:, :], in0=qg[:, :], scalar=1.0, in1=rs[:, :], op0=ALU.mult, op1=ALU.mult),
                        reads=[qg, rs], writes=[o])
                    dst = (qT_d if isq else kT_d)[head, :, hs]
                    S.dma("sp", dst, o[:, :], o, reads=[o])
        elif g < 36:
            vs = vst[g % 2]
            for tt in range(8):
                p = pq[ctr["pq"] % 4]
                ctr["pq"] += 1
                pairs = [(hT[:, c, tt * 128:(tt + 1) * 128], w[:, c, 0:256]) for c in range(DC)]
                mm_group(S, p[:, 0:256], pairs, [w, hT], p)
                if os.environ.get('KMMONLY') == '1':
                    continue
                eng = os.environ.get("KVENG") or ("act" if tt % 2 == 0 else "dve")
                if eng == "act":
                    S.op("act", lambda e, p=p: e.activation(out=vf32[:, :], in_=p[:, 0:256], func=AF.Copy),
                         reads=[p], writes=[vf32])
                    S.op("dve", lambda e, vs=vs, tt=tt: e.tensor_copy(out=vs[:, tt, :], in_=vf32[:, :]),
                         reads=[vf32], writes=[vs])
                else:
                    S.op("dve", lambda e, p=p, vs=vs, tt=tt: e.tensor_copy(out=vs[:, tt, :], in_=p[:, 0:256]),
                         reads=[p], writes=[vs])
            c0 = (g - 24) * 256
            if os.environ.get('KMMONLY') == '1':
                continue
            S.dma("sp", v_d.rearrange("(t p) f -> p t f", p=128)[:, :, c0:c0 + 256], vs[:, :, :], vs, reads=[vs])
        else:
            hm = g - 36
            for half in range(2):
                hs = slice(half * 512, (half + 1) * 512)
                i2 = ctr["t"] % 2
                ssp, rs = ssb[i2], rsb[i2]
                qgs = []
                for ch in range(2):
                    p = proj_tile(w, ch, half)
                    i = ctr["t"] % 4
                    ctr["t"] += 1
                    sqt, qg = sqb[i], qgb[i]
                    qgs.append(qg)
                    pr_ = praw[ch]
                    S.op("act", lambda e, p=p, pr_=pr_: e.activation(out=pr_[:, :], in_=p[:, :], func=AF.Copy),
                         reads=[p], writes=[pr_])
                    S.op("act", lambda e, pr_=pr_, sqt=sqt: e.activation(out=sqt[:, :], in_=pr_[:, :], func=AF.Square),
                         reads=[pr_], writes=[sqt])
                    S.op("dve", lambda e, pr_=pr_, qg=qg, ch=ch: e.tensor_scalar(
                        out=qg[:, :], in0=pr_[:, :], scalar1=gq[:, 2 + ch:3 + ch], scalar2=None, op0=ALU.mult),
                        reads=[pr_, gq], writes=[qg])

                    def fn(e, ssp=ssp, sqt=sqt, ch=ch):
                        return e.matmul(ssp[:, :], lhsT=ones[:, :], rhs=sqt[:, :], start=(ch == 0), stop=(ch == 1))
                    S.op("pe", fn, reads=[ones, sqt], writes=[ssp])
                rstd_from_ss(S, rs, rs[:, :], ssp, ssp[:, :], 1.0 / 256.0, NORM_EPS)
                for ch in range(2):
                    o = ob[ctr["ob"] % 4]
                    ctr["ob"] += 1
                    qg = qgs[ch]
                    S.op("dve", lambda e, o=o, qg=qg, rs=rs: e.scalar_tensor_tensor(
                        out=o[:, :], in0=qg[:, :], scalar=1.0, in1=rs[:, :], op0=ALU.mult, op1=ALU.mult),
                        reads=[qg, rs], writes=[o])
                    S.dma("sp", qmT_d[hm * 2 + ch, :, hs], o[:, :], o, reads=[o])
    S.end()


def core_token_index(c):
    j = np.arange(8)[:, None]
    i = np.arange(128)[None, :]
    return (1024 * j + 128 * c + i).reshape(-1)


def host_consts(c, inputs):
    bf = ml_dtypes.bfloat16
    K = {}
    K["ones"] = np.ones((128, 128), bf)
    K["ident"] = np.eye(128, dtype=np.float32).astype(bf)
    perm = np.zeros((32, 32), np.float32)
    for m in range(32):
        perm[(m + 16) % 32, m] = 1.0
    K["perm"] = perm.astype(bf)
    idx = core_token_index(c)
    pos = np.asarray(inputs["positions"]).reshape(-1)[idx].astype(np.int32)
    K["pos32"] = np.ascontiguousarray(np.broadcast_to(pos[None, :], (32, NLOC)))
    invf = (ROPE_THETA ** (-np.arange(0, 32, 2, dtype=np.float32) / np.float32(32))).astype(np.float32)
    for lay in range(2):
        g = np.zeros((128, 6), np.float32)
        g[:, 0] = np.asarray(inputs["g_qnorm"])[lay]
        g[:, 1] = np.asarray(inputs["g_knorm"])[lay]
        g[:, 2] = np.asarray(inputs["g_mem_qnorm"])[lay][0:128]
        g[:, 3] = np.asarray(inputs["g_mem_qnorm"])[lay][128:256]
        g[0:16, 4] = invf
        g[16:32, 4] = invf
        g[0:16, 5] = -1.0
        g[16:32, 5] = 1.0
        K["gq%d" % lay] = g
        K["g_attn%d" % lay] = np.ascontiguousarray(np.asarray(inputs["g_attn_norm"])[lay].reshape(DC, 128).T)
        K["g_ffn%d" % lay] = np.ascontiguousarray(np.asarray(inputs["g_ffn_norm"])[lay].reshape(DC, 128).T)
    K["g_memn"] = np.ascontiguousarray(np.asarray(inputs["g_mem_norm"]).reshape(DC, 128).T)
    gk = np.asarray(inputs["g_mem_knorm"])
    K["gmk"] = np.ascontiguousarray(np.stack([gk[0][0:128], gk[0][128:256], gk[1][0:128], gk[1][128:256]], axis=1))
    K["lamv"] = np.ascontiguousarray(np.stack([np.asarray(inputs[n])[0] for n in ("lambda_q1", "lambda_k1", "lambda_q2", "lambda_k2")], axis=1))
    K["gsub"] = np.ascontiguousarray(np.broadcast_to(np.asarray(inputs["g_subln"])[0][None, :], (128, 256)))
    dm = np.zeros((128, 8, 128), np.float32)
    kk = np.arange(128)[:, None]
    qq = np.arange(128)[None, :]
    for cp in range(8):
        if cp < c:
            dm[:, cp, :] = 1.0
        elif cp == c:
            dm[:, cp, :] = (kk <= qq).astype(np.float32)
    K["dm"] = dm.reshape(128, 1024).astype(bf)
    cb = np.zeros((8, 32), np.float32)
    cand = np.zeros((8, 32), np.float32)
    own = np.zeros((8, 32), np.float32)
    for j in range(8):
        b0 = 4 * j + c // 2
        cand[j, :b0] = 1.0
        cb[j, b0:] = -1e30
        own[j, b0] = 1.0
    for nm, arr in (("cb", cb), ("cand", cand), ("own", own)):
        K[nm] = np.ascontiguousarray(np.broadcast_to(arr.reshape(1, 256), (128, 256)))
    return K


CONST_SPECS = {
    "ones": ([128, 128], BF16), "ident": ([128, 128], BF16), "perm": ([32, 32], BF16),
    "pos32": ([32, NLOC], I32), "gq0": ([128, 6], F32), "gq1": ([128, 6], F32),
    "g_memn": ([128, DC], F32), "gmk": ([128, 4], F32), "lamv": ([128, 4], F32), "gsub": ([128, 256], F32),
    "dm": ([128, 1024], BF16), "cb": ([128, 256], F32), "cand": ([128, 256], F32), "own": ([128, 256], F32),
    "g_attn0": ([128, DC], F32), "g_attn1": ([128, DC], F32), "g_ffn0": ([128, DC], F32), "g_ffn1": ([128, DC], F32),
}


class Prog:
    def __init__(self):
        self.nc = bass.Bass("TRN2", target_bir_lowering=False)
        self.ins = []
        self.outs = []
        self.T = {}
        self.C = {}

    def t(self, name, shape, dt, kind):
        if kind == "in":
            h = self.nc.dram_tensor(name, list(shape), dt, kind="ExternalInput")
            self.ins.append(name)
        elif kind == "out":
            h = self.nc.dram_tensor(name, list(shape), dt, kind="ExternalOutput")
            self.outs.append(name)
        else:
            h = self.nc.dram_tensor(name, list(shape), dt)
        self.T[name] = h.ap()
        self.T["#" + name] = h
        return h

    def consts(self, names):
        for n in names:
            shape, dt = CONST_SPECS[n]
            h = self.nc.dram_tensor(n, list(shape), dt, kind="ExternalInput")
            self.ins.append(n)
            self.C[n] = h.ap()


def phase_outproj(S, T, C, lay, x_in, x_out):
    S.begin()
    aT = S.sb([128, DC, NLOC], BF16, dma=True, name="aT")
    for q4 in range(4):
        S.dma("sp", aT[:, q4 * 8:(q4 + 1) * 8, :], T["attnT_d"][q4 * 8:(q4 + 1) * 8].rearrange("c p n -> p c n"),
              aT, writes=[aT])
    wv = T["w_out%d" % lay]
    wb = [S.sb([128, DC, 256], BF16, dma=True, name="wb") for _ in range(2)]
    pq = [S.ps([128, 512], name="pq") for _ in range(4)]
    xs = [S.sb([128, 512], F32, dma=True, name="xs") for _ in range(4)]
    k = 0
    for g in range(16):
        w = wb[g % 2]
        load_w(S, w, wv, 0, DC, g * 256, (g + 1) * 256)
        for ch in range(2):
            dc = g * 2 + ch
            for half in range(2):
                hs = slice(half * 512, (half + 1) * 512)
                p = pq[k % 4]
                x = xs[k % 4]
                k += 1
                S.dma("sp", x[:, :], x_in[dc * 128:(dc + 1) * 128, hs], x, writes=[x])
                pairs = [(w[:, c, ch * 128:(ch + 1) * 128], aT[:, c, hs]) for c in range(DC)]
                mm_group(S, p[:, :], pairs, [w, aT], p)
                S.op("dve", lambda e, x=x, p=p: e.tensor_tensor(out=x[:, :], in0=x[:, :], in1=p[:, :], op=ALU.add),
                     reads=[x, p], writes=[x])
                S.dma("sp", x_out[dc * 128:(dc + 1) * 128, hs], x[:, :], x, reads=[x])
    S.end()


def phase_ffn1(S, T, C, lay, x_in):
    S.begin()
    K = load_consts(S, C)
    fT = S.sb([128, DC, NLOC], BF16, name="fT")
    ssb = [S.ps([128, 512], name="ssq") for _ in range(2)]
    phase_norm(S, x_in, C["g_ffn%d" % lay], fT, K["ones"], ssb)
    wg_v = T["w_gate%d" % lay]
    wu_v = T["w_up%d" % lay]
    wg = [S.sb([128, DC, 256], BF16, dma=True, name="wg") for _ in range(2)]
    wu = [S.sb([128, DC, 256], BF16, dma=True, name="wu") for _ in range(2)]
    pg = [S.ps([128, 512], name="pg") for _ in range(3)]
    pu = [S.ps([128, 512], name="pu") for _ in range(3)]
    sg = [S.sb([128, 512], F32, name="sg") for _ in range(3)]
    hb = [S.sb([128, NLOC], BF16, dma=True, name="hb") for _ in range(3)]
    k = 0
    for g in range(43):
        a, b = wg[g % 2], wu[g % 2]
        load_w(S, a, wg_v, 0, DC, g * 256, (g + 1) * 256)
        load_w(S, b, wu_v, 0, DC, g * 256, (g + 1) * 256)
        for ch in range(2):
            fc = g * 2 + ch
            h = hb[fc % 3]
            for half in range(2):
                hs = slice(half * 512, (half + 1) * 512)
                p1, p2, s1 = pg[k % 3], pu[k % 3], sg[k % 3]
                k += 1
                mm_group(S, p1[:, :], [(a[:, c, ch * 128:(ch + 1) * 128], fT[:, c, hs]) for c in range(DC)], [a, fT], p1)
                mm_group(S, p2[:, :], [(b[:, c, ch * 128:(ch + 1) * 128], fT[:, c, hs]) for c in range(DC)], [b, fT], p2)
                S.op("act", lambda e, p1=p1, s1=s1: e.activation(out=s1[:, :], in_=p1[:, :], func=AF.Silu),
                     reads=[p1], writes=[s1])
                S.op("dve", lambda e, h=h, hs=hs, s1=s1, p2=p2: e.tensor_tensor(out=h[:, hs], in0=s1[:, :], in1=p2[:, :],
                                                                                op=ALU.mult), reads=[s1, p2], writes=[h])
            S.dma("sp", T["hff_d"][fc, :, :], h[:, :], h, reads=[h])
    S.end()


def phase_ffn2(S, T, C, lay, x_in, x_out):
    wv = T["w_down%d" % lay]
    for half in range(2):
        hs = slice(half * 512, (half + 1) * 512)
        S.begin()
        hT = S.sb([128, FC, 512], BF16, dma=True, name="hT2")
        for s0 in range(0, FC, 16):
            s1 = min(FC, s0 + 16)
            S.dma("sp", hT[:, s0:s1, :], T["hff_d"][s0:s1, :, hs].rearrange("c p n -> p c n"), hT, writes=[hT])
        wb = [S.sb([128, FC, 256], BF16, dma=True, name="wd") for _ in range(2)]
        pq = [S.ps([128, 512], name="pq") for _ in range(4)]
        xs = [S.sb([128, 512], F32, dma=True, name="xs") for _ in range(4)]
        k = 0
        for g in range(16):
            w = wb[g % 2]
            load_w(S, w, wv, 0, FC, g * 256, (g + 1) * 256, nsplit=8)
            for ch in range(2):
                dc = g * 2 + ch
                p = pq[k % 4]
                x = xs[k % 4]
                k += 1
                S.dma("sp", x[:, :], x_in[dc * 128:(dc + 1) * 128, hs], x, writes=[x])
                mm_group(S, p[:, :], [(w[:, c, ch * 128:(ch + 1) * 128], hT[:, c, :]) for c in range(FC)], [w, hT], p)
                S.op("dve", lambda e, x=x, p=p: e.tensor_tensor(out=x[:, :], in0=x[:, :], in1=p[:, :], op=ALU.add),
                     reads=[x, p], writes=[x])
                S.dma("sp", x_out[dc * 128:(dc + 1) * 128, hs], x[:, :], x, reads=[x])
        S.end()


def phase_memkv(S, T, C):
    S.begin()
    K = load_consts(S, C)
    ones = K["ones"]
    mT = S.sb([128, DC, MEM_LEN], BF16, name="mT")
    xs = [S.sb([128, MEM_LEN], F32, dma=True, name="xs") for _ in range(2)]
    sq = [S.sb([128, MEM_LEN], BF16, name="sq") for _ in range(2)]
    gc = S.sb([128, DC], F32, dma=True, name="gc")
    gmk = S.sb([128, 4], F32, dma=True, name="gmk")
    rstd = S.sb([128, MEM_LEN], F32, name="rstd")
    ss = S.ps([128, 512], name="ss")
    S.dma("sp", gc[:, :], C["g_memn"][:, :], gc, writes=[gc])
    S.dma("sp", gmk[:, :], C["gmk"][:, :], gmk, writes=[gmk])
    for c in range(DC):
        x, q = xs[c % 2], sq[c % 2]
        S.dma("sp", x[:, :], T["memT"][c * 128:(c + 1) * 128, :], x, writes=[x])
        S.op("act", lambda e, x=x, q=q: e.activation(out=q[:, :], in_=x[:, :], func=AF.Square), reads=[x], writes=[q])
        S.op("dve", lambda e, x=x, c=c: e.tensor_scalar(out=mT[:, c, :], in0=x[:, :], scalar1=gc[:, c:c + 1],
                                                        scalar2=None, op0=ALU.mult), reads=[x, gc], writes=[mT])
        S.op("pe", lambda e, q=q, c=c: e.matmul(ss[:, 0:MEM_LEN], lhsT=ones[:, :], rhs=q[:, :], start=(c == 0),
                                                stop=(c == DC - 1)), reads=[q, ones], writes=[ss])
    rstd_from_ss(S, rstd, rstd[:, :], ss, ss[:, 0:MEM_LEN], 1.0 / D, NORM_EPS)
    for c in range(DC):
        S.op("dve", lambda e, c=c: e.tensor_tensor(out=mT[:, c, :], in0=mT[:, c, :], in1=rstd[:, :], op=ALU.mult),
             reads=[mT, rstd], writes=[mT])
    wv = T["w_mem_kv"]
    wb = [S.sb([128, DC, 256], BF16, dma=True, name="wb") for _ in range(2)]
    pk = [S.ps([128, 512], name="pk") for _ in range(2)]
    ss2 = S.ps([128, 512], name="ss2")
    sqk = [S.sb([128, MEM_LEN], BF16, name="sqk") for _ in range(2)]
    kr = [S.sb([128, MEM_LEN], F32, name="kr") for _ in range(2)]
    rs = S.sb([128, MEM_LEN], F32, name="rs")
    ko = [S.sb([128, MEM_LEN], BF16, dma=True, name="ko") for _ in range(4)]
    vo = [S.sb([128, 256], BF16, dma=True, name="vo") for _ in range(2)]
    n = 0
    for g in range(8):
        w = wb[g % 2]
        load_w(S, w, wv, 0, DC, g * 256, (g + 1) * 256)
        if g < 4:
            for ch in range(2):
                p = pk[ch]
                mm_group(S, p[:, 0:MEM_LEN], [(w[:, c, ch * 128:(ch + 1) * 128], mT[:, c, :]) for c in range(DC)], [w, mT], p)
                S.op("dve", lambda e, p=p, ch=ch: e.tensor_copy(out=kr[ch][:, :], in_=p[:, 0:MEM_LEN]),
                     reads=[p], writes=[kr[ch]])
                S.op("act", lambda e, ch=ch: e.activation(out=sqk[ch][:, :], in_=kr[ch][:, :], func=AF.Square),
                     reads=[kr[ch]], writes=[sqk[ch]])
                S.op("pe", lambda e, ch=ch: e.matmul(ss2[:, 0:MEM_LEN], lhsT=ones[:, :], rhs=sqk[ch][:, :], start=(ch == 0),
                                                     stop=(ch == 1)), reads=[sqk[ch], ones], writes=[ss2])
            rstd_from_ss(S, rs, rs[:, :], ss2, ss2[:, 0:MEM_LEN], 1.0 / 256.0, NORM_EPS)
            for lay in range(2):
                for ch in range(2):
                    o = ko[n % 4]
                    n += 1
                    S.op("dve", lambda e, o=o, ch=ch, lay=lay: e.scalar_tensor_tensor(
                        out=o[:, :], in0=kr[ch][:, :], scalar=gmk[:, lay * 2 + ch:lay * 2 + ch + 1], in1=rs[:, :],
                        op0=ALU.mult, op1=ALU.mult), reads=[kr[ch], rs, gmk], writes=[o])
                    S.dma("sp", T["kmhT_d"][lay, g * 2 + ch, :, :], o[:, :], o, reads=[o])
        else:
            hm = g - 4
            for mt in range(2):
                p = pk[mt]
                mm_group(S, p[:, 0:256], [(mT[:, c, mt * 128:(mt + 1) * 128], w[:, c, 0:256]) for c in range(DC)], [w, mT], p)
                o = vo[mt]
                S.op("dve", lambda e, o=o, p=p: e.tensor_copy(out=o[:, :], in_=p[:, 0:256]), reads=[p], writes=[o])
                S.dma("sp", T["mv_d"][mt * 128:(mt + 1) * 128, hm * 256:(hm + 1) * 256], o[:, :], o, reads=[o])
    S.end()


def phase_attn(S, T, C, lay):
    S.begin()
    moba = (lay == 0)
    VW = 132 if moba else 260
    NV = 129 if moba else 257
    scale = 1.0 / math.sqrt(128.0)
    attnT = S.sb([128, DC, NLOC], BF16, dma=True, name="attnT")
    ident = S.sb([128, 128], BF16, dma=True, name="ident")
    ones = S.sb([128, 128], BF16, dma=True, name="ones")
    dm = S.sb([128, 1024], BF16, dma=True, name="dm")
    S.dma("sp", ident[:, :], C["ident"][:, :], ident, writes=[ident])
    S.dma("sp", ones[:, :], C["ones"][:, :], ones, writes=[ones])
    S.dma("sp", dm[:, :], C["dm"][:, :], dm, writes=[dm])
    KT = [S.sb([128, 8192], BF16, dma=True, name="KT") for _ in range(2)]
    VA = [S.sb([128, 64, VW], BF16, dma=True, name="VA") for _ in range(2)]
    QT = [S.sb([128, NLOC], BF16, dma=True, name="QT") for _ in range(2)]
    sT = [S.ps([128, 512], name="sT") for _ in range(2)]
    Ob = [S.ps([128, 512], name="Ob") for _ in range(2)]
    gps = S.ps([128, 512], name="gps")
    tp = S.ps([128, 1024], BF16, name="tp")
    eT = [S.sb([128, 512], BF16, name="eT") for _ in range(3)]
    ef32 = [S.sb([128, 512], F32, name="ef32") for _ in range(2)]
    ot = [S.sb([128, 256], BF16, name="ot") for _ in range(2)]
    cnt = {"g": 0, "o": 0, "t": 0}
    for v in VA:
        S.op("dve", lambda e, v=v: e.memset(v[:, :, VW - 4:VW], 1.0), writes=[v])

    kT_all, v_all, qT_d = T["kT_all%d" % lay], T["v_all%d" % lay], T["qT%d" % lay]

    def load_kq(i, chunk):
        kt, qt = KT[i % 2], QT[i % 2]
        for r0 in range(0, 8, 4):
            S.dma("sp", kt[:, r0 * 1024:(r0 + 4) * 1024].rearrange("p (r n) -> p r n", r=4),
                  kT_all[r0:r0 + 4, chunk, :, :].rearrange("r p n -> p r n"), kt, writes=[kt])
        S.dma("sp", qt[:, :], qT_d[chunk, :, :], qt, writes=[qt])
        return kt, qt

    def load_v(i, col0, width):
        va = VA[i % 2]
        for r in range(8):
            S.dma("sp", va[:, r * 8:(r + 1) * 8, 0:width],
                  v_all[r, :, col0:col0 + width].rearrange("(j p) d -> p j d", p=128), va, writes=[va])
        return va

    def qk_exp(kt, qt, j, grp):
        jp, c0 = grp // 2, 4 * (grp % 2)
        s = sT[cnt["g"] % 2]
        et = eT[cnt["g"] % 3]
        cnt["g"] += 1

        def fn(e):
            ins = None
            for i in range(4):
                k0 = (c0 + i) * 1024 + jp * 128
                ins = e.matmul(s[:, i * 128:(i + 1) * 128], lhsT=kt[:, k0:k0 + 128], rhs=qt[:, j * 128:(j + 1) * 128],
                               start=True, stop=True)
            return ins
        S.op("pe", fn, reads=[kt, qt], writes=[s])
        ef = ef32[cnt["g"] % 2]
        S.op("act", lambda e: e.activation(out=ef[:, :], in_=s[:, :], func=AF.Exp, scale=scale), reads=[s], writes=[ef])
        ceng = "pool" if (moba or cnt["g"] % 2 == 0) else "dve"
        if jp == j:
            S.op(ceng, lambda e: e.tensor_tensor(out=et[:, :], in0=ef[:, :], in1=dm[:, c0 * 128:(c0 + 4) * 128],
                                                 op=ALU.mult), reads=[ef, dm], writes=[et])
        else:
            S.op(ceng, lambda e: e.tensor_copy(out=et[:, :], in_=ef[:, :]), reads=[ef], writes=[et])
        return et, jp, c0

    def transpose_out(o_ap_list, chunk0, j):
        for i, (ap, sb) in enumerate(o_ap_list):
            t0 = (cnt["t"] % 8) * 128
            cnt["t"] += 1
            S.op("pe", lambda e, ap=ap, t0=t0: e.transpose(tp[:, t0:t0 + 128], ap, ident[:, :]), reads=[sb, ident], writes=[tp])
            S.op("dve", lambda e, t0=t0, i=i: e.tensor_copy(out=attnT[:, chunk0 + i, j * 128:(j + 1) * 128],
                                                            in_=tp[:, t0:t0 + 128]), reads=[tp], writes=[attnT])

    if moba:
        cb = S.sb([128, 256], F32, dma=True, name="cb")
        cand = S.sb([128, 256], F32, dma=True, name="cand")
        own = S.sb([128, 256], F32, dma=True, name="own")
        S.dma("sp", cb[:, :], C["cb"][:, :], cb, writes=[cb])
        S.dma("sp", cand[:, :], C["cand"][:, :], cand, writes=[cand])
        S.dma("sp", own[:, :], C["own"][:, :], own, writes=[own])
        tsum = S.sb([128, 64], F32, name="tsum")
        ksum = S.sb([128, 32], F32, name="ksum")
        khi = S.sb([128, 32], BF16, name="khi")
        klo = S.sb([128, 32], BF16, name="klo")
        gm = S.sb([128, 32], F32, name="gm")
        top8 = S.sb([128, 8], F32, name="top8")
        mp = S.sb([128, 32], F32, name="mp")
        acc = S.sb([128, 132], F32, name="acc")
        rec = S.sb([128, 1], F32, name="rec")
        for h in range(24):
            kt, qt = load_kq(h, h)
            va = load_v(h, h * 128, 128)
            S.op("dve", lambda e, kt=kt: e.reduce_sum(out=tsum[:, :], in_=kt[:, :].rearrange("p (t i) -> p t i", i=128),
                                                      axis=AX.X), reads=[kt], writes=[tsum])
            tv = tsum[:, :].rearrange("p (m two j) -> p m two j", two=2, j=8)
            S.op("dve", lambda e, tv=tv: e.tensor_tensor(out=ksum[:, :].rearrange("p (j m) -> p m j", m=4),
                                                         in0=tv[:, :, 0, :], in1=tv[:, :, 1, :], op=ALU.add),
                 reads=[tsum], writes=[ksum])
            S.op("dve", lambda e: e.tensor_copy(out=khi[:, :], in_=ksum[:, :]), reads=[ksum], writes=[khi])
            S.op("dve", lambda e: e.tensor_tensor(out=klo[:, :], in0=ksum[:, :], in1=khi[:, :], op=ALU.subtract),
                 reads=[ksum, khi], writes=[klo])
            for j in range(8):
                js = slice(j * 32, (j + 1) * 32)
                mm_group(S, gps[:, 0:32], [(qt[:, j * 128:(j + 1) * 128], khi[:, :]), (qt[:, j * 128:(j + 1) * 128], klo[:, :])],
                         [qt, khi, klo], gps)
                S.op("dve", lambda e, js=js: e.tensor_tensor(out=gm[:, :], in0=gps[:, 0:32], in1=cb[:, js], op=ALU.add),
                     reads=[gps, cb], writes=[gm])
                S.op("dve", lambda e: e.max(out=top8[:, :], in_=gm[:, :]), reads=[gm], writes=[top8])
                S.op("dve", lambda e, js=js: e.scalar_tensor_tensor(out=mp[:, :], in0=gm[:, :], scalar=top8[:, 2:3],
                                                                    in1=cand[:, js], op0=ALU.is_ge, op1=ALU.mult),
                     reads=[gm, top8, cand], writes=[mp])
                S.op("dve", lambda e, js=js: e.tensor_tensor(out=mp[:, :], in0=mp[:, :], in1=own[:, js], op=ALU.add),
                     reads=[mp, own], writes=[mp])
                first = True
                for grp in range(2 * (j + 1)):
                    et, jp, c0 = qk_exp(kt, qt, j, grp)
                    for ml in range(2):
                        o = Ob[cnt["o"] % 2]
                        cnt["o"] += 1
                        b = 4 * jp + c0 // 2 + ml

                        def fn(e, et=et, o=o, ml=ml, jp=jp, c0=c0, va=va):
                            e.matmul(o[:, 0:NV], lhsT=et[:, (2 * ml) * 128:(2 * ml + 1) * 128],
                                     rhs=va[:, (c0 + 2 * ml) * 8 + jp, 0:NV], start=True, stop=False)
                            return e.matmul(o[:, 0:NV], lhsT=et[:, (2 * ml + 1) * 128:(2 * ml + 2) * 128],
                                            rhs=va[:, (c0 + 2 * ml + 1) * 8 + jp, 0:NV], start=False, stop=True)
                        S.op("pe", fn, reads=[et, va], writes=[o])
                        if first:
                            S.op("dve", lambda e, o=o, b=b: e.tensor_scalar(out=acc[:, 0:NV], in0=o[:, 0:NV], scalar1=mp[:, b:b + 1],
                                                                            scalar2=None, op0=ALU.mult), reads=[o, mp], writes=[acc])
                            first = False
                        else:
                            S.op("dve", lambda e, o=o, b=b: e.scalar_tensor_tensor(
                                out=acc[:, 0:NV], in0=o[:, 0:NV], scalar=mp[:, b:b + 1], in1=acc[:, 0:NV],
                                op0=ALU.mult, op1=ALU.add), reads=[o, mp, acc], writes=[acc])
                S.op("dve", lambda e: e.reciprocal(out=rec[:, :], in_=acc[:, 128:129]), reads=[acc], writes=[rec])
                o2 = ot[j % 2]
                S.op("dve", lambda e, o2=o2: e.tensor_scalar(out=o2[:, 0:128], in0=acc[:, 0:128], scalar1=rec[:, 0:1],
                                                             scalar2=None, op0=ALU.mult), reads=[acc, rec], writes=[o2])
                transpose_out([(o2[:, 0:128], o2)], h, j)
    else:
        lamv = S.sb([128, 4], F32, dma=True, name="lamv")
        gsub = S.sb([128, 256], F32, dma=True, name="gsub")
        S.dma("sp", lamv[:, :], C["lamv"][:, :], lamv, writes=[lamv])
        S.dma("sp", gsub[:, :], C["gsub"][:, :], gsub, writes=[gsub])
        prod = S.sb([128, 2], F32, name="prod")
        phi = S.sb([128, 2], BF16, name="phi")
        plo = S.sb([128, 2], BF16, name="plo")
        ex = S.sb([128, 2], F32, name="ex")
        nlam = S.sb([128, 1], F32, name="nlam")
        S.op("dve", lambda e: e.tensor_tensor(out=prod[:, :], in0=lamv[:, 0:4:2], in1=lamv[:, 1:4:2], op=ALU.mult),
             reads=[lamv], writes=[prod])
        S.op("dve", lambda e: e.tensor_copy(out=phi[:, :], in_=prod[:, :]), reads=[prod], writes=[phi])
        S.op("dve", lambda e: e.tensor_tensor(out=plo[:, :], in0=prod[:, :], in1=phi[:, :], op=ALU.subtract),
             reads=[prod, phi], writes=[plo])
        mm_group(S, gps[:, 0:2], [(ones[:, :], phi[:, :]), (ones[:, :], plo[:, :])], [ones, phi, plo], gps)
        S.op("act", lambda e: e.activation(out=ex[:, :], in_=gps[:, 0:2], func=AF.Exp), reads=[gps], writes=[ex])
        S.op("dve", lambda e: e.tensor_tensor(out=nlam[:, :], in0=ex[:, 1:2], in1=ex[:, 0:1], op=ALU.subtract),
             reads=[ex], writes=[nlam])
        S.op("dve", lambda e: e.tensor_scalar(out=nlam[:, :], in0=nlam[:, :], scalar1=-LAM_INIT1, scalar2=None, op0=ALU.add),
             reads=[nlam], writes=[nlam])
        osb = S.sb([128, 2, 8, 260], F32, name="osb")
        rr = S.sb([128, 4], F32, name="rr")
        ta = S.sb([128, 256], F32, name="ta")
        tb = S.sb([128, 256], F32, name="tb")
        i = 0
        for hd in range(12):
            va = load_v(hd, hd * 256, 256)
            for comp in range(2):
                kt, qt = load_kq(i, 2 * hd + comp)
                i += 1
                for j in range(8):
                    o = Ob[cnt["o"] % 2]
                    cnt["o"] += 1
                    ng = 2 * (j + 1)
                    for grp in range(ng):
                        et, jp, c0 = qk_exp(kt, qt, j, grp)

                        def fn(e, et=et, o=o, jp=jp, c0=c0, grp=grp, ng=ng, va=va):
                            ins = None
                            for t in range(4):
                                ins = e.matmul(o[:, 0:NV], lhsT=et[:, t * 128:(t + 1) * 128], rhs=va[:, (c0 + t) * 8 + jp, 0:NV],
                                               start=(grp == 0 and t == 0), stop=(grp == ng - 1 and t == 3))
                            return ins
                        S.op("pe", fn, reads=[et, va], writes=[o])
                    S.op("dve", lambda e, o=o, comp=comp, j=j: e.tensor_copy(out=osb[:, comp, j, 0:NV], in_=o[:, 0:NV]),
                         reads=[o], writes=[osb])
            for j in range(8):
                S.op("dve", lambda e, j=j: e.reciprocal(out=rr[:, 0:1], in_=osb[:, 0, j, 256:257]), reads=[osb], writes=[rr])
                S.op("dve", lambda e, j=j: e.reciprocal(out=rr[:, 1:2], in_=osb[:, 1, j, 256:257]), reads=[osb, rr], writes=[rr])
                S.op("dve", lambda e: e.tensor_tensor(out=rr[:, 2:3], in0=rr[:, 1:2], in1=nlam[:, 0:1], op=ALU.mult),
                     reads=[rr, nlam], writes=[rr])
                S.op("dve", lambda e, j=j: e.tensor_scalar(out=ta[:, :], in0=osb[:, 0, j, 0:256], scalar1=rr[:, 0:1], scalar2=None,
                                                           op0=ALU.mult), reads=[osb, rr], writes=[ta])
                S.op("dve", lambda e, j=j: e.scalar_tensor_tensor(out=ta[:, :], in0=osb[:, 1, j, 0:256], scalar=rr[:, 2:3],
                                                                  in1=ta[:, :], op0=ALU.mult, op1=ALU.add),
                     reads=[osb, rr, ta], writes=[ta])
                S.op("dve", lambda e: e.tensor_tensor(out=tb[:, :], in0=ta[:, :], in1=ta[:, :], op=ALU.mult), reads=[ta], writes=[tb])
                S.op("dve", lambda e: e.reduce_sum(out=rr[:, 3:4], in_=tb[:, :], axis=AX.X), reads=[tb, rr], writes=[rr])
                rstd_from_ss(S, rr, rr[:, 3:4], rr, rr[:, 3:4], 1.0 / 256.0, SUBLN_EPS)
                S.op("dve", lambda e: e.tensor_scalar(out=ta[:, :], in0=ta[:, :], scalar1=rr[:, 3:4], scalar2=1.0 - LAM_INIT1,
                                                      op0=ALU.mult, op1=ALU.mult), reads=[ta, rr], writes=[ta])
                o2 = ot[j % 2]
                S.op("dve", lambda e, o2=o2: e.tensor_tensor(out=o2[:, :], in0=ta[:, :], in1=gsub[:, :], op=ALU.mult),
                     reads=[ta, gsub], writes=[o2])
                transpose_out([(o2[:, 0:128], o2), (o2[:, 128:256], o2)], 2 * hd, j)

    kmT = KT[0]
    qmT = KT[1]
    mva = VA[0]
    S.dma("sp", kmT[:, 0:2048].rearrange("p (c m) -> p c m", c=8), T["kmhT_d"][lay].rearrange("c p m -> p c m"), kmT, writes=[kmT])
    S.dma("sp", qmT[:, 0:8192].rearrange("p (c n) -> p c n", c=8), T["qmT%d" % lay].rearrange("c p n -> p c n"), qmT, writes=[qmT])
    mvt = S.sb([128, 8, 260], BF16, dma=True, name="mvt")
    S.op("dve", lambda e: e.memset(mvt[:, :, 256:260], 1.0), writes=[mvt])
    for mt in range(2):
        S.dma("sp", mvt[:, mt * 4:(mt + 1) * 4, 0:256], T["mv_d"][mt * 128:(mt + 1) * 128, :].rearrange("p (h d) -> p h d", h=4),
              mvt, writes=[mvt])
    orec = S.sb([128, 1], F32, name="orec")
    for hm in range(4):
        for half in range(2):
            ets = []
            for mt in range(2):
                s = sT[cnt["g"] % 2]
                et = eT[cnt["g"] % 3]
                cnt["g"] += 1
                pairs = [(kmT[:, (2 * hm + ch) * 256 + mt * 128:(2 * hm + ch) * 256 + (mt + 1) * 128],
                          qmT[:, (2 * hm + ch) * 1024 + half * 512:(2 * hm + ch) * 1024 + (half + 1) * 512]) for ch in range(2)]
                mm_group(S, s[:, :], pairs, [kmT, qmT], s)
                ef = ef32[cnt["g"] % 2]
                S.op("act", lambda e, s=s, ef=ef: e.activation(out=ef[:, :], in_=s[:, :], func=AF.Exp, scale=1.0 / 16.0),
                     reads=[s], writes=[ef])
                S.op("pool", lambda e, ef=ef, et=et: e.tensor_copy(out=et[:, :], in_=ef[:, :]), reads=[ef], writes=[et])
                ets.append(et)
            for qt_ in range(4):
                j = half * 4 + qt_
                o = Ob[cnt["o"] % 2]
                cnt["o"] += 1
                pairs = [(ets[mt][:, qt_ * 128:(qt_ + 1) * 128], mvt[:, mt * 4 + hm, 0:257]) for mt in range(2)]
                mm_group(S, o[:, 0:257], pairs, ets + [mvt], o)
                S.op("dve", lambda e, o=o: e.reciprocal(out=orec[:, :], in_=o[:, 256:257]), reads=[o], writes=[orec])
                o2 = ot[j % 2]
                S.op("dve", lambda e, o=o, o2=o2: e.tensor_scalar(out=o2[:, :], in0=o[:, 0:256], scalar1=orec[:, 0:1], scalar2=None,
                                                                  op0=ALU.mult), reads=[o, orec], writes=[o2])
                transpose_out([(o2[:, 0:128], o2), (o2[:, 128:256], o2)], 24 + 2 * hm, j)
    for q4 in range(4):
        S.dma("sp", T["attnT_d"][q4 * 8:(q4 + 1) * 8].rearrange("c p n -> p c n"), attnT[:, q4 * 8:(q4 + 1) * 8, :], attnT,
              reads=[attnT])
    S.end()


W_SHAPES = {"w_in": [PROJ_W // 256, 128, DC, 256], "w_out": [D // 256, 128, DC, 256], "w_gate": [DFF // 256, 128, DC, 256],
            "w_up": [DFF // 256, 128, DC, 256], "w_down": [D // 256, 128, FC, 256]}


def decl_handoff(P, lay, kinds):
    P.t("qT%d" % lay, [24, 128, NLOC], BF16, kinds["qT"])
    P.t("qmT%d" % lay, [8, 128, NLOC], BF16, kinds["qmT"])
    if kinds.get("kT_loc"):
        h = P.t("kT_loc%d" % lay, [24 * 128, NLOC], BF16, kinds["kT_loc"])
        P.T["kT_loc%d" % lay] = h.ap().rearrange("(c p) n -> c p n", p=128)
        P.t("v_loc%d" % lay, [NLOC, SELF_W], BF16, kinds["v_loc"])
    if kinds.get("kT_all"):
        h = P.t("kT_all%d" % lay, [8 * 24 * 128, NLOC], BF16, kinds["kT_all"])
        P.T["kT_all%d" % lay] = h.ap().rearrange("(r c p) n -> r c p n", r=8, p=128)
        h = P.t("v_all%d" % lay, [8 * NLOC, SELF_W], BF16, kinds["v_all"])
        P.T["v_all%d" % lay] = h.ap().rearrange("(r n) f -> r n f", r=8)


def emit_tail(S, P, lay, x_in, x_out):
    phase_attn(S, P.T, P.C, lay)
    phase_outproj(S, P.T, P.C, lay, x_in, P.T["xT_mid"])
    phase_ffn1(S, P.T, P.C, lay, P.T["xT_mid"])
    phase_ffn2(S, P.T, P.C, lay, P.T["xT_mid"], x_out)


def decl_scratch(P, dbg=False):
    k = "out" if dbg else "int"
    P.t("attnT_d", [DC, 128, NLOC], BF16, k)
    P.t("hff_d", [FC, 128, NLOC], BF16, "int")
    P.t("xT_mid", [D, NLOC], F32, k)
    P.t("kmhT_d", [2, 8, 128, MEM_LEN], BF16, k)
    P.t("mv_d", [MEM_LEN, MEM_W], BF16, k)


def build_A(lay=0):
    P = Prog()
    P.consts(["ones", "perm", "pos32", "gq%d" % lay, "g_attn%d" % lay])
    P.t("xT_in%d" % lay, [D, NLOC], F32, "in")
    P.t("w_in%d" % lay, W_SHAPES["w_in"], F32, "in")
    decl_handoff(P, lay, {"qT": "out", "qmT": "out", "kT_loc": "out", "v_loc": "out"})
    with ExitStack() as st:
        S = Sched(P.nc, st)
        phase_inproj(S, P.T, P.C, lay)
    return P


def build_B(lay, with_next, dbg=False):
    P = Prog()
    cn = ["ones", "ident", "dm", "g_ffn%d" % lay, "g_memn", "gmk"]
    cn += ["cb", "cand", "own"] if lay == 0 else ["lamv", "gsub"]
    if with_next:
        cn += ["perm", "pos32", "gq%d" % (lay + 1), "g_attn%d" % (lay + 1)]
    P.consts(cn)
    P.t("xT_in%d" % lay, [D, NLOC], F32, "in")
    P.t("memT", [D, MEM_LEN], F32, "in")
    P.t("w_mem_kv", [8, 128, DC, 256], F32, "in")
    for w in ("w_out", "w_gate", "w_up", "w_down"):
        P.t("%s%d" % (w, lay), W_SHAPES[w], F32, "in")
    decl_handoff(P, lay, {"qT": "in", "qmT": "in", "kT_all": "in", "v_all": "in"})
    decl_scratch(P, dbg)
    P.t("xT_in%d" % (lay + 1), [D, NLOC], F32, "out")
    if with_next:
        P.t("w_in%d" % (lay + 1), W_SHAPES["w_in"], F32, "in")
        decl_handoff(P, lay + 1, {"qT": "out", "qmT": "out", "kT_loc": "out", "v_loc": "out"})
    with ExitStack() as st:
        S = Sched(P.nc, st)
        phase_memkv(S, P.T, P.C)
        emit_tail(S, P, lay, P.T["xT_in%d" % lay], P.T["xT_in%d" % (lay + 1)])
        if with_next:
            phase_inproj(S, P.T, P.C, lay + 1)
    return P


def phase_gather(S, P, lay):
    S.begin()
    a, b = Buf(None), Buf(None)
    S.allgather(P.T["#kT_loc%d" % lay], P.T["#kT_all%d" % lay], [], [a])
    S.allgather(P.T["#v_loc%d" % lay], P.T["#v_all%d" % lay], [], [b])
    S.end()


def build_fused():
    P = Prog()
    P.consts(list(CONST_SPECS.keys()))
    P.t("xT_in0", [D, NLOC], F32, "in")
    P.t("memT", [D, MEM_LEN], F32, "in")
    P.t("w_mem_kv", [8, 128, DC, 256], F32, "in")
    for lay in range(2):
        for w in ("w_in", "w_out", "w_gate", "w_up", "w_down"):
            P.t("%s%d" % (w, lay), W_SHAPES[w], F32, "in")
        decl_handoff(P, lay, {"qT": "int", "qmT": "int", "kT_loc": "int", "v_loc": "int", "kT_all": "int", "v_all": "int"})
    decl_scratch(P)
    P.t("xT_in1", [D, NLOC], F32, "int")
    P.t("xT_in2", [D, NLOC], F32, "out")
    with ExitStack() as st:
        S = Sched(P.nc, st)
        phase_memkv(S, P.T, P.C)
        for lay in range(2):
            phase_inproj(S, P.T, P.C, lay)
            phase_gather(S, P, lay)
            emit_tail(S, P, lay, P.T["xT_in%d" % lay], P.T["xT_in%d" % (lay + 1)])
    return P


FUSED = True


def _tile_w(w):
    K_, F_ = w.shape
    return np.ascontiguousarray(w.reshape(K_ // 128, 128, F_ // 256, 256).transpose(2, 1, 0, 3))


def _weights(inputs, P):
    m = {}
    for n in P.ins:
        for w in ("w_in", "w_out", "w_gate", "w_up", "w_down"):
            if n.startswith(w) and n[len(w):] in ("0", "1"):
                m[n] = _tile_w(np.asarray(inputs[w])[int(n[len(w):])])
    if "w_mem_kv" in P.ins:
        m["w_mem_kv"] = _tile_w(np.asarray(inputs["w_mem_kv"]))
    if "memT" in P.ins:
        m["memT"] = np.ascontiguousarray(np.asarray(inputs["mem"])[0].T)
    return m


def _run(P, maps):
    res = run_bass_kernel_spmd(P.nc, maps, core_ids=list(range(NCORES)))
    return res.results


def kernel(**inputs):
    x = np.asarray(inputs["x"])[0]
    hcs = [host_consts(c, inputs) for c in range(NCORES)]
    xT = [np.ascontiguousarray(x[core_token_index(c)].T) for c in range(NCORES)]
    if FUSED:
        P = build_fused()
        w = _weights(inputs, P)
        maps = []
        for c in range(NCORES):
            m = {n: hcs[c][n] for n in P.ins if n in hcs[c]}
            m.update(w)
            m["xT_in0"] = xT[c]
            maps.append(m)
        outs = _run(P, maps)
        fin = [np.asarray(o["xT_in2"]) for o in outs]
    else:
        PA = build_A(0)
        w = _weights(inputs, PA)
        maps = []
        for c in range(NCORES):
            m = {n: hcs[c][n] for n in PA.ins if n in hcs[c]}
            m.update(w)
            m["xT_in0"] = xT[c]
            maps.append(m)
        prev = _run(PA, maps)
        cur_x = xT
        for lay in range(2):
            PB = build_B(lay, with_next=(lay == 0))
            w = _weights(inputs, PB)
            kT_all = np.concatenate([np.asarray(prev[c]["kT_loc%d" % lay]) for c in range(NCORES)], axis=0)
            v_all = np.concatenate([np.asarray(prev[c]["v_loc%d" % lay]) for c in range(NCORES)], axis=0)
            maps = []
            for c in range(NCORES):
                m = {n: hcs[c][n] for n in PB.ins if n in hcs[c]}
                m.update(w)
                m["xT_in%d" % lay] = cur_x[c]
                m["qT%d" % lay] = np.asarray(prev[c]["qT%d" % lay])
                m["qmT%d" % lay] = np.asarray(prev[c]["qmT%d" % lay])
                m["kT_all%d" % lay] = kT_all
                m["v_all%d" % lay] = v_all
                maps.append(m)
            prev = _run(PB, maps)
            cur_x = [np.asarray(prev[c]["xT_in%d" % (lay + 1)]) for c in range(NCORES)]
        fin = cur_x
    out = np.zeros((SEQ, D), np.float32)
    for c in range(NCORES):
        out[core_token_index(c)] = fin[c].T
    return out[None]
```

```python
import math
from contextlib import ExitStack

import numpy as np
import ml_dtypes

import concourse.bass as bass
import concourse.mybir as mybir
from concourse.bass_utils import run_bass_kernel_spmd

F32 = mybir.dt.float32
BF16 = mybir.dt.bfloat16
I32 = mybir.dt.int32
ALU = mybir.AluOpType
AF = mybir.ActivationFunctionType
AX = mybir.AxisListType

NCORES = 8
D = 4096
SEQ = 8192
NLOC = SEQ // NCORES
DC = D // 128
SELF_W = 3072
MEM_W = 1024
PROJ_W = 10240
DFF = 11008
FC = DFF // 128
MEM_LEN = 256
NORM_EPS = 1e-6
SUBLN_EPS = 1e-5
ROPE_THETA = 500000.0
LAM_INIT1 = 0.8 - 0.6 * math.exp(-0.3 * 1)
TWO_PI = 2.0 * math.pi
NDMASEM = 40


class Buf:
    def __init__(self, t, dsem=None):
        self.t = t
        self.w = None
        self.r = []
        self.dsem = dsem

    def __getitem__(self, k):
        return self.t[k]


class Sched:
    CE = ("pe", "act", "dve", "pool")

    def __init__(self, nc, stack):
        self.nc = nc
        self.sem = {}
        self.cnt = {}
        for e in self.CE:
            self.sem[e] = stack.enter_context(nc.semaphore("s_" + e))
            self.cnt[e] = 0
        for i in range(NDMASEM):
            self.sem[("d", i)] = stack.enter_context(nc.semaphore("d%d" % i))
            self.cnt[("d", i)] = 0
        self.sem["cc"] = stack.enter_context(nc.semaphore("ccs"))
        self.cnt["cc"] = 0
        self.free_d = [i for i in range(NDMASEM) if getattr(self.sem[("d", i)], "num", 0) != 192]
        self.phase_d = []
        self.ops = {e: [] for e in ("pe", "act", "dve", "pool", "sp")}
        self.waited = {e: {} for e in ("pe", "act", "dve", "pool", "sp")}
        self.pstack = None
        self.uid = 0

    def begin(self):
        self.pstack = ExitStack()
        self.pstack.__enter__()
        self.ops = {e: [] for e in self.ops}
        self.phase_d = []

    def sb(self, shape, dt, dma=False, name=None):
        self.uid += 1
        t = self.pstack.enter_context(self.nc.sbuf_tensor("%s_%d" % (name or "sb", self.uid), list(shape), dt))
        ds = None
        if dma:
            ds = self.free_d.pop()
            self.phase_d.append(ds)
        return Buf(t, ds)

    def ps(self, shape, dt=F32, name=None):
        self.uid += 1
        t = self.pstack.enter_context(self.nc.psum_tensor("%s_%d" % (name or "ps", self.uid), list(shape), dt))
        return Buf(t)

    def dr(self, t):
        return Buf(t)

    def end(self):
        nc = self.nc
        finals = [(k, v) for k, v in self.cnt.items() if v > 0]
        with nc.Block() as block:
            def emit(ename, eng):
                waited = self.waited[ename]
                for deps, fn, inc in self.ops[ename]:
                    for (k, v) in deps:
                        if k == ename and ename == "pe":
                            continue
                        if waited.get(k, 0) >= v:
                            continue
                        eng.wait_ge(self.sem[k], v)
                        waited[k] = v
                    ins = fn(eng)
                    if inc is not None:
                        ins.then_inc(self.sem[inc[0]], inc[1])
                for (k, v) in finals:
                    if waited.get(k, 0) >= v:
                        continue
                    eng.wait_ge(self.sem[k], v)
                    waited[k] = v

            @block.tensor
            def _(e):
                emit("pe", e)

            @block.scalar
            def _(e):
                emit("act", e)

            @block.vector
            def _(e):
                emit("dve", e)

            @block.gpsimd
            def _(e):
                emit("pool", e)

            @block.sync
            def _(e):
                emit("sp", e)
        self.free_d.extend(self.phase_d)
        self.phase_d = []
        self.pstack.__exit__(None, None, None)
        self.pstack = None

    def _deps(self, reads, writes):
        deps = []
        for b in reads:
            if b.w is not None:
                deps.append(b.w)
        for b in writes:
            if b.w is not None:
                deps.append(b.w)
            deps.extend(b.r)
        return deps

    def _commit(self, tok, reads, writes):
        for b in writes:
            b.w = tok
            b.r = []
        for b in reads:
            if b not in writes:
                b.r.append(tok)

    def op(self, eng, fn, reads=(), writes=()):
        deps = self._deps(reads, writes)
        self.cnt[eng] += 1
        tok = (eng, self.cnt[eng])
        self.ops[eng].append((deps, fn, (eng, 1)))
        self._commit(tok, reads, writes)
        return tok

    def dma(self, q, out_ap, in_ap, semb, reads=(), writes=()):
        import os
        if os.environ.get("KNOSTORE") == "1" and len(writes) == 0:
            return None
        deps = self._deps(reads, writes)
        k = ("d", semb.dsem)
        self.cnt[k] += 16
        tok = (k, self.cnt[k])
        self.ops[q].append((deps, lambda e: e.dma_start(out=out_ap, in_=in_ap), (k, 16)))
        self._commit(tok, reads, writes)
        return tok

    def allgather(self, in_t, out_t, reads, writes):
        deps = self._deps(reads, writes)
        self.cnt["cc"] += 1
        tok = ("cc", self.cnt["cc"])

        def fn(e):
            return e.collective_compute("AllGather", ALU.bypass, replica_groups=[list(range(NCORES))],
                                        ins=[in_t.ap().opt()], outs=[out_t.ap().opt()])
        self.ops["pool"].append((deps, fn, ("cc", 1)))
        self._commit(tok, reads, writes)
        return tok


def mm_group(S, out_ap, pairs, reads, psb):
    def fn(e):
        ins = None
        n = len(pairs)
        for i, (l, r) in enumerate(pairs):
            ins = e.matmul(out_ap, lhsT=l, rhs=r, start=(i == 0), stop=(i == n - 1))
        return ins
    return S.op("pe", fn, reads=reads, writes=[psb])


def load_consts(S, C):
    k = {}
    k["ones"] = S.sb([128, 128], BF16, dma=True, name="ones")
    S.dma("sp", k["ones"][:, :], C["ones"][:, :], k["ones"], writes=[k["ones"]])
    return k


def rstd_from_ss(S, ob, o_ap, sb_, s_ap, scale, eps):
    S.op("act", lambda e: e.activation(out=o_ap, in_=s_ap, func=AF.Sqrt, bias=float(eps), scale=float(scale)),
         reads=[sb_], writes=[ob])
    S.op("dve", lambda e: e.reciprocal(out=o_ap, in_=o_ap), reads=[ob], writes=[ob])


def phase_norm(S, xT_d, gcols_d, hT, ones, ss):
    xs = [S.sb([128, NLOC], F32, dma=True, name="xs") for _ in range(2)]
    sq = [S.sb([128, NLOC], BF16, name="sq") for _ in range(2)]
    gc = S.sb([128, DC], F32, dma=True, name="gc")
    rstd = S.sb([128, NLOC], F32, name="rstd")
    S.dma("sp", gc[:, :], gcols_d[:, :], gc, writes=[gc])
    for c in range(DC):
        x = xs[c % 2]
        q = sq[c % 2]
        S.dma("sp", x[:, :], xT_d[c * 128:(c + 1) * 128, :], x, writes=[x])
        S.op("act", lambda e, x=x, q=q: e.activation(out=q[:, :], in_=x[:, :], func=AF.Square), reads=[x], writes=[q])
        S.op("dve", lambda e, x=x, c=c: e.tensor_scalar(out=hT[:, c, :], in0=x[:, :], scalar1=gc[:, c:c + 1],
                                                        scalar2=None, op0=ALU.mult), reads=[x, gc], writes=[hT])
        for h in range(2):
            def fn(e, q=q, h=h, c=c):
                return e.matmul(ss[h][:, :], lhsT=ones[:, :], rhs=q[:, h * 512:(h + 1) * 512],
                                start=(c == 0), stop=(c == DC - 1))
            S.op("pe", fn, reads=[q, ones], writes=[ss[h]])
    for h in range(2):
        sl = slice(h * 512, (h + 1) * 512)
        rstd_from_ss(S, rstd, rstd[:, sl], ss[h], ss[h][:, :], 1.0 / D, NORM_EPS)
    for c in range(DC):
        S.op("dve", lambda e, c=c: e.tensor_tensor(out=hT[:, c, :], in0=hT[:, c, :], in1=rstd[:, :], op=ALU.mult),
             reads=[hT, rstd], writes=[hT])


def load_w(S, wb, w_view, c0, c1, f0, f1, nsplit=4):
    g = f0 // 256
    n = c1 - c0
    step = (n + nsplit - 1) // nsplit
    for s in range(0, n, step):
        e = min(n, s + step)
        S.dma("pool", wb[:, s:e, 0:f1 - f0], w_view[g, :, c0 + s:c0 + e, :], wb, writes=[wb])


def phase_inproj(S, T, C, lay):
    S.begin()
    K = load_consts(S, C)
    ones = K["ones"]
    hT = S.sb([128, DC, NLOC], BF16, name="hT")
    ssb = [S.ps([128, 512], name="ssq") for _ in range(2)]
    import os
    if os.environ.get("KNORM", "1") == "1":
        phase_norm(S, T["xT_in%d" % lay], C["g_attn%d" % lay], hT, ones, ssb)
    perm = S.sb([32, 32], BF16, dma=True, name="perm")
    S.dma("sp", perm[:, :], C["perm"][:, :], perm, writes=[perm])
    gq = S.sb([128, 6], F32, dma=True, name="gq")
    S.dma("sp", gq[:, :], C["gq%d" % lay][:, :], gq, writes=[gq])
    posi = S.sb([32, NLOC], I32, dma=True, name="posi")
    S.dma("sp", posi[:, :], C["pos32"][:, :], posi, writes=[posi])
    ang = S.sb([32, NLOC], F32, name="ang")
    tmp = S.sb([32, NLOC], F32, name="tmpa")
    cosT = S.sb([32, NLOC], F32, name="cosT")
    sinT = S.sb([32, NLOC], F32, name="sinT")
    S.op("dve", lambda e: e.tensor_copy(out=ang[:, :], in_=posi[:, :]), reads=[posi], writes=[ang])
    S.op("dve", lambda e: e.tensor_scalar(out=ang[:, :], in0=ang[:, :], scalar1=gq[0:32, 4:5], scalar2=None,
                                          op0=ALU.mult), reads=[ang, gq], writes=[ang])
    ki = S.sb([32, NLOC], I32, name="ki")
    kf = S.sb([32, NLOC], F32, name="kf")

    def sin_of(dst, offset):
        if offset != 0.0:
            S.op("dve", lambda e: e.tensor_scalar(out=tmp[:, :], in0=ang[:, :], scalar1=float(offset), scalar2=None,
                                                  op0=ALU.add), reads=[ang], writes=[tmp])
        else:
            S.op("dve", lambda e: e.tensor_copy(out=tmp[:, :], in_=ang[:, :]), reads=[ang], writes=[tmp])
        S.op("dve", lambda e: e.tensor_scalar(out=kf[:, :], in0=tmp[:, :], scalar1=1.0 / TWO_PI, scalar2=None,
                                              op0=ALU.mult), reads=[tmp], writes=[kf])
        S.op("dve", lambda e: e.tensor_copy(out=ki[:, :], in_=kf[:, :]), reads=[kf], writes=[ki])
        S.op("dve", lambda e: e.tensor_copy(out=kf[:, :], in_=ki[:, :]), reads=[ki], writes=[kf])
        S.op("dve", lambda e: e.scalar_tensor_tensor(out=tmp[:, :], in0=kf[:, :], scalar=-TWO_PI, in1=tmp[:, :],
                                                     op0=ALU.mult, op1=ALU.add), reads=[kf, tmp], writes=[tmp])
        S.op("dve", lambda e: e.tensor_scalar(out=kf[:, :], in0=tmp[:, :], scalar1=math.pi, scalar2=-TWO_PI,
                                              op0=ALU.is_gt, op1=ALU.mult), reads=[tmp], writes=[kf])
        S.op("dve", lambda e: e.tensor_tensor(out=tmp[:, :], in0=tmp[:, :], in1=kf[:, :], op=ALU.add),
             reads=[tmp, kf], writes=[tmp])
        S.op("dve", lambda e: e.tensor_scalar(out=kf[:, :], in0=tmp[:, :], scalar1=-math.pi, scalar2=TWO_PI,
                                              op0=ALU.is_lt, op1=ALU.mult), reads=[tmp], writes=[kf])
        S.op("dve", lambda e: e.tensor_tensor(out=tmp[:, :], in0=tmp[:, :], in1=kf[:, :], op=ALU.add),
             reads=[tmp, kf], writes=[tmp])
        S.op("dve", lambda e: e.tensor_scalar(out=kf[:, :], in0=tmp[:, :], scalar1=-1.0, scalar2=math.pi,
                                              op0=ALU.mult, op1=ALU.add), reads=[tmp], writes=[kf])
        S.op("dve", lambda e: e.tensor_tensor(out=kf[:, :], in0=kf[:, :], in1=tmp[:, :], op=ALU.min),
             reads=[tmp, kf], writes=[kf])
        S.op("dve", lambda e: e.tensor_scalar(out=tmp[:, :], in0=tmp[:, :], scalar1=-1.0, scalar2=-math.pi,
                                              op0=ALU.mult, op1=ALU.add), reads=[tmp], writes=[tmp])
        S.op("dve", lambda e: e.tensor_tensor(out=tmp[:, :], in0=kf[:, :], in1=tmp[:, :], op=ALU.max),
             reads=[tmp, kf], writes=[tmp])
        S.op("dve", lambda e: e.tensor_tensor(out=kf[:, :], in0=tmp[:, :], in1=tmp[:, :], op=ALU.mult),
             reads=[tmp], writes=[kf])
        cs = [-1.0 / 39916800.0, 1.0 / 362880.0, -1.0 / 5040.0, 1.0 / 120.0, -1.0 / 6.0]
        S.op("dve", lambda e: e.tensor_scalar(out=dst[:, :], in0=kf[:, :], scalar1=cs[0], scalar2=None, op0=ALU.mult),
             reads=[kf], writes=[dst])
        for cc in cs[1:]:
            S.op("dve", lambda e, cc=cc: e.scalar_tensor_tensor(out=dst[:, :], in0=dst[:, :], scalar=cc, in1=kf[:, :],
                                                                op0=ALU.add, op1=ALU.mult), reads=[dst, kf], writes=[dst])
        S.op("dve", lambda e: e.scalar_tensor_tensor(out=dst[:, :], in0=dst[:, :], scalar=1.0, in1=tmp[:, :],
                                                     op0=ALU.add, op1=ALU.mult), reads=[dst, tmp], writes=[dst])

    if os.environ.get("KTAB", "1") == "1":
        sin_of(sinT, 0.0)
        sin_of(cosT, 0.5 * math.pi)
    S.op("dve", lambda e: e.tensor_scalar(out=sinT[:, :], in0=sinT[:, :], scalar1=gq[0:32, 5:6], scalar2=None,
                                          op0=ALU.mult), reads=[sinT, gq], writes=[sinT])

    wv = T["w_in%d" % lay]
    wb = [S.sb([128, DC, 256], BF16, dma=True, name="wb") for _ in range(2)]
    pq = [S.ps([128, 512], name="pq") for _ in range(4)]
    prb = [S.ps([128, 512], name="prp") for _ in range(2)]
    sqb = [S.sb([128, 512], BF16, name="sqb") for _ in range(4)]
    qgb = [S.sb([128, 512], BF16, name="qgb") for _ in range(4)]
    rsb = [S.sb([128, 512], F32, name="rsb") for _ in range(2)]
    t1b = [S.sb([32, 512], F32, name="t1b") for _ in range(2)]
    t2b = [S.sb([32, 512], F32, name="t2b") for _ in range(2)]
    ob = [S.sb([128, 512], BF16, dma=True, name="ob") for _ in range(4)]
    vst = [S.sb([128, 8, 256], BF16, dma=True, name="vst") for _ in range(2)]
    vf32 = S.sb([128, 256], F32, name="vf32")
    praw = [S.sb([128, 512], F32, name="praw") for _ in range(2)]
    qT_d, kT_d, v_d, qmT_d = T["qT%d" % lay], T["kT_loc%d" % lay], T["v_loc%d" % lay], T["qmT%d" % lay]
    ctr = {"pq": 0, "t": 0, "ob": 0}

    def proj_tile(w, ch, half):
        p = pq[ctr["pq"] % 4]
        ctr["pq"] += 1
        pairs = [(w[:, c, ch * 128:(ch + 1) * 128], hT[:, c, half * 512:(half + 1) * 512]) for c in range(DC)]
        mm_group(S, p[:, :], pairs, [w, hT], p)
        return p

    import os
    for g in [int(t) for t in os.environ.get('KGROUPS', ','.join(str(i) for i in range(40))).split(',') if t != '']:
        w = wb[g % 2]
        load_w(S, w, wv, 0, DC, g * 256, (g + 1) * 256)
        if os.environ.get('KONLYLOAD') == '1':
            continue
        if g < 24:
            isq = g < 12
            gcol = 0 if isq else 1
            for ch in range(2):
                head = (g % 12) * 2 + ch
                for half in range(2):
                    hs = slice(half * 512, (half + 1) * 512)
                    p = proj_tile(w, ch, half)
                    i = ctr["t"] % 4
                    i2 = ctr["t"] % 2
                    ctr["t"] += 1
                    sqt, qg, rs, t1, t2, ssp, prp = sqb[i], qgb[i], rsb[i2], t1b[i2], t2b[i2], ssb[i2], prb[i2]
                    o = ob[ctr["ob"] % 4]
                    ctr["ob"] += 1
                    pr_ = praw[i2]
                    S.op("act", lambda e, p=p, pr_=pr_: e.activation(out=pr_[:, :], in_=p[:, :], func=AF.Copy),
                         reads=[p], writes=[pr_])
                    S.op("act", lambda e, pr_=pr_, sqt=sqt: e.activation(out=sqt[:, :], in_=pr_[:, :], func=AF.Square),
                         reads=[pr_], writes=[sqt])
                    S.op("dve", lambda e, pr_=pr_, qg=qg, gcol=gcol: e.tensor_scalar(
                        out=qg[:, :], in0=pr_[:, :], scalar1=gq[:, gcol:gcol + 1], scalar2=None, op0=ALU.mult),
                        reads=[pr_, gq], writes=[qg])
                    mm_group(S, ssp[:, :], [(ones[:, :], sqt[:, :])], [ones, sqt], ssp)
                    mm_group(S, prp[0:32, :], [(perm[:, :], qg[0:32, :])], [perm, qg], prp)
                    rstd_from_ss(S, rs, rs[:, :], ssp, ssp[:, :], 1.0 / 128.0, NORM_EPS)
                    S.op("dve", lambda e, t1=t1, qg=qg, hs=hs: e.tensor_tensor(
                        out=t1[:, :], in0=qg[0:32, :], in1=cosT[:, hs], op=ALU.mult), reads=[qg, cosT], writes=[t1])
                    S.op("dve", lambda e, t2=t2, prp=prp, hs=hs: e.tensor_tensor(
                        out=t2[:, :], in0=prp[0:32, :], in1=sinT[:, hs], op=ALU.mult), reads=[prp, sinT], writes=[t2])
                    S.op("dve", lambda e, t1=t1, t2=t2, qg=qg: e.tensor_tensor(
                        out=qg[0:32, :], in0=t1[:, :], in1=t2[:, :], op=ALU.add), reads=[t1, t2], writes=[qg])
                    S.op("dve", lambda e, o=o, qg=qg, rs=rs: e.scalar_tensor_tensor(
                        out=o[:, :], in0=qg[:, :], scalar=1.0, in1=rs[:, :], op0=ALU.mult, op1=ALU.mult),
                        reads=[qg, rs], writes=[o])
                    dst = (qT_d if isq else kT_d)[head, :, hs]
                    S.dma("sp", dst, o[:, :], o, reads=[o])
        elif g < 36:
            vs = vst[g % 2]
            for tt in range(8):
                p = pq[ctr["pq"] % 4]
                ctr["pq"] += 1
                pairs = [(hT[:, c, tt * 128:(tt + 1) * 128], w[:, c, 0:256]) for c in range(DC)]
                mm_group(S, p[:, 0:256], pairs, [w, hT], p)
                if os.environ.get('KMMONLY') == '1':
                    continue
                eng = os.environ.get("KVENG") or ("act" if tt % 2 == 0 else "dve")
                if eng == "act":
                    S.op("act", lambda e, p=p: e.activation(out=vf32[:, :], in_=p[:, 0:256], func=AF.Copy),
                         reads=[p], writes=[vf32])
                    S.op("dve", lambda e, vs=vs, tt=tt: e.tensor_copy(out=vs[:, tt, :], in_=vf32[:, :]),
                         reads=[vf32], writes=[vs])
                else:
                    S.op("dve", lambda e, p=p, vs=vs, tt=tt: e.tensor_copy(out=vs[:, tt, :], in_=p[:, 0:256]),
                         reads=[p], writes=[vs])
            c0 = (g - 24) * 256
            if os.environ.get('KMMONLY') == '1':
                continue
            S.dma("sp", v_d.rearrange("(t p) f -> p t f", p=128)[:, :, c0:c0 + 256], vs[:, :, :], vs, reads=[vs])
        else:
            hm = g - 36
            for half in range(2):
                hs = slice(half * 512, (half + 1) * 512)
                i2 = ctr["t"] % 2
                ssp, rs = ssb[i2], rsb[i2]
                qgs = []
                for ch in range(2):
                    p = proj_tile(w, ch, half)
                    i = ctr["t"] % 4
                    ctr["t"] += 1
                    sqt, qg = sqb[i], qgb[i]
                    qgs.append(qg)
                    pr_ = praw[ch]
                    S.op("act", lambda e, p=p, pr_=pr_: e.activation(out=pr_[:, :], in_=p[:, :], func=AF.Copy),
                         reads=[p], writes=[pr_])
                    S.op("act", lambda e, pr_=pr_, sqt=sqt: e.activation(out=sqt[:, :], in_=pr_[:, :], func=AF.Square),
                         reads=[pr_], writes=[sqt])
                    S.op("dve", lambda e, pr_=pr_, qg=qg, ch=ch: e.tensor_scalar(
                        out=qg[:, :], in0=pr_[:, :], scalar1=gq[:, 2 + ch:3 + ch], scalar2=None, op0=ALU.mult),
                        reads=[pr_, gq], writes=[qg])

                    def fn(e, ssp=ssp, sqt=sqt, ch=ch):
                        return e.matmul(ssp[:, :], lhsT=ones[:, :], rhs=sqt[:, :], start=(ch == 0), stop=(ch == 1))
                    S.op("pe", fn, reads=[ones, sqt], writes=[ssp])
                rstd_from_ss(S, rs, rs[:, :], ssp, ssp[:, :], 1.0 / 256.0, NORM_EPS)
                for ch in range(2):
                    o = ob[ctr["ob"] % 4]
                    ctr["ob"] += 1
                    qg = qgs[ch]
                    S.op("dve", lambda e, o=o, qg=qg, rs=rs: e.scalar_tensor_tensor(
                        out=o[:, :], in0=qg[:, :], scalar=1.0, in1=rs[:, :], op0=ALU.mult, op1=ALU.mult),
                        reads=[qg, rs], writes=[o])
                    S.dma("sp", qmT_d[hm * 2 + ch, :, hs], o[:, :], o, reads=[o])
    S.end()


def core_token_index(c):
    j = np.arange(8)[:, None]
    i = np.arange(128)[None, :]
    return (1024 * j + 128 * c + i).reshape(-1)


def host_consts(c, inputs):
    bf = ml_dtypes.bfloat16
    K = {}
    K["ones"] = np.ones((128, 128), bf)
    K["ident"] = np.eye(128, dtype=np.float32).astype(bf)
    perm = np.zeros((32, 32), np.float32)
    for m in range(32):
        perm[(m + 16) % 32, m] = 1.0
    K["perm"] = perm.astype(bf)
    idx = core_token_index(c)
    pos = np.asarray(inputs["positions"]).reshape(-1)[idx].astype(np.int32)
    K["pos32"] = np.ascontiguousarray(np.broadcast_to(pos[None, :], (32, NLOC)))
    invf = (ROPE_THETA ** (-np.arange(0, 32, 2, dtype=np.float32) / np.float32(32))).astype(np.float32)
    for lay in range(2):
        g = np.zeros((128, 6), np.float32)
        g[:, 0] = np.asarray(inputs["g_qnorm"])[lay]
        g[:, 1] = np.asarray(inputs["g_knorm"])[lay]
        g[:, 2] = np.asarray(inputs["g_mem_qnorm"])[lay][0:128]
        g[:, 3] = np.asarray(inputs["g_mem_qnorm"])[lay][128:256]
        g[0:16, 4] = invf
        g[16:32, 4] = invf
        g[0:16, 5] = -1.0
        g[16:32, 5] = 1.0
        K["gq%d" % lay] = g
        K["g_attn%d" % lay] = np.ascontiguousarray(np.asarray(inputs["g_attn_norm"])[lay].reshape(DC, 128).T)
        K["g_ffn%d" % lay] = np.ascontiguousarray(np.asarray(inputs["g_ffn_norm"])[lay].reshape(DC, 128).T)
    K["g_memn"] = np.ascontiguousarray(np.asarray(inputs["g_mem_norm"]).reshape(DC, 128).T)
    gk = np.asarray(inputs["g_mem_knorm"])
    K["gmk"] = np.ascontiguousarray(np.stack([gk[0][0:128], gk[0][128:256], gk[1][0:128], gk[1][128:256]], axis=1))
    K["lamv"] = np.ascontiguousarray(np.stack([np.asarray(inputs[n])[0] for n in ("lambda_q1", "lambda_k1", "lambda_q2", "lambda_k2")], axis=1))
    K["gsub"] = np.ascontiguousarray(np.broadcast_to(np.asarray(inputs["g_subln"])[0][None, :], (128, 256)))
    dm = np.zeros((128, 8, 128), np.float32)
    kk = np.arange(128)[:, None]
    qq = np.arange(128)[None, :]
    for cp in range(8):
        if cp < c:
            dm[:, cp, :] = 1.0
        elif cp == c:
            dm[:, cp, :] = (kk <= qq).astype(np.float32)
    K["dm"] = dm.reshape(128, 1024).astype(bf)
    cb = np.zeros((8, 32), np.float32)
    cand = np.zeros((8, 32), np.float32)
    own = np.zeros((8, 32), np.float32)
    for j in range(8):
        b0 = 4 * j + c // 2
        cand[j, :b0] = 1.0
        cb[j, b0:] = -1e30
        own[j, b0] = 1.0
    for nm, arr in (("cb", cb), ("cand", cand), ("own", own)):
        K[nm] = np.ascontiguousarray(np.broadcast_to(arr.reshape(1, 256), (128, 256)))
    return K


CONST_SPECS = {
    "ones": ([128, 128], BF16), "ident": ([128, 128], BF16), "perm": ([32, 32], BF16),
    "pos32": ([32, NLOC], I32), "gq0": ([128, 6], F32), "gq1": ([128, 6], F32),
    "g_memn": ([128, DC], F32), "gmk": ([128, 4], F32), "lamv": ([128, 4], F32), "gsub": ([128, 256], F32),
    "dm": ([128, 1024], BF16), "cb": ([128, 256], F32), "cand": ([128, 256], F32), "own": ([128, 256], F32),
    "g_attn0": ([128, DC], F32), "g_attn1": ([128, DC], F32), "g_ffn0": ([128, DC], F32), "g_ffn1": ([128, DC], F32),
}


class Prog:
    def __init__(self):
        self.nc = bass.Bass("TRN2", target_bir_lowering=False)
        self.ins = []
        self.outs = []
        self.T = {}
        self.C = {}

    def t(self, name, shape, dt, kind):
        if kind == "in":
            h = self.nc.dram_tensor(name, list(shape), dt, kind="ExternalInput")
            self.ins.append(name)
        elif kind == "out":
            h = self.nc.dram_tensor(name, list(shape), dt, kind="ExternalOutput")
            self.outs.append(name)
        else:
            h = self.nc.dram_tensor(name, list(shape), dt)
        self.T[name] = h.ap()
        self.T["#" + name] = h
        return h

    def consts(self, names):
        for n in names:
            shape, dt = CONST_SPECS[n]
            h = self.nc.dram_tensor(n, list(shape), dt, kind="ExternalInput")
            self.ins.append(n)
            self.C[n] = h.ap()


def phase_outproj(S, T, C, lay, x_in, x_out):
    S.begin()
    aT = S.sb([128, DC, NLOC], BF16, dma=True, name="aT")
    for q4 in range(4):
        S.dma("sp", aT[:, q4 * 8:(q4 + 1) * 8, :], T["attnT_d"][q4 * 8:(q4 + 1) * 8].rearrange("c p n -> p c n"),
              aT, writes=[aT])
    wv = T["w_out%d" % lay]
    wb = [S.sb([128, DC, 256], BF16, dma=True, name="wb") for _ in range(2)]
    pq = [S.ps([128, 512], name="pq") for _ in range(4)]
    xs = [S.sb([128, 512], F32, dma=True, name="xs") for _ in range(4)]
    k = 0
    for g in range(16):
        w = wb[g % 2]
        load_w(S, w, wv, 0, DC, g * 256, (g + 1) * 256)
        for ch in range(2):
            dc = g * 2 + ch
            for half in range(2):
                hs = slice(half * 512, (half + 1) * 512)
                p = pq[k % 4]
                x = xs[k % 4]
                k += 1
                S.dma("sp", x[:, :], x_in[dc * 128:(dc + 1) * 128, hs], x, writes=[x])
                pairs = [(w[:, c, ch * 128:(ch + 1) * 128], aT[:, c, hs]) for c in range(DC)]
                mm_group(S, p[:, :], pairs, [w, aT], p)
                S.op("dve", lambda e, x=x, p=p: e.tensor_tensor(out=x[:, :], in0=x[:, :], in1=p[:, :], op=ALU.add),
                     reads=[x, p], writes=[x])
                S.dma("sp", x_out[dc * 128:(dc + 1) * 128, hs], x[:, :], x, reads=[x])
    S.end()


def phase_ffn1(S, T, C, lay, x_in):
    S.begin()
    K = load_consts(S, C)
    fT = S.sb([128, DC, NLOC], BF16, name="fT")
    ssb = [S.ps([128, 512], name="ssq") for _ in range(2)]
    phase_norm(S, x_in, C["g_ffn%d" % lay], fT, K["ones"], ssb)
    wg_v = T["w_gate%d" % lay]
    wu_v = T["w_up%d" % lay]
    wg = [S.sb([128, DC, 256], BF16, dma=True, name="wg") for _ in range(2)]
    wu = [S.sb([128, DC, 256], BF16, dma=True, name="wu") for _ in range(2)]
    pg = [S.ps([128, 512], name="pg") for _ in range(3)]
    pu = [S.ps([128, 512], name="pu") for _ in range(3)]
    sg = [S.sb([128, 512], F32, name="sg") for _ in range(3)]
    hb = [S.sb([128, NLOC], BF16, dma=True, name="hb") for _ in range(3)]
    k = 0
    for g in range(43):
        a, b = wg[g % 2], wu[g % 2]
        load_w(S, a, wg_v, 0, DC, g * 256, (g + 1) * 256)
        load_w(S, b, wu_v, 0, DC, g * 256, (g + 1) * 256)
        for ch in range(2):
            fc = g * 2 + ch
            h = hb[fc % 3]
            for half in range(2):
                hs = slice(half * 512, (half + 1) * 512)
                p1, p2, s1 = pg[k % 3], pu[k % 3], sg[k % 3]
                k += 1
                mm_group(S, p1[:, :], [(a[:, c, ch * 128:(ch + 1) * 128], fT[:, c, hs]) for c in range(DC)], [a, fT], p1)
                mm_group(S, p2[:, :], [(b[:, c, ch * 128:(ch + 1) * 128], fT[:, c, hs]) for c in range(DC)], [b, fT], p2)
                S.op("act", lambda e, p1=p1, s1=s1: e.activation(out=s1[:, :], in_=p1[:, :], func=AF.Silu),
                     reads=[p1], writes=[s1])
                S.op("dve", lambda e, h=h, hs=hs, s1=s1, p2=p2: e.tensor_tensor(out=h[:, hs], in0=s1[:, :], in1=p2[:, :],
                                                                                op=ALU.mult), reads=[s1, p2], writes=[h])
            S.dma("sp", T["hff_d"][fc, :, :], h[:, :], h, reads=[h])
    S.end()


def phase_ffn2(S, T, C, lay, x_in, x_out):
    wv = T["w_down%d" % lay]
    for half in range(2):
        hs = slice(half * 512, (half + 1) * 512)
        S.begin()
        hT = S.sb([128, FC, 512], BF16, dma=True, name="hT2")
        for s0 in range(0, FC, 16):
            s1 = min(FC, s0 + 16)
            S.dma("sp", hT[:, s0:s1, :], T["hff_d"][s0:s1, :, hs].rearrange("c p n -> p c n"), hT, writes=[hT])
        wb = [S.sb([128, FC, 256], BF16, dma=True, name="wd") for _ in range(2)]
        pq = [S.ps([128, 512], name="pq") for _ in range(4)]
        xs = [S.sb([128, 512], F32, dma=True, name="xs") for _ in range(4)]
        k = 0
        for g in range(16):
            w = wb[g % 2]
            load_w(S, w, wv, 0, FC, g * 256, (g + 1) * 256, nsplit=8)
            for ch in range(2):
                dc = g * 2 + ch
                p = pq[k % 4]
                x = xs[k % 4]
                k += 1
                S.dma("sp", x[:, :], x_in[dc * 128:(dc + 1) * 128, hs], x, writes=[x])
                mm_group(S, p[:, :], [(w[:, c, ch * 128:(ch + 1) * 128], hT[:, c, :]) for c in range(FC)], [w, hT], p)
                S.op("dve", lambda e, x=x, p=p: e.tensor_tensor(out=x[:, :], in0=x[:, :], in1=p[:, :], op=ALU.add),
                     reads=[x, p], writes=[x])
                S.dma("sp", x_out[dc * 128:(dc + 1) * 128, hs], x[:, :], x, reads=[x])
        S.end()


def phase_memkv(S, T, C):
    S.begin()
    K = load_consts(S, C)
    ones = K["ones"]
    mT = S.sb([128, DC, MEM_LEN], BF16, name="mT")
    xs = [S.sb([128, MEM_LEN], F32, dma=True, name="xs") for _ in range(2)]
    sq = [S.sb([128, MEM_LEN], BF16, name="sq") for _ in range(2)]
    gc = S.sb([128, DC], F32, dma=True, name="gc")
    gmk = S.sb([128, 4], F32, dma=True, name="gmk")
    rstd = S.sb([128, MEM_LEN], F32, name="rstd")
    ss = S.ps([128, 512], name="ss")
    S.dma("sp", gc[:, :], C["g_memn"][:, :], gc, writes=[gc])
    S.dma("sp", gmk[:, :], C["gmk"][:, :], gmk, writes=[gmk])
    for c in range(DC):
        x, q = xs[c % 2], sq[c % 2]
        S.dma("sp", x[:, :], T["memT"][c * 128:(c + 1) * 128, :], x, writes=[x])
        S.op("act", lambda e, x=x, q=q: e.activation(out=q[:, :], in_=x[:, :], func=AF.Square), reads=[x], writes=[q])
        S.op("dve", lambda e, x=x, c=c: e.tensor_scalar(out=mT[:, c, :], in0=x[:, :], scalar1=gc[:, c:c + 1],
                                                        scalar2=None, op0=ALU.mult), reads=[x, gc], writes=[mT])
        S.op("pe", lambda e, q=q, c=c: e.matmul(ss[:, 0:MEM_LEN], lhsT=ones[:, :], rhs=q[:, :], start=(c == 0),
                                                stop=(c == DC - 1)), reads=[q, ones], writes=[ss])
    rstd_from_ss(S, rstd, rstd[:, :], ss, ss[:, 0:MEM_LEN], 1.0 / D, NORM_EPS)
    for c in range(DC):
        S.op("dve", lambda e, c=c: e.tensor_tensor(out=mT[:, c, :], in0=mT[:, c, :], in1=rstd[:, :], op=ALU.mult),
             reads=[mT, rstd], writes=[mT])
    wv = T["w_mem_kv"]
    wb = [S.sb([128, DC, 256], BF16, dma=True, name="wb") for _ in range(2)]
    pk = [S.ps([128, 512], name="pk") for _ in range(2)]
    ss2 = S.ps([128, 512], name="ss2")
    sqk = [S.sb([128, MEM_LEN], BF16, name="sqk") for _ in range(2)]
    kr = [S.sb([128, MEM_LEN], F32, name="kr") for _ in range(2)]
    rs = S.sb([128, MEM_LEN], F32, name="rs")
    ko = [S.sb([128, MEM_LEN], BF16, dma=True, name="ko") for _ in range(4)]
    vo = [S.sb([128, 256], BF16, dma=True, name="vo") for _ in range(2)]
    n = 0
    for g in range(8):
        w = wb[g % 2]
        load_w(S, w, wv, 0, DC, g * 256, (g + 1) * 256)
        if g < 4:
            for ch in range(2):
                p = pk[ch]
                mm_group(S, p[:, 0:MEM_LEN], [(w[:, c, ch * 128:(ch + 1) * 128], mT[:, c, :]) for c in range(DC)], [w, mT], p)
                S.op("dve", lambda e, p=p, ch=ch: e.tensor_copy(out=kr[ch][:, :], in_=p[:, 0:MEM_LEN]),
                     reads=[p], writes=[kr[ch]])
                S.op("act", lambda e, ch=ch: e.activation(out=sqk[ch][:, :], in_=kr[ch][:, :], func=AF.Square),
                     reads=[kr[ch]], writes=[sqk[ch]])
                S.op("pe", lambda e, ch=ch: e.matmul(ss2[:, 0:MEM_LEN], lhsT=ones[:, :], rhs=sqk[ch][:, :], start=(ch == 0),
                                                     stop=(ch == 1)), reads=[sqk[ch], ones], writes=[ss2])
            rstd_from_ss(S, rs, rs[:, :], ss2, ss2[:, 0:MEM_LEN], 1.0 / 256.0, NORM_EPS)
            for lay in range(2):
                for ch in range(2):
                    o = ko[n % 4]
                    n += 1
                    S.op("dve", lambda e, o=o, ch=ch, lay=lay: e.scalar_tensor_tensor(
                        out=o[:, :], in0=kr[ch][:, :], scalar=gmk[:, lay * 2 + ch:lay * 2 + ch + 1], in1=rs[:, :],
                        op0=ALU.mult, op1=ALU.mult), reads=[kr[ch], rs, gmk], writes=[o])
                    S.dma("sp", T["kmhT_d"][lay, g * 2 + ch, :, :], o[:, :], o, reads=[o])
        else:
            hm = g - 4
            for mt in range(2):
                p = pk[mt]
                mm_group(S, p[:, 0:256], [(mT[:, c, mt * 128:(mt + 1) * 128], w[:, c, 0:256]) for c in range(DC)], [w, mT], p)
                o = vo[mt]
                S.op("dve", lambda e, o=o, p=p: e.tensor_copy(out=o[:, :], in_=p[:, 0:256]), reads=[p], writes=[o])
                S.dma("sp", T["mv_d"][mt * 128:(mt + 1) * 128, hm * 256:(hm + 1) * 256], o[:, :], o, reads=[o])
    S.end()


def phase_attn(S, T, C, lay):
    S.begin()
    moba = (lay == 0)
    VW = 132 if moba else 260
    NV = 129 if moba else 257
    scale = 1.0 / math.sqrt(128.0)
    attnT = S.sb([128, DC, NLOC], BF16, dma=True, name="attnT")
    ident = S.sb([128, 128], BF16, dma=True, name="ident")
    ones = S.sb([128, 128], BF16, dma=True, name="ones")
    dm = S.sb([128, 1024], BF16, dma=True, name="dm")
    S.dma("sp", ident[:, :], C["ident"][:, :], ident, writes=[ident])
    S.dma("sp", ones[:, :], C["ones"][:, :], ones, writes=[ones])
    S.dma("sp", dm[:, :], C["dm"][:, :], dm, writes=[dm])
    KT = [S.sb([128, 8192], BF16, dma=True, name="KT") for _ in range(2)]
    VA = [S.sb([128, 64, VW], BF16, dma=True, name="VA") for _ in range(2)]
    QT = [S.sb([128, NLOC], BF16, dma=True, name="QT") for _ in range(2)]
    sT = [S.ps([128, 512], name="sT") for _ in range(3)]
    Ob = [S.ps([128, 512], name="Ob") for _ in range(2)]
    gps = S.ps([128, 512], name="gps")
    tp = S.ps([128, 1024], BF16, name="tp")
    eT = [S.sb([128, 512], BF16, name="eT") for _ in range(4)]
    ef32 = [S.sb([128, 512], F32, name="ef32") for _ in range(3)]
    LOOK = 2
    ot = [S.sb([128, 256], BF16, name="ot") for _ in range(2)]
    cnt = {"g": 0, "o": 0, "t": 0}
    for v in VA:
        S.op("dve", lambda e, v=v: e.memset(v[:, :, VW - 4:VW], 1.0), writes=[v])

    kT_all, v_all, qT_d = T["kT_all%d" % lay], T["v_all%d" % lay], T["qT%d" % lay]

    def load_kq(i, chunk):
        kt, qt = KT[i % 2], QT[i % 2]
        for r0 in range(0, 8, 4):
            S.dma("sp", kt[:, r0 * 1024:(r0 + 4) * 1024].rearrange("p (r n) -> p r n", r=4),
                  kT_all[r0:r0 + 4, chunk, :, :].rearrange("r p n -> p r n"), kt, writes=[kt])
        S.dma("sp", qt[:, :], qT_d[chunk, :, :], qt, writes=[qt])
        return kt, qt

    def load_v(i, col0, width):
        va = VA[i % 2]
        for r in range(8):
            S.dma("sp", va[:, r * 8:(r + 1) * 8, 0:width],
                  v_all[r, :, col0:col0 + width].rearrange("(j p) d -> p j d", p=128), va, writes=[va])
        return va

    def qk_exp(kt, qt, j, grp):
        jp, c0 = grp // 2, 4 * (grp % 2)
        s = sT[cnt["g"] % 3]
        et = eT[cnt["g"] % 4]
        cnt["g"] += 1

        def fn(e):
            ins = None
            for i in range(4):
                k0 = (c0 + i) * 1024 + jp * 128
                ins = e.matmul(s[:, i * 128:(i + 1) * 128], lhsT=kt[:, k0:k0 + 128], rhs=qt[:, j * 128:(j + 1) * 128],
                               start=True, stop=True)
            return ins
        S.op("pe", fn, reads=[kt, qt], writes=[s])
        ef = ef32[cnt["g"] % 3]
        S.op("act", lambda e: e.activation(out=ef[:, :], in_=s[:, :], func=AF.Exp, scale=scale), reads=[s], writes=[ef])
        if jp == j:
            S.op("dve", lambda e: e.tensor_tensor(out=et[:, :], in0=ef[:, :], in1=dm[:, c0 * 128:(c0 + 4) * 128],
                                                  op=ALU.mult), reads=[ef, dm], writes=[et])
        elif moba:
            S.op("act", lambda e: e.activation(out=et[:, :], in_=ef[:, :], func=AF.Copy), reads=[ef], writes=[et])
        else:
            S.op("dve", lambda e: e.tensor_copy(out=et[:, :], in_=ef[:, :]), reads=[ef], writes=[et])
        return et, jp, c0

    def transpose_out(o_ap_list, chunk0, j):
        for i, (ap, sb) in enumerate(o_ap_list):
            t0 = (cnt["t"] % 8) * 128
            cnt["t"] += 1
            S.op("pe", lambda e, ap=ap, t0=t0: e.transpose(tp[:, t0:t0 + 128], ap, ident[:, :]), reads=[sb, ident], writes=[tp])
            S.op("dve", lambda e, t0=t0, i=i: e.tensor_copy(out=attnT[:, chunk0 + i, j * 128:(j + 1) * 128],
                                                            in_=tp[:, t0:t0 + 128]), reads=[tp], writes=[attnT])

    if moba:
        cb = S.sb([128, 256], F32, dma=True, name="cb")
        cand = S.sb([128, 256], F32, dma=True, name="cand")
        own = S.sb([128, 256], F32, dma=True, name="own")
        S.dma("sp", cb[:, :], C["cb"][:, :], cb, writes=[cb])
        S.dma("sp", cand[:, :], C["cand"][:, :], cand, writes=[cand])
        S.dma("sp", own[:, :], C["own"][:, :], own, writes=[own])
        tsum = S.sb([128, 64], F32, name="tsum")
        ksum = S.sb([128, 32], F32, name="ksum")
        khi = S.sb([128, 32], BF16, name="khi")
        klo = S.sb([128, 32], BF16, name="klo")
        gm = S.sb([128, 32], F32, name="gm")
        top8 = S.sb([128, 8], F32, name="top8")
        mps = [S.sb([128, 32], F32, name="mp") for _ in range(2)]
        acc = S.sb([128, 132], F32, name="acc")
        rec = S.sb([128, 1], F32, name="rec")

        def moba_pv(et, jp, c0, va, mp, first_grp):
            for ml in range(2):
                o = Ob[cnt["o"] % 2]
                cnt["o"] += 1
                b = 4 * jp + c0 // 2 + ml

                def fn(e, et=et, o=o, ml=ml, jp=jp, c0=c0, va=va):
                    e.matmul(o[:, 0:NV], lhsT=et[:, (2 * ml) * 128:(2 * ml + 1) * 128],
                             rhs=va[:, (c0 + 2 * ml) * 8 + jp, 0:NV], start=True, stop=False)
                    return e.matmul(o[:, 0:NV], lhsT=et[:, (2 * ml + 1) * 128:(2 * ml + 2) * 128],
                                    rhs=va[:, (c0 + 2 * ml + 1) * 8 + jp, 0:NV], start=False, stop=True)
                S.op("pe", fn, reads=[et, va], writes=[o])
                if first_grp and ml == 0:
                    S.op("dve", lambda e, o=o, b=b, mp=mp: e.tensor_scalar(out=acc[:, 0:NV], in0=o[:, 0:NV], scalar1=mp[:, b:b + 1],
                                                                           scalar2=None, op0=ALU.mult), reads=[o, mp], writes=[acc])
                else:
                    S.op("dve", lambda e, o=o, b=b, mp=mp: e.scalar_tensor_tensor(
                        out=acc[:, 0:NV], in0=o[:, 0:NV], scalar=mp[:, b:b + 1], in1=acc[:, 0:NV],
                        op0=ALU.mult, op1=ALU.add), reads=[o, mp, acc], writes=[acc])

        def moba_fin(h, j):
            S.op("dve", lambda e: e.reciprocal(out=rec[:, :], in_=acc[:, 128:129]), reads=[acc], writes=[rec])
            o2 = ot[j % 2]
            S.op("dve", lambda e, o2=o2: e.tensor_scalar(out=o2[:, 0:128], in0=acc[:, 0:128], scalar1=rec[:, 0:1],
                                                         scalar2=None, op0=ALU.mult), reads=[acc, rec], writes=[o2])
            transpose_out([(o2[:, 0:128], o2)], h, j)

        for h in range(24):
            kt, qt = load_kq(h, h)
            va = load_v(h, h * 128, 128)
            S.op("dve", lambda e, kt=kt: e.reduce_sum(out=tsum[:, :], in_=kt[:, :].rearrange("p (t i) -> p t i", i=128),
                                                      axis=AX.X), reads=[kt], writes=[tsum])
            tv = tsum[:, :].rearrange("p (m two j) -> p m two j", two=2, j=8)
            S.op("dve", lambda e, tv=tv: e.tensor_tensor(out=ksum[:, :].rearrange("p (j m) -> p m j", m=4),
                                                         in0=tv[:, :, 0, :], in1=tv[:, :, 1, :], op=ALU.add),
                 reads=[tsum], writes=[ksum])
            S.op("dve", lambda e: e.tensor_copy(out=khi[:, :], in_=ksum[:, :]), reads=[ksum], writes=[khi])
            S.op("dve", lambda e: e.tensor_tensor(out=klo[:, :], in0=ksum[:, :], in1=khi[:, :], op=ALU.subtract),
                 reads=[ksum, khi], writes=[klo])
            pending = []
            for j in range(8):
                js = slice(j * 32, (j + 1) * 32)
                mp = mps[j % 2]
                mm_group(S, gps[:, 0:32], [(qt[:, j * 128:(j + 1) * 128], khi[:, :]), (qt[:, j * 128:(j + 1) * 128], klo[:, :])],
                         [qt, khi, klo], gps)
                S.op("dve", lambda e, js=js: e.tensor_tensor(out=gm[:, :], in0=gps[:, 0:32], in1=cb[:, js], op=ALU.add),
                     reads=[gps, cb], writes=[gm])
                S.op("dve", lambda e: e.max(out=top8[:, :], in_=gm[:, :]), reads=[gm], writes=[top8])
                S.op("dve", lambda e, js=js, mp=mp: e.scalar_tensor_tensor(out=mp[:, :], in0=gm[:, :], scalar=top8[:, 2:3],
                                                                           in1=cand[:, js], op0=ALU.is_ge, op1=ALU.mult),
                     reads=[gm, top8, cand], writes=[mp])
                S.op("dve", lambda e, js=js, mp=mp: e.tensor_tensor(out=mp[:, :], in0=mp[:, :], in1=own[:, js], op=ALU.add),
                     reads=[mp, own], writes=[mp])
                ng = 2 * (j + 1)
                for grp in range(ng):
                    et, jp, c0 = qk_exp(kt, qt, j, grp)
                    pending.append((et, jp, c0, mp, grp == 0, (j if grp == ng - 1 else None)))
                    if len(pending) > LOOK:
                        it = pending.pop(0)
                        moba_pv(it[0], it[1], it[2], va, it[3], it[4])
                        if it[5] is not None:
                            moba_fin(h, it[5])
            while pending:
                it = pending.pop(0)
                moba_pv(it[0], it[1], it[2], va, it[3], it[4])
                if it[5] is not None:
                    moba_fin(h, it[5])
    else:
        lamv = S.sb([128, 4], F32, dma=True, name="lamv")
        gsub = S.sb([128, 256], F32, dma=True, name="gsub")
        S.dma("sp", lamv[:, :], C["lamv"][:, :], lamv, writes=[lamv])
        S.dma("sp", gsub[:, :], C["gsub"][:, :], gsub, writes=[gsub])
        prod = S.sb([128, 2], F32, name="prod")
        phi = S.sb([128, 2], BF16, name="phi")
        plo = S.sb([128, 2], BF16, name="plo")
        ex = S.sb([128, 2], F32, name="ex")
        nlam = S.sb([128, 1], F32, name="nlam")
        S.op("dve", lambda e: e.tensor_tensor(out=prod[:, :], in0=lamv[:, 0:4:2], in1=lamv[:, 1:4:2], op=ALU.mult),
             reads=[lamv], writes=[prod])
        S.op("dve", lambda e: e.tensor_copy(out=phi[:, :], in_=prod[:, :]), reads=[prod], writes=[phi])
        S.op("dve", lambda e: e.tensor_tensor(out=plo[:, :], in0=prod[:, :], in1=phi[:, :], op=ALU.subtract),
             reads=[prod, phi], writes=[plo])
        mm_group(S, gps[:, 0:2], [(ones[:, :], phi[:, :]), (ones[:, :], plo[:, :])], [ones, phi, plo], gps)
        S.op("act", lambda e: e.activation(out=ex[:, :], in_=gps[:, 0:2], func=AF.Exp), reads=[gps], writes=[ex])
        S.op("dve", lambda e: e.tensor_tensor(out=nlam[:, :], in0=ex[:, 1:2], in1=ex[:, 0:1], op=ALU.subtract),
             reads=[ex], writes=[nlam])
        S.op("dve", lambda e: e.tensor_scalar(out=nlam[:, :], in0=nlam[:, :], scalar1=-LAM_INIT1, scalar2=None, op0=ALU.add),
             reads=[nlam], writes=[nlam])
        osb = S.sb([128, 2, 8, 260], F32, name="osb")
        rr = S.sb([128, 4], F32, name="rr")
        ta = S.sb([128, 256], F32, name="ta")
        tb = S.sb([128, 256], F32, name="tb")
        i = 0
        for hd in range(12):
            va = load_v(hd, hd * 256, 256)
            for comp in range(2):
                kt, qt = load_kq(i, 2 * hd + comp)
                i += 1
                pending = []

                def diff_pv(et, jp, c0, o, grp, ng, comp, j, va=va):
                    def fn(e):
                        ins = None
                        for t in range(4):
                            ins = e.matmul(o[:, 0:NV], lhsT=et[:, t * 128:(t + 1) * 128], rhs=va[:, (c0 + t) * 8 + jp, 0:NV],
                                           start=(grp == 0 and t == 0), stop=(grp == ng - 1 and t == 3))
                        return ins
                    S.op("pe", fn, reads=[et, va], writes=[o])
                    if grp == ng - 1:
                        S.op("dve", lambda e: e.tensor_copy(out=osb[:, comp, j, 0:NV], in_=o[:, 0:NV]), reads=[o], writes=[osb])

                for j in range(8):
                    o = Ob[cnt["o"] % 2]
                    cnt["o"] += 1
                    ng = 2 * (j + 1)
                    for grp in range(ng):
                        et, jp, c0 = qk_exp(kt, qt, j, grp)
                        pending.append((et, jp, c0, o, grp, ng, comp, j))
                        if len(pending) > LOOK:
                            diff_pv(*pending.pop(0))
                while pending:
                    diff_pv(*pending.pop(0))
            for j in range(8):
                S.op("dve", lambda e, j=j: e.reciprocal(out=rr[:, 0:1], in_=osb[:, 0, j, 256:257]), reads=[osb], writes=[rr])
                S.op("dve", lambda e, j=j: e.reciprocal(out=rr[:, 1:2], in_=osb[:, 1, j, 256:257]), reads=[osb, rr], writes=[rr])
                S.op("dve", lambda e: e.tensor_tensor(out=rr[:, 2:3], in0=rr[:, 1:2], in1=nlam[:, 0:1], op=ALU.mult),
                     reads=[rr, nlam], writes=[rr])
                S.op("dve", lambda e, j=j: e.tensor_scalar(out=ta[:, :], in0=osb[:, 0, j, 0:256], scalar1=rr[:, 0:1], scalar2=None,
                                                           op0=ALU.mult), reads=[osb, rr], writes=[ta])
                S.op("dve", lambda e, j=j: e.scalar_tensor_tensor(out=ta[:, :], in0=osb[:, 1, j, 0:256], scalar=rr[:, 2:3],
                                                                  in1=ta[:, :], op0=ALU.mult, op1=ALU.add),
                     reads=[osb, rr, ta], writes=[ta])
                S.op("dve", lambda e: e.tensor_tensor(out=tb[:, :], in0=ta[:, :], in1=ta[:, :], op=ALU.mult), reads=[ta], writes=[tb])
                S.op("dve", lambda e: e.reduce_sum(out=rr[:, 3:4], in_=tb[:, :], axis=AX.X), reads=[tb, rr], writes=[rr])
                rstd_from_ss(S, rr, rr[:, 3:4], rr, rr[:, 3:4], 1.0 / 256.0, SUBLN_EPS)
                S.op("dve", lambda e: e.tensor_scalar(out=ta[:, :], in0=ta[:, :], scalar1=rr[:, 3:4], scalar2=1.0 - LAM_INIT1,
                                                      op0=ALU.mult, op1=ALU.mult), reads=[ta, rr], writes=[ta])
                o2 = ot[j % 2]
                S.op("dve", lambda e, o2=o2: e.tensor_tensor(out=o2[:, :], in0=ta[:, :], in1=gsub[:, :], op=ALU.mult),
                     reads=[ta, gsub], writes=[o2])
                transpose_out([(o2[:, 0:128], o2), (o2[:, 128:256], o2)], 2 * hd, j)

    kmT = KT[0]
    qmT = KT[1]
    mva = VA[0]
    S.dma("sp", kmT[:, 0:2048].rearrange("p (c m) -> p c m", c=8), T["kmhT_d"][lay].rearrange("c p m -> p c m"), kmT, writes=[kmT])
    S.dma("sp", qmT[:, 0:8192].rearrange("p (c n) -> p c n", c=8), T["qmT%d" % lay].rearrange("c p n -> p c n"), qmT, writes=[qmT])
    mvt = S.sb([128, 8, 260], BF16, dma=True, name="mvt")
    S.op("dve", lambda e: e.memset(mvt[:, :, 256:260], 1.0), writes=[mvt])
    for mt in range(2):
        S.dma("sp", mvt[:, mt * 4:(mt + 1) * 4, 0:256], T["mv_d"][mt * 128:(mt + 1) * 128, :].rearrange("p (h d) -> p h d", h=4),
              mvt, writes=[mvt])
    orec = S.sb([128, 1], F32, name="orec")
    for hm in range(4):
        for half in range(2):
            ets = []
            for mt in range(2):
                s = sT[cnt["g"] % 3]
                et = eT[cnt["g"] % 4]
                cnt["g"] += 1
                pairs = [(kmT[:, (2 * hm + ch) * 256 + mt * 128:(2 * hm + ch) * 256 + (mt + 1) * 128],
                          qmT[:, (2 * hm + ch) * 1024 + half * 512:(2 * hm + ch) * 1024 + (half + 1) * 512]) for ch in range(2)]
                mm_group(S, s[:, :], pairs, [kmT, qmT], s)
                ef = ef32[cnt["g"] % 3]
                S.op("act", lambda e, s=s, ef=ef: e.activation(out=ef[:, :], in_=s[:, :], func=AF.Exp, scale=1.0 / 16.0),
                     reads=[s], writes=[ef])
                S.op("dve", lambda e, ef=ef, et=et: e.tensor_copy(out=et[:, :], in_=ef[:, :]), reads=[ef], writes=[et])
                ets.append(et)
            for qt_ in range(4):
                j = half * 4 + qt_
                o = Ob[cnt["o"] % 2]
                cnt["o"] += 1
                pairs = [(ets[mt][:, qt_ * 128:(qt_ + 1) * 128], mvt[:, mt * 4 + hm, 0:257]) for mt in range(2)]
                mm_group(S, o[:, 0:257], pairs, ets + [mvt], o)
                S.op("dve", lambda e, o=o: e.reciprocal(out=orec[:, :], in_=o[:, 256:257]), reads=[o], writes=[orec])
                o2 = ot[j % 2]
                S.op("dve", lambda e, o=o, o2=o2: e.tensor_scalar(out=o2[:, :], in0=o[:, 0:256], scalar1=orec[:, 0:1], scalar2=None,
                                                                  op0=ALU.mult), reads=[o, orec], writes=[o2])
                transpose_out([(o2[:, 0:128], o2), (o2[:, 128:256], o2)], 24 + 2 * hm, j)
    for q4 in range(4):
        S.dma("sp", T["attnT_d"][q4 * 8:(q4 + 1) * 8].rearrange("c p n -> p c n"), attnT[:, q4 * 8:(q4 + 1) * 8, :], attnT,
              reads=[attnT])
    S.end()


W_SHAPES = {"w_in": [PROJ_W // 256, 128, DC, 256], "w_out": [D // 256, 128, DC, 256], "w_gate": [DFF // 256, 128, DC, 256],
            "w_up": [DFF // 256, 128, DC, 256], "w_down": [D // 256, 128, FC, 256]}


def decl_handoff(P, lay, kinds):
    P.t("qT%d" % lay, [24, 128, NLOC], BF16, kinds["qT"])
    P.t("qmT%d" % lay, [8, 128, NLOC], BF16, kinds["qmT"])
    if kinds.get("kT_loc"):
        h = P.t("kT_loc%d" % lay, [24 * 128, NLOC], BF16, kinds["kT_loc"])
        P.T["kT_loc%d" % lay] = h.ap().rearrange("(c p) n -> c p n", p=128)
        P.t("v_loc%d" % lay, [NLOC, SELF_W], BF16, kinds["v_loc"])
    if kinds.get("kT_all"):
        h = P.t("kT_all%d" % lay, [8 * 24 * 128, NLOC], BF16, kinds["kT_all"])
        P.T["kT_all%d" % lay] = h.ap().rearrange("(r c p) n -> r c p n", r=8, p=128)
        h = P.t("v_all%d" % lay, [8 * NLOC, SELF_W], BF16, kinds["v_all"])
        P.T["v_all%d" % lay] = h.ap().rearrange("(r n) f -> r n f", r=8)


def emit_tail(S, P, lay, x_in, x_out):
    phase_attn(S, P.T, P.C, lay)
    phase_outproj(S, P.T, P.C, lay, x_in, P.T["xT_mid"])
    phase_ffn1(S, P.T, P.C, lay, P.T["xT_mid"])
    phase_ffn2(S, P.T, P.C, lay, P.T["xT_mid"], x_out)


def decl_scratch(P, dbg=False):
    k = "out" if dbg else "int"
    P.t("attnT_d", [DC, 128, NLOC], BF16, k)
    P.t("hff_d", [FC, 128, NLOC], BF16, "int")
    P.t("xT_mid", [D, NLOC], F32, k)
    P.t("kmhT_d", [2, 8, 128, MEM_LEN], BF16, k)
    P.t("mv_d", [MEM_LEN, MEM_W], BF16, k)


def build_A(lay=0):
    P = Prog()
    P.consts(["ones", "perm", "pos32", "gq%d" % lay, "g_attn%d" % lay])
    P.t("xT_in%d" % lay, [D, NLOC], F32, "in")
    P.t("w_in%d" % lay, W_SHAPES["w_in"], F32, "in")
    decl_handoff(P, lay, {"qT": "out", "qmT": "out", "kT_loc": "out", "v_loc": "out"})
    with ExitStack() as st:
        S = Sched(P.nc, st)
        phase_inproj(S, P.T, P.C, lay)
    return P


def build_B(lay, with_next, dbg=False):
    P = Prog()
    cn = ["ones", "ident", "dm", "g_ffn%d" % lay, "g_memn", "gmk"]
    cn += ["cb", "cand", "own"] if lay == 0 else ["lamv", "gsub"]
    if with_next:
        cn += ["perm", "pos32", "gq%d" % (lay + 1), "g_attn%d" % (lay + 1)]
    P.consts(cn)
    P.t("xT_in%d" % lay, [D, NLOC], F32, "in")
    P.t("memT", [D, MEM_LEN], F32, "in")
    P.t("w_mem_kv", [8, 128, DC, 256], F32, "in")
    for w in ("w_out", "w_gate", "w_up", "w_down"):
        P.t("%s%d" % (w, lay), W_SHAPES[w], F32, "in")
    decl_handoff(P, lay, {"qT": "in", "qmT": "in", "kT_all": "in", "v_all": "in"})
    decl_scratch(P, dbg)
    P.t("xT_in%d" % (lay + 1), [D, NLOC], F32, "out")
    if with_next:
        P.t("w_in%d" % (lay + 1), W_SHAPES["w_in"], F32, "in")
        decl_handoff(P, lay + 1, {"qT": "out", "qmT": "out", "kT_loc": "out", "v_loc": "out"})
    with ExitStack() as st:
        S = Sched(P.nc, st)
        phase_memkv(S, P.T, P.C)
        emit_tail(S, P, lay, P.T["xT_in%d" % lay], P.T["xT_in%d" % (lay + 1)])
        if with_next:
            phase_inproj(S, P.T, P.C, lay + 1)
    return P


def phase_gather(S, P, lay):
    S.begin()
    a, b = Buf(None), Buf(None)
    S.allgather(P.T["#kT_loc%d" % lay], P.T["#kT_all%d" % lay], [], [a])
    S.allgather(P.T["#v_loc%d" % lay], P.T["#v_all%d" % lay], [], [b])
    S.end()


def build_fused():
    P = Prog()
    P.consts(list(CONST_SPECS.keys()))
    P.t("xT_in0", [D, NLOC], F32, "in")
    P.t("memT", [D, MEM_LEN], F32, "in")
    P.t("w_mem_kv", [8, 128, DC, 256], F32, "in")
    for lay in range(2):
        for w in ("w_in", "w_out", "w_gate", "w_up", "w_down"):
            P.t("%s%d" % (w, lay), W_SHAPES[w], F32, "in")
        decl_handoff(P, lay, {"qT": "int", "qmT": "int", "kT_loc": "int", "v_loc": "int", "kT_all": "int", "v_all": "int"})
    decl_scratch(P)
    P.t("xT_in1", [D, NLOC], F32, "int")
    P.t("xT_in2", [D, NLOC], F32, "out")
    with ExitStack() as st:
        S = Sched(P.nc, st)
        phase_memkv(S, P.T, P.C)
        for lay in range(2):
            phase_inproj(S, P.T, P.C, lay)
            phase_gather(S, P, lay)
            emit_tail(S, P, lay, P.T["xT_in%d" % lay], P.T["xT_in%d" % (lay + 1)])
    return P


FUSED = True


def _tile_w(w):
    K_, F_ = w.shape
    return np.ascontiguousarray(w.reshape(K_ // 128, 128, F_ // 256, 256).transpose(2, 1, 0, 3))


def _weights(inputs, P):
    m = {}
    for n in P.ins:
        for w in ("w_in", "w_out", "w_gate", "w_up", "w_down"):
            if n.startswith(w) and n[len(w):] in ("0", "1"):
                m[n] = _tile_w(np.asarray(inputs[w])[int(n[len(w):])])
    if "w_mem_kv" in P.ins:
        m["w_mem_kv"] = _tile_w(np.asarray(inputs["w_mem_kv"]))
    if "memT" in P.ins:
        m["memT"] = np.ascontiguousarray(np.asarray(inputs["mem"])[0].T)
    return m


def _run(P, maps):
    res = run_bass_kernel_spmd(P.nc, maps, core_ids=list(range(NCORES)))
    return res.results


def kernel(**inputs):
    x = np.asarray(inputs["x"])[0]
    hcs = [host_consts(c, inputs) for c in range(NCORES)]
    xT = [np.ascontiguousarray(x[core_token_index(c)].T) for c in range(NCORES)]
    if FUSED:
        P = build_fused()
        w = _weights(inputs, P)
        maps = []
        for c in range(NCORES):
            m = {n: hcs[c][n] for n in P.ins if n in hcs[c]}
            m.update(w)
            m["xT_in0"] = xT[c]
            maps.append(m)
        outs = _run(P, maps)
        fin = [np.asarray(o["xT_in2"]) for o in outs]
    else:
        PA = build_A(0)
        w = _weights(inputs, PA)
        maps = []
        for c in range(NCORES):
            m = {n: hcs[c][n] for n in PA.ins if n in hcs[c]}
            m.update(w)
            m["xT_in0"] = xT[c]
            maps.append(m)
        prev = _run(PA, maps)
        cur_x = xT
        for lay in range(2):
            PB = build_B(lay, with_next=(lay == 0))
            w = _weights(inputs, PB)
            kT_all = np.concatenate([np.asarray(prev[c]["kT_loc%d" % lay]) for c in range(NCORES)], axis=0)
            v_all = np.concatenate([np.asarray(prev[c]["v_loc%d" % lay]) for c in range(NCORES)], axis=0)
            maps = []
            for c in range(NCORES):
                m = {n: hcs[c][n] for n in PB.ins if n in hcs[c]}
                m.update(w)
                m["xT_in%d" % lay] = cur_x[c]
                m["qT%d" % lay] = np.asarray(prev[c]["qT%d" % lay])
                m["qmT%d" % lay] = np.asarray(prev[c]["qmT%d" % lay])
                m["kT_all%d" % lay] = kT_all
                m["v_all%d" % lay] = v_all
                maps.append(m)
            prev = _run(PB, maps)
            cur_x = [np.asarray(prev[c]["xT_in%d" % (lay + 1)]) for c in range(NCORES)]
        fin = cur_x
    out = np.zeros((SEQ, D), np.float32)
    for c in range(NCORES):
        out[core_token_index(c)] = fin[c].T
    return out[None]
```

```python
import math
from contextlib import ExitStack

import numpy as np
import ml_dtypes

import concourse.bass as bass
import concourse.mybir as mybir
from concourse.bass_utils import run_bass_kernel_spmd

F32 = mybir.dt.float32
BF16 = mybir.dt.bfloat16
I32 = mybir.dt.int32
ALU = mybir.AluOpType
AF = mybir.ActivationFunctionType
AX = mybir.AxisListType

NCORES = 8
D = 4096
SEQ = 8192
NLOC = SEQ // NCORES
DC = D // 128
SELF_W = 3072
MEM_W = 1024
PROJ_W = 10240
DFF = 11008
FC = DFF // 128
MEM_LEN = 256
NORM_EPS = 1e-6
SUBLN_EPS = 1e-5
ROPE_THETA = 500000.0
LAM_INIT1 = 0.8 - 0.6 * math.exp(-0.3 * 1)
TWO_PI = 2.0 * math.pi
NDMASEM = 40


class Buf:
    def __init__(self, t, dsem=None):
        self.t = t
        self.w = None
        self.r = []
        self.dsem = dsem

    def __getitem__(self, k):
        return self.t[k]


class Sched:
    CE = ("pe", "act", "dve", "pool")

    def __init__(self, nc, stack):
        self.nc = nc
        self.sem = {}
        self.cnt = {}
        for e in self.CE:
            self.sem[e] = stack.enter_context(nc.semaphore("s_" + e))
            self.cnt[e] = 0
        for i in range(NDMASEM):
            self.sem[("d", i)] = stack.enter_context(nc.semaphore("d%d" % i))
            self.cnt[("d", i)] = 0
        self.sem["cc"] = stack.enter_context(nc.semaphore("ccs"))
        self.cnt["cc"] = 0
        self.free_d = [i for i in range(NDMASEM) if getattr(self.sem[("d", i)], "num", 0) != 192]
        self.phase_d = []
        self.ops = {e: [] for e in ("pe", "act", "dve", "pool", "sp")}
        self.waited = {e: {} for e in ("pe", "act", "dve", "pool", "sp")}
        self.pstack = None
        self.uid = 0

    def begin(self):
        self.pstack = ExitStack()
        self.pstack.__enter__()
        self.ops = {e: [] for e in self.ops}
        self.phase_d = []

    def sb(self, shape, dt, dma=False, name=None):
        self.uid += 1
        t = self.pstack.enter_context(self.nc.sbuf_tensor("%s_%d" % (name or "sb", self.uid), list(shape), dt))
        ds = None
        if dma:
            ds = self.free_d.pop()
            self.phase_d.append(ds)
        return Buf(t, ds)

    def ps(self, shape, dt=F32, name=None):
        self.uid += 1
        t = self.pstack.enter_context(self.nc.psum_tensor("%s_%d" % (name or "ps", self.uid), list(shape), dt))
        return Buf(t)

    def dr(self, t):
        return Buf(t)

    def end(self):
        nc = self.nc
        finals = [(k, v) for k, v in self.cnt.items() if v > 0]
        with nc.Block() as block:
            def emit(ename, eng):
                waited = self.waited[ename]
                for deps, fn, inc in self.ops[ename]:
                    for (k, v) in deps:
                        if k == ename and ename == "pe":
                            continue
                        if waited.get(k, 0) >= v:
                            continue
                        eng.wait_ge(self.sem[k], v)
                        waited[k] = v
                    ins = fn(eng)
                    if inc is not None:
                        ins.then_inc(self.sem[inc[0]], inc[1])
                for (k, v) in finals:
                    if waited.get(k, 0) >= v:
                        continue
                    eng.wait_ge(self.sem[k], v)
                    waited[k] = v

            @block.tensor
            def _(e):
                emit("pe", e)

            @block.scalar
            def _(e):
                emit("act", e)

            @block.vector
            def _(e):
                emit("dve", e)

            @block.gpsimd
            def _(e):
                emit("pool", e)

            @block.sync
            def _(e):
                emit("sp", e)
        self.free_d.extend(self.phase_d)
        self.phase_d = []
        self.pstack.__exit__(None, None, None)
        self.pstack = None

    def _deps(self, reads, writes):
        deps = []
        for b in reads:
            if b.w is not None:
                deps.append(b.w)
        for b in writes:
            if b.w is not None:
                deps.append(b.w)
            deps.extend(b.r)
        return deps

    def _commit(self, tok, reads, writes):
        for b in writes:
            b.w = tok
            b.r = []
        for b in reads:
            if b not in writes:
                b.r.append(tok)

    def op(self, eng, fn, reads=(), writes=()):
        deps = self._deps(reads, writes)
        self.cnt[eng] += 1
        tok = (eng, self.cnt[eng])
        self.ops[eng].append((deps, fn, (eng, 1)))
        self._commit(tok, reads, writes)
        return tok

    def dma(self, q, out_ap, in_ap, semb, reads=(), writes=()):
        import os
        if os.environ.get("KNOSTORE") == "1" and len(writes) == 0:
            return None
        deps = self._deps(reads, writes)
        k = ("d", semb.dsem)
        self.cnt[k] += 16
        tok = (k, self.cnt[k])
        self.ops[q].append((deps, lambda e: e.dma_start(out=out_ap, in_=in_ap), (k, 16)))
        self._commit(tok, reads, writes)
        return tok

    def allgather(self, in_t, out_t, reads, writes):
        deps = self._deps(reads, writes)
        self.cnt["cc"] += 1
        tok = ("cc", self.cnt["cc"])

        def fn(e):
            return e.collective_compute("AllGather", ALU.bypass, replica_groups=[list(range(NCORES))],
                                        ins=[in_t.ap().opt()], outs=[out_t.ap().opt()])
        self.ops["pool"].append((deps, fn, ("cc", 1)))
        self._commit(tok, reads, writes)
        return tok


def mm_group(S, out_ap, pairs, reads, psb):
    def fn(e):
        ins = None
        n = len(pairs)
        for i, (l, r) in enumerate(pairs):
            ins = e.matmul(out_ap, lhsT=l, rhs=r, start=(i == 0), stop=(i == n - 1))
        return ins
    return S.op("pe", fn, reads=reads, writes=[psb])


def load_consts(S, C):
    k = {}
    k["ones"] = S.sb([128, 128], BF16, dma=True, name="ones")
    S.dma("sp", k["ones"][:, :], C["ones"][:, :], k["ones"], writes=[k["ones"]])
    return k


def rstd_from_ss(S, ob, o_ap, sb_, s_ap, scale, eps):
    S.op("act", lambda e: e.activation(out=o_ap, in_=s_ap, func=AF.Sqrt, bias=float(eps), scale=float(scale)),
         reads=[sb_], writes=[ob])
    S.op("dve", lambda e: e.reciprocal(out=o_ap, in_=o_ap), reads=[ob], writes=[ob])


def phase_norm(S, xT_d, gcols_d, hT, ones, ss):
    xs = [S.sb([128, NLOC], F32, dma=True, name="xs") for _ in range(2)]
    sq = [S.sb([128, NLOC], BF16, name="sq") for _ in range(2)]
    gc = S.sb([128, DC], F32, dma=True, name="gc")
    rstd = S.sb([128, NLOC], F32, name="rstd")
    S.dma("sp", gc[:, :], gcols_d[:, :], gc, writes=[gc])
    for c in range(DC):
        x = xs[c % 2]
        q = sq[c % 2]
        S.dma("sp", x[:, :], xT_d[c * 128:(c + 1) * 128, :], x, writes=[x])
        S.op("act", lambda e, x=x, q=q: e.activation(out=q[:, :], in_=x[:, :], func=AF.Square), reads=[x], writes=[q])
        S.op("dve", lambda e, x=x, c=c: e.tensor_scalar(out=hT[:, c, :], in0=x[:, :], scalar1=gc[:, c:c + 1],
                                                        scalar2=None, op0=ALU.mult), reads=[x, gc], writes=[hT])
        for h in range(2):
            def fn(e, q=q, h=h, c=c):
                return e.matmul(ss[h][:, :], lhsT=ones[:, :], rhs=q[:, h * 512:(h + 1) * 512],
                                start=(c == 0), stop=(c == DC - 1))
            S.op("pe", fn, reads=[q, ones], writes=[ss[h]])
    for h in range(2):
        sl = slice(h * 512, (h + 1) * 512)
        rstd_from_ss(S, rstd, rstd[:, sl], ss[h], ss[h][:, :], 1.0 / D, NORM_EPS)
    for c in range(DC):
        S.op("dve", lambda e, c=c: e.tensor_tensor(out=hT[:, c, :], in0=hT[:, c, :], in1=rstd[:, :], op=ALU.mult),
             reads=[hT, rstd], writes=[hT])


def load_w(S, wb, w_view, c0, c1, f0, f1, nsplit=4):
    g = f0 // 256
    n = c1 - c0
    step = (n + nsplit - 1) // nsplit
    for s in range(0, n, step):
        e = min(n, s + step)
        S.dma("pool", wb[:, s:e, 0:f1 - f0], w_view[g, :, c0 + s:c0 + e, :], wb, writes=[wb])


def phase_inproj(S, T, C, lay):
    S.begin()
    K = load_consts(S, C)
    ones = K["ones"]
    hT = S.sb([128, DC, NLOC], BF16, name="hT")
    ssb = [S.ps([128, 512], name="ssq") for _ in range(2)]
    import os
    if os.environ.get("KNORM", "1") == "1":
        phase_norm(S, T["xT_in%d" % lay], C["g_attn%d" % lay], hT, ones, ssb)
    perm = S.sb([32, 32], BF16, dma=True, name="perm")
    S.dma("sp", perm[:, :], C["perm"][:, :], perm, writes=[perm])
    gq = S.sb([128, 6], F32, dma=True, name="gq")
    S.dma("sp", gq[:, :], C["gq%d" % lay][:, :], gq, writes=[gq])
    posi = S.sb([32, NLOC], I32, dma=True, name="posi")
    S.dma("sp", posi[:, :], C["pos32"][:, :], posi, writes=[posi])
    ang = S.sb([32, NLOC], F32, name="ang")
    tmp = S.sb([32, NLOC], F32, name="tmpa")
    cosT = S.sb([32, NLOC], F32, name="cosT")
    sinT = S.sb([32, NLOC], F32, name="sinT")
    S.op("dve", lambda e: e.tensor_copy(out=ang[:, :], in_=posi[:, :]), reads=[posi], writes=[ang])
    S.op("dve", lambda e: e.tensor_scalar(out=ang[:, :], in0=ang[:, :], scalar1=gq[0:32, 4:5], scalar2=None,
                                          op0=ALU.mult), reads=[ang, gq], writes=[ang])
    ki = S.sb([32, NLOC], I32, name="ki")
    kf = S.sb([32, NLOC], F32, name="kf")

    def sin_of(dst, offset):
        if offset != 0.0:
            S.op("dve", lambda e: e.tensor_scalar(out=tmp[:, :], in0=ang[:, :], scalar1=float(offset), scalar2=None,
                                                  op0=ALU.add), reads=[ang], writes=[tmp])
        else:
            S.op("dve", lambda e: e.tensor_copy(out=tmp[:, :], in_=ang[:, :]), reads=[ang], writes=[tmp])
        S.op("dve", lambda e: e.tensor_scalar(out=kf[:, :], in0=tmp[:, :], scalar1=1.0 / TWO_PI, scalar2=None,
                                              op0=ALU.mult), reads=[tmp], writes=[kf])
        S.op("dve", lambda e: e.tensor_copy(out=ki[:, :], in_=kf[:, :]), reads=[kf], writes=[ki])
        S.op("dve", lambda e: e.tensor_copy(out=kf[:, :], in_=ki[:, :]), reads=[ki], writes=[kf])
        S.op("dve", lambda e: e.scalar_tensor_tensor(out=tmp[:, :], in0=kf[:, :], scalar=-TWO_PI, in1=tmp[:, :],
                                                     op0=ALU.mult, op1=ALU.add), reads=[kf, tmp], writes=[tmp])
        S.op("dve", lambda e: e.tensor_scalar(out=kf[:, :], in0=tmp[:, :], scalar1=math.pi, scalar2=-TWO_PI,
                                              op0=ALU.is_gt, op1=ALU.mult), reads=[tmp], writes=[kf])
        S.op("dve", lambda e: e.tensor_tensor(out=tmp[:, :], in0=tmp[:, :], in1=kf[:, :], op=ALU.add),
             reads=[tmp, kf], writes=[tmp])
        S.op("dve", lambda e: e.tensor_scalar(out=kf[:, :], in0=tmp[:, :], scalar1=-math.pi, scalar2=TWO_PI,
                                              op0=ALU.is_lt, op1=ALU.mult), reads=[tmp], writes=[kf])
        S.op("dve", lambda e: e.tensor_tensor(out=tmp[:, :], in0=tmp[:, :], in1=kf[:, :], op=ALU.add),
             reads=[tmp, kf], writes=[tmp])
        S.op("dve", lambda e: e.tensor_scalar(out=kf[:, :], in0=tmp[:, :], scalar1=-1.0, scalar2=math.pi,
                                              op0=ALU.mult, op1=ALU.add), reads=[tmp], writes=[kf])
        S.op("dve", lambda e: e.tensor_tensor(out=kf[:, :], in0=kf[:, :], in1=tmp[:, :], op=ALU.min),
             reads=[tmp, kf], writes=[kf])
        S.op("dve", lambda e: e.tensor_scalar(out=tmp[:, :], in0=tmp[:, :], scalar1=-1.0, scalar2=-math.pi,
                                              op0=ALU.mult, op1=ALU.add), reads=[tmp], writes=[tmp])
        S.op("dve", lambda e: e.tensor_tensor(out=tmp[:, :], in0=kf[:, :], in1=tmp[:, :], op=ALU.max),
             reads=[tmp, kf], writes=[tmp])
        S.op("dve", lambda e: e.tensor_tensor(out=kf[:, :], in0=tmp[:, :], in1=tmp[:, :], op=ALU.mult),
             reads=[tmp], writes=[kf])
        cs = [-1.0 / 39916800.0, 1.0 / 362880.0, -1.0 / 5040.0, 1.0 / 120.0, -1.0 / 6.0]
        S.op("dve", lambda e: e.tensor_scalar(out=dst[:, :], in0=kf[:, :], scalar1=cs[0], scalar2=None, op0=ALU.mult),
             reads=[kf], writes=[dst])
        for cc in cs[1:]:
            S.op("dve", lambda e, cc=cc: e.scalar_tensor_tensor(out=dst[:, :], in0=dst[:, :], scalar=cc, in1=kf[:, :],
                                                                op0=ALU.add, op1=ALU.mult), reads=[dst, kf], writes=[dst])
        S.op("dve", lambda e: e.scalar_tensor_tensor(out=dst[:, :], in0=dst[:, :], scalar=1.0, in1=tmp[:, :],
                                                     op0=ALU.add, op1=ALU.mult), reads=[dst, tmp], writes=[dst])

    if os.environ.get("KTAB", "1") == "1":
        sin_of(sinT, 0.0)
        sin_of(cosT, 0.5 * math.pi)
    S.op("dve", lambda e: e.tensor_scalar(out=sinT[:, :], in0=sinT[:, :], scalar1=gq[0:32, 5:6], scalar2=None,
                                          op0=ALU.mult), reads=[sinT, gq], writes=[sinT])

    wv = T["w_in%d" % lay]
    wb = [S.sb([128, DC, 256], BF16, dma=True, name="wb") for _ in range(2)]
    pq = [S.ps([128, 512], name="pq") for _ in range(4)]
    prb = [S.ps([128, 512], name="prp") for _ in range(2)]
    sqb = [S.sb([128, 512], BF16, name="sqb") for _ in range(4)]
    qgb = [S.sb([128, 512], BF16, name="qgb") for _ in range(4)]
    rsb = [S.sb([128, 512], F32, name="rsb") for _ in range(2)]
    t1b = [S.sb([32, 512], F32, name="t1b") for _ in range(2)]
    t2b = [S.sb([32, 512], F32, name="t2b") for _ in range(2)]
    ob = [S.sb([128, 512], BF16, dma=True, name="ob") for _ in range(4)]
    vst = [S.sb([128, 8, 256], BF16, dma=True, name="vst") for _ in range(2)]
    vf32 = S.sb([128, 256], F32, name="vf32")
    praw = [S.sb([128, 512], F32, name="praw") for _ in range(2)]
    qT_d, kT_d, v_d, qmT_d = T["qT%d" % lay], T["kT_loc%d" % lay], T["v_loc%d" % lay], T["qmT%d" % lay]
    ctr = {"pq": 0, "t": 0, "ob": 0}

    def proj_tile(w, ch, half):
        p = pq[ctr["pq"] % 4]
        ctr["pq"] += 1
        pairs = [(w[:, c, ch * 128:(ch + 1) * 128], hT[:, c, half * 512:(half + 1) * 512]) for c in range(DC)]
        mm_group(S, p[:, :], pairs, [w, hT], p)
        return p

    import os
    for g in [int(t) for t in os.environ.get('KGROUPS', ','.join(str(i) for i in range(40))).split(',') if t != '']:
        w = wb[g % 2]
        load_w(S, w, wv, 0, DC, g * 256, (g + 1) * 256)
        if os.environ.get('KONLYLOAD') == '1':
            continue
        if g < 24:
            isq = g < 12
            gcol = 0 if isq else 1
            for ch in range(2):
                head = (g % 12) * 2 + ch
                for half in range(2):
                    hs = slice(half * 512, (half + 1) * 512)
                    p = proj_tile(w, ch, half)
                    i = ctr["t"] % 4
                    i2 = ctr["t"] % 2
                    ctr["t"] += 1
                    sqt, qg, rs, t1, t2, ssp, prp = sqb[i], qgb[i], rsb[i2], t1b[i2], t2b[i2], ssb[i2], prb[i2]
                    o = ob[ctr["ob"] % 4]
                    ctr["ob"] += 1
                    pr_ = praw[i2]
                    S.op("act", lambda e, p=p, pr_=pr_: e.activation(out=pr_[:, :], in_=p[:, :], func=AF.Copy),
                         reads=[p], writes=[pr_])
                    S.op("act", lambda e, pr_=pr_, sqt=sqt: e.activation(out=sqt[:, :], in_=pr_[:, :], func=AF.Square),
                         reads=[pr_], writes=[sqt])
                    S.op("dve", lambda e, pr_=pr_, qg=qg, gcol=gcol: e.tensor_scalar(
                        out=qg[:, :], in0=pr_[:, :], scalar1=gq[:, gcol:gcol + 1], scalar2=None, op0=ALU.mult),
                        reads=[pr_, gq], writes=[qg])
                    mm_group(S, ssp[:, :], [(ones[:, :], sqt[:, :])], [ones, sqt], ssp)
                    mm_group(S, prp[0:32, :], [(perm[:, :], qg[0:32, :])], [perm, qg], prp)
                    rstd_from_ss(S, rs, rs[:, :], ssp, ssp[:, :], 1.0 / 128.0, NORM_EPS)
                    S.op("dve", lambda e, t1=t1, qg=qg, hs=hs: e.tensor_tensor(
                        out=t1[:, :], in0=qg[0:32, :], in1=cosT[:, hs], op=ALU.mult), reads=[qg, cosT], writes=[t1])
                    S.op("dve", lambda e, t2=t2, prp=prp, hs=hs: e.tensor_tensor(
                        out=t2[:, :], in0=prp[0:32, :], in1=sinT[:, hs], op=ALU.mult), reads=[prp, sinT], writes=[t2])
                    S.op("dve", lambda e, t1=t1, t2=t2, qg=qg: e.tensor_tensor(
                        out=qg[0:32, :], in0=t1[:, :], in1=t2[:, :], op=ALU.add), reads=[t1, t2], writes=[qg])
                    S.op("dve", lambda e, o=o, qg=qg, rs=rs: e.scalar_tensor_tensor(
                        out=o[:, :], in0=qg[:, :], scalar=1.0, in1=rs[:, :], op0=ALU.mult, op1=ALU.mult),
                        reads=[qg, rs], writes=[o])
                    dst = (qT_d if isq else kT_d)[head, :, hs]
                    S.dma("sp", dst, o[:, :], o, reads=[o])
        elif g < 36:
            vs = vst[g % 2]
            for tt in range(8):
                p = pq[ctr["pq"] % 4]
                ctr["pq"] += 1
                pairs = [(hT[:, c, tt * 128:(tt + 1) * 128], w[:, c, 0:256]) for c in range(DC)]
                mm_group(S, p[:, 0:256], pairs, [w, hT], p)
                if os.environ.get('KMMONLY') == '1':
                    continue
                eng = os.environ.get("KVENG") or ("act" if tt % 2 == 0 else "dve")
                if eng == "act":
                    S.op("act", lambda e, p=p: e.activation(out=vf32[:, :], in_=p[:, 0:256], func=AF.Copy),
                         reads=[p], writes=[vf32])
                    S.op("dve", lambda e, vs=vs, tt=tt: e.tensor_copy(out=vs[:, tt, :], in_=vf32[:, :]),
                         reads=[vf32], writes=[vs])
                else:
                    S.op("dve", lambda e, p=p, vs=vs, tt=tt: e.tensor_copy(out=vs[:, tt, :], in_=p[:, 0:256]),
                         reads=[p], writes=[vs])
            c0 = (g - 24) * 256
            if os.environ.get('KMMONLY') == '1':
                continue
            S.dma("sp", v_d.rearrange("(t p) f -> p t f", p=128)[:, :, c0:c0 + 256], vs[:, :, :], vs, reads=[vs])
        else:
            hm = g - 36
            for half in range(2):
                hs = slice(half * 512, (half + 1) * 512)
                i2 = ctr["t"] % 2
                ssp, rs = ssb[i2], rsb[i2]
                qgs = []
                for ch in range(2):
                    p = proj_tile(w, ch, half)
                    i = ctr["t"] % 4
                    ctr["t"] += 1
                    sqt, qg = sqb[i], qgb[i]
                    qgs.append(qg)
                    pr_ = praw[ch]
                    S.op("act", lambda e, p=p, pr_=pr_: e.activation(out=pr_[:, :], in_=p[:, :], func=AF.Copy),
                         reads=[p], writes=[pr_])
                    S.op("act", lambda e, pr_=pr_, sqt=sqt: e.activation(out=sqt[:, :], in_=pr_[:, :], func=AF.Square),
                         reads=[pr_], writes=[sqt])
                    S.op("dve", lambda e, pr_=pr_, qg=qg, ch=ch: e.tensor_scalar(
                        out=qg[:, :], in0=pr_[:, :], scalar1=gq[:, 2 + ch:3 + ch], scalar2=None, op0=ALU.mult),
                        reads=[pr_, gq], writes=[qg])

                    def fn(e, ssp=ssp, sqt=sqt, ch=ch):
                        return e.matmul(ssp[:, :], lhsT=ones[:, :], rhs=sqt[:, :], start=(ch == 0), stop=(ch == 1))
                    S.op("pe", fn, reads=[ones, sqt], writes=[ssp])
                rstd_from_ss(S, rs, rs[:, :], ssp, ssp[:, :], 1.0 / 256.0, NORM_EPS)
                for ch in range(2):
                    o = ob[ctr["ob"] % 4]
                    ctr["ob"] += 1
                    qg = qgs[ch]
                    S.op("dve", lambda e, o=o, qg=qg, rs=rs: e.scalar_tensor_tensor(
                        out=o[:, :], in0=qg[:, :], scalar=1.0, in1=rs[:, :], op0=ALU.mult, op1=ALU.mult),
                        reads=[qg, rs], writes=[o])
                    S.dma("sp", qmT_d[hm * 2 + ch, :, hs], o[:, :], o, reads=[o])
    S.end()


def core_token_index(c):
    j = np.arange(8)[:, None]
    i = np.arange(128)[None, :]
    return (1024 * j + 128 * c + i).reshape(-1)


def host_consts(c, inputs):
    bf = ml_dtypes.bfloat16
    K = {}
    K["ones"] = np.ones((128, 128), bf)
    K["ident"] = np.eye(128, dtype=np.float32).astype(bf)
    perm = np.zeros((32, 32), np.float32)
    for m in range(32):
        perm[(m + 16) % 32, m] = 1.0
    K["perm"] = perm.astype(bf)
    idx = core_token_index(c)
    pos = np.asarray(inputs["positions"]).reshape(-1)[idx].astype(np.int32)
    K["pos32"] = np.ascontiguousarray(np.broadcast_to(pos[None, :], (32, NLOC)))
    invf = (ROPE_THETA ** (-np.arange(0, 32, 2, dtype=np.float32) / np.float32(32))).astype(np.float32)
    for lay in range(2):
        g = np.zeros((128, 6), np.float32)
        g[:, 0] = np.asarray(inputs["g_qnorm"])[lay]
        g[:, 1] = np.asarray(inputs["g_knorm"])[lay]
        g[:, 2] = np.asarray(inputs["g_mem_qnorm"])[lay][0:128]
        g[:, 3] = np.asarray(inputs["g_mem_qnorm"])[lay][128:256]
        g[0:16, 4] = invf
        g[16:32, 4] = invf
        g[0:16, 5] = -1.0
        g[16:32, 5] = 1.0
        K["gq%d" % lay] = g
        K["g_attn%d" % lay] = np.ascontiguousarray(np.asarray(inputs["g_attn_norm"])[lay].reshape(DC, 128).T)
        K["g_ffn%d" % lay] = np.ascontiguousarray(np.asarray(inputs["g_ffn_norm"])[lay].reshape(DC, 128).T)
    K["g_memn"] = np.ascontiguousarray(np.asarray(inputs["g_mem_norm"]).reshape(DC, 128).T)
    gk = np.asarray(inputs["g_mem_knorm"])
    K["gmk"] = np.ascontiguousarray(np.stack([gk[0][0:128], gk[0][128:256], gk[1][0:128], gk[1][128:256]], axis=1))
    K["lamv"] = np.ascontiguousarray(np.stack([np.asarray(inputs[n])[0] for n in ("lambda_q1", "lambda_k1", "lambda_q2", "lambda_k2")], axis=1))
    K["gsub"] = np.ascontiguousarray(np.broadcast_to(np.asarray(inputs["g_subln"])[0][None, :], (128, 256)))
    dm = np.zeros((128, 8, 128), np.float32)
    kk = np.arange(128)[:, None]
    qq = np.arange(128)[None, :]
    for cp in range(8):
        if cp < c:
            dm[:, cp, :] = 1.0
        elif cp == c:
            dm[:, cp, :] = (kk <= qq).astype(np.float32)
    K["dm"] = dm.reshape(128, 1024).astype(bf)
    cb = np.zeros((8, 32), np.float32)
    cand = np.zeros((8, 32), np.float32)
    own = np.zeros((8, 32), np.float32)
    for j in range(8):
        b0 = 4 * j + c // 2
        cand[j, :b0] = 1.0
        cb[j, b0:] = -1e30
        own[j, b0] = 1.0
    for nm, arr in (("cb", cb), ("cand", cand), ("own", own)):
        K[nm] = np.ascontiguousarray(np.broadcast_to(arr.reshape(1, 256), (128, 256)))
    return K


CONST_SPECS = {
    "ones": ([128, 128], BF16), "ident": ([128, 128], BF16), "perm": ([32, 32], BF16),
    "pos32": ([32, NLOC], I32), "gq0": ([128, 6], F32), "gq1": ([128, 6], F32),
    "g_memn": ([128, DC], F32), "gmk": ([128, 4], F32), "lamv": ([128, 4], F32), "gsub": ([128, 256], F32),
    "dm": ([128, 1024], BF16), "cb": ([128, 256], F32), "cand": ([128, 256], F32), "own": ([128, 256], F32),
    "g_attn0": ([128, DC], F32), "g_attn1": ([128, DC], F32), "g_ffn0": ([128, DC], F32), "g_ffn1": ([128, DC], F32),
}


class Prog:
    def __init__(self):
        self.nc = bass.Bass("TRN2", target_bir_lowering=False)
        self.ins = []
        self.outs = []
        self.T = {}
        self.C = {}

    def t(self, name, shape, dt, kind):
        if kind == "in":
            h = self.nc.dram_tensor(name, list(shape), dt, kind="ExternalInput")
            self.ins.append(name)
        elif kind == "out":
            h = self.nc.dram_tensor(name, list(shape), dt, kind="ExternalOutput")
            self.outs.append(name)
        else:
            h = self.nc.dram_tensor(name, list(shape), dt)
        self.T[name] = h.ap()
        self.T["#" + name] = h
        return h

    def consts(self, names):
        for n in names:
            shape, dt = CONST_SPECS[n]
            h = self.nc.dram_tensor(n, list(shape), dt, kind="ExternalInput")
            self.ins.append(n)
            self.C[n] = h.ap()


def phase_outproj(S, T, C, lay, x_in, x_out):
    S.begin()
    aT = S.sb([128, DC, NLOC], BF16, dma=True, name="aT")
    for q4 in range(4):
        S.dma("sp", aT[:, q4 * 8:(q4 + 1) * 8, :], T["attnT_d"][q4 * 8:(q4 + 1) * 8].rearrange("c p n -> p c n"),
              aT, writes=[aT])
    wv = T["w_out%d" % lay]
    wb = [S.sb([128, DC, 256], BF16, dma=True, name="wb") for _ in range(2)]
    pq = [S.ps([128, 512], name="pq") for _ in range(4)]
    xs = [S.sb([128, 512], F32, dma=True, name="xs") for _ in range(4)]
    k = 0
    for g in range(16):
        w = wb[g % 2]
        load_w(S, w, wv, 0, DC, g * 256, (g + 1) * 256)
        for ch in range(2):
            dc = g * 2 + ch
            for half in range(2):
                hs = slice(half * 512, (half + 1) * 512)
                p = pq[k % 4]
                x = xs[k % 4]
                k += 1
                S.dma("sp", x[:, :], x_in[dc * 128:(dc + 1) * 128, hs], x, writes=[x])
                pairs = [(w[:, c, ch * 128:(ch + 1) * 128], aT[:, c, hs]) for c in range(DC)]
                mm_group(S, p[:, :], pairs, [w, aT], p)
                S.op("dve", lambda e, x=x, p=p: e.tensor_tensor(out=x[:, :], in0=x[:, :], in1=p[:, :], op=ALU.add),
                     reads=[x, p], writes=[x])
                S.dma("sp", x_out[dc * 128:(dc + 1) * 128, hs], x[:, :], x, reads=[x])
    S.end()


def phase_ffn1(S, T, C, lay, x_in):
    S.begin()
    K = load_consts(S, C)
    fT = S.sb([128, DC, NLOC], BF16, name="fT")
    ssb = [S.ps([128, 512], name="ssq") for _ in range(2)]
    phase_norm(S, x_in, C["g_ffn%d" % lay], fT, K["ones"], ssb)
    wg_v = T["w_gate%d" % lay]
    wu_v = T["w_up%d" % lay]
    wg = [S.sb([128, DC, 256], BF16, dma=True, name="wg") for _ in range(2)]
    wu = [S.sb([128, DC, 256], BF16, dma=True, name="wu") for _ in range(2)]
    pg = [S.ps([128, 512], name="pg") for _ in range(3)]
    pu = [S.ps([128, 512], name="pu") for _ in range(3)]
    sg = [S.sb([128, 512], F32, name="sg") for _ in range(3)]
    hb = [S.sb([128, NLOC], BF16, dma=True, name="hb") for _ in range(3)]
    k = 0
    for g in range(43):
        a, b = wg[g % 2], wu[g % 2]
        load_w(S, a, wg_v, 0, DC, g * 256, (g + 1) * 256)
        load_w(S, b, wu_v, 0, DC, g * 256, (g + 1) * 256)
        for ch in range(2):
            fc = g * 2 + ch
            h = hb[fc % 3]
            for half in range(2):
                hs = slice(half * 512, (half + 1) * 512)
                p1, p2, s1 = pg[k % 3], pu[k % 3], sg[k % 3]
                k += 1
                mm_group(S, p1[:, :], [(a[:, c, ch * 128:(ch + 1) * 128], fT[:, c, hs]) for c in range(DC)], [a, fT], p1)
                mm_group(S, p2[:, :], [(b[:, c, ch * 128:(ch + 1) * 128], fT[:, c, hs]) for c in range(DC)], [b, fT], p2)
                S.op("act", lambda e, p1=p1, s1=s1: e.activation(out=s1[:, :], in_=p1[:, :], func=AF.Silu),
                     reads=[p1], writes=[s1])
                S.op("dve", lambda e, h=h, hs=hs, s1=s1, p2=p2: e.tensor_tensor(out=h[:, hs], in0=s1[:, :], in1=p2[:, :],
                                                                                op=ALU.mult), reads=[s1, p2], writes=[h])
            S.dma("sp", T["hff_d"][fc, :, :], h[:, :], h, reads=[h])
    S.end()


def phase_ffn2(S, T, C, lay, x_in, x_out):
    wv = T["w_down%d" % lay]
    for half in range(2):
        hs = slice(half * 512, (half + 1) * 512)
        S.begin()
        hT = S.sb([128, FC, 512], BF16, dma=True, name="hT2")
        for s0 in range(0, FC, 16):
            s1 = min(FC, s0 + 16)
            S.dma("sp", hT[:, s0:s1, :], T["hff_d"][s0:s1, :, hs].rearrange("c p n -> p c n"), hT, writes=[hT])
        wb = [S.sb([128, FC, 256], BF16, dma=True, name="wd") for _ in range(2)]
        pq = [S.ps([128, 512], name="pq") for _ in range(4)]
        xs = [S.sb([128, 512], F32, dma=True, name="xs") for _ in range(4)]
        k = 0
        for g in range(16):
            w = wb[g % 2]
            load_w(S, w, wv, 0, FC, g * 256, (g + 1) * 256, nsplit=8)
            for ch in range(2):
                dc = g * 2 + ch
                p = pq[k % 4]
                x = xs[k % 4]
                k += 1
                S.dma("sp", x[:, :], x_in[dc * 128:(dc + 1) * 128, hs], x, writes=[x])
                mm_group(S, p[:, :], [(w[:, c, ch * 128:(ch + 1) * 128], hT[:, c, :]) for c in range(FC)], [w, hT], p)
                S.op("dve", lambda e, x=x, p=p: e.tensor_tensor(out=x[:, :], in0=x[:, :], in1=p[:, :], op=ALU.add),
                     reads=[x, p], writes=[x])
                S.dma("sp", x_out[dc * 128:(dc + 1) * 128, hs], x[:, :], x, reads=[x])
        S.end()


def phase_memkv(S, T, C):
    S.begin()
    K = load_consts(S, C)
    ones = K["ones"]
    mT = S.sb([128, DC, MEM_LEN], BF16, name="mT")
    xs = [S.sb([128, MEM_LEN], F32, dma=True, name="xs") for _ in range(2)]
    sq = [S.sb([128, MEM_LEN], BF16, name="sq") for _ in range(2)]
    gc = S.sb([128, DC], F32, dma=True, name="gc")
    gmk = S.sb([128, 4], F32, dma=True, name="gmk")
    rstd = S.sb([128, MEM_LEN], F32, name="rstd")
    ss = S.ps([128, 512], name="ss")
    S.dma("sp", gc[:, :], C["g_memn"][:, :], gc, writes=[gc])
    S.dma("sp", gmk[:, :], C["gmk"][:, :], gmk, writes=[gmk])
    for c in range(DC):
        x, q = xs[c % 2], sq[c % 2]
        S.dma("sp", x[:, :], T["memT"][c * 128:(c + 1) * 128, :], x, writes=[x])
        S.op("act", lambda e, x=x, q=q: e.activation(out=q[:, :], in_=x[:, :], func=AF.Square), reads=[x], writes=[q])
        S.op("dve", lambda e, x=x, c=c: e.tensor_scalar(out=mT[:, c, :], in0=x[:, :], scalar1=gc[:, c:c + 1],
                                                        scalar2=None, op0=ALU.mult), reads=[x, gc], writes=[mT])
        S.op("pe", lambda e, q=q, c=c: e.matmul(ss[:, 0:MEM_LEN], lhsT=ones[:, :], rhs=q[:, :], start=(c == 0),
                                                stop=(c == DC - 1)), reads=[q, ones], writes=[ss])
    rstd_from_ss(S, rstd, rstd[:, :], ss, ss[:, 0:MEM_LEN], 1.0 / D, NORM_EPS)
    for c in range(DC):
        S.op("dve", lambda e, c=c: e.tensor_tensor(out=mT[:, c, :], in0=mT[:, c, :], in1=rstd[:, :], op=ALU.mult),
             reads=[mT, rstd], writes=[mT])
    wv = T["w_mem_kv"]
    wb = [S.sb([128, DC, 256], BF16, dma=True, name="wb") for _ in range(2)]
    pk = [S.ps([128, 512], name="pk") for _ in range(2)]
    ss2 = S.ps([128, 512], name="ss2")
    sqk = [S.sb([128, MEM_LEN], BF16, name="sqk") for _ in range(2)]
    kr = [S.sb([128, MEM_LEN], F32, name="kr") for _ in range(2)]
    rs = S.sb([128, MEM_LEN], F32, name="rs")
    ko = [S.sb([128, MEM_LEN], BF16, dma=True, name="ko") for _ in range(4)]
    vo = [S.sb([128, 256], BF16, dma=True, name="vo") for _ in range(2)]
    n = 0
    for g in range(8):
        w = wb[g % 2]
        load_w(S, w, wv, 0, DC, g * 256, (g + 1) * 256)
        if g < 4:
            for ch in range(2):
                p = pk[ch]
                mm_group(S, p[:, 0:MEM_LEN], [(w[:, c, ch * 128:(ch + 1) * 128], mT[:, c, :]) for c in range(DC)], [w, mT], p)
                S.op("dve", lambda e, p=p, ch=ch: e.tensor_copy(out=kr[ch][:, :], in_=p[:, 0:MEM_LEN]),
                     reads=[p], writes=[kr[ch]])
                S.op("act", lambda e, ch=ch: e.activation(out=sqk[ch][:, :], in_=kr[ch][:, :], func=AF.Square),
                     reads=[kr[ch]], writes=[sqk[ch]])
                S.op("pe", lambda e, ch=ch: e.matmul(ss2[:, 0:MEM_LEN], lhsT=ones[:, :], rhs=sqk[ch][:, :], start=(ch == 0),
                                                     stop=(ch == 1)), reads=[sqk[ch], ones], writes=[ss2])
            rstd_from_ss(S, rs, rs[:, :], ss2, ss2[:, 0:MEM_LEN], 1.0 / 256.0, NORM_EPS)
            for lay in range(2):
                for ch in range(2):
                    o = ko[n % 4]
                    n += 1
                    S.op("dve", lambda e, o=o, ch=ch, lay=lay: e.scalar_tensor_tensor(
                        out=o[:, :], in0=kr[ch][:, :], scalar=gmk[:, lay * 2 + ch:lay * 2 + ch + 1], in1=rs[:, :],
                        op0=ALU.mult, op1=ALU.mult), reads=[kr[ch], rs, gmk], writes=[o])
                    S.dma("sp", T["kmhT_d"][lay, g * 2 + ch, :, :], o[:, :], o, reads=[o])
        else:
            hm = g - 4
            for mt in range(2):
                p = pk[mt]
                mm_group(S, p[:, 0:256], [(mT[:, c, mt * 128:(mt + 1) * 128], w[:, c, 0:256]) for c in range(DC)], [w, mT], p)
                o = vo[mt]
                S.op("dve", lambda e, o=o, p=p: e.tensor_copy(out=o[:, :], in_=p[:, 0:256]), reads=[p], writes=[o])
                S.dma("sp", T["mv_d"][mt * 128:(mt + 1) * 128, hm * 256:(hm + 1) * 256], o[:, :], o, reads=[o])
    S.end()


def phase_attn(S, T, C, lay):
    S.begin()
    moba = (lay == 0)
    VW = 132 if moba else 260
    NV = 129 if moba else 257
    scale = 1.0 / math.sqrt(128.0)
    attnT = S.sb([128, DC, NLOC], BF16, dma=True, name="attnT")
    ident = S.sb([128, 128], BF16, dma=True, name="ident")
    ones = S.sb([128, 128], BF16, dma=True, name="ones")
    dm = S.sb([128, 1024], BF16, dma=True, name="dm")
    S.dma("sp", ident[:, :], C["ident"][:, :], ident, writes=[ident])
    S.dma("sp", ones[:, :], C["ones"][:, :], ones, writes=[ones])
    S.dma("sp", dm[:, :], C["dm"][:, :], dm, writes=[dm])
    KT = [S.sb([128, 8192], BF16, dma=True, name="KT") for _ in range(2)]
    VA = [S.sb([128, 64, VW], BF16, dma=True, name="VA") for _ in range(2)]
    QT = [S.sb([128, NLOC], BF16, dma=True, name="QT") for _ in range(2)]
    sT = [S.ps([128, 512], name="sT") for _ in range(3)]
    Ob = [S.ps([128, 512], name="Ob") for _ in range(2)]
    gps = S.ps([128, 512], name="gps")
    tp = S.ps([128, 1024], BF16, name="tp")
    eT = [S.sb([128, 512], BF16, name="eT") for _ in range(4)]
    ef32 = [S.sb([128, 512], F32, name="ef32") for _ in range(3)]
    LOOK = 2
    ot = [S.sb([128, 256], BF16, name="ot") for _ in range(2)]
    cnt = {"g": 0, "o": 0, "t": 0}
    for v in VA:
        S.op("dve", lambda e, v=v: e.memset(v[:, :, VW - 4:VW], 1.0), writes=[v])

    kT_all, v_all, qT_d = T["kT_all%d" % lay], T["v_all%d" % lay], T["qT%d" % lay]

    def load_kq(i, chunk):
        kt, qt = KT[i % 2], QT[i % 2]
        for r0 in range(0, 8, 4):
            S.dma("sp", kt[:, r0 * 1024:(r0 + 4) * 1024].rearrange("p (r n) -> p r n", r=4),
                  kT_all[r0:r0 + 4, chunk, :, :].rearrange("r p n -> p r n"), kt, writes=[kt])
        S.dma("sp", qt[:, :], qT_d[chunk, :, :], qt, writes=[qt])
        return kt, qt

    def load_v(i, col0, width):
        va = VA[i % 2]
        for r in range(8):
            S.dma("sp", va[:, r * 8:(r + 1) * 8, 0:width],
                  v_all[r, :, col0:col0 + width].rearrange("(j p) d -> p j d", p=128), va, writes=[va])
        return va

    def qk_exp(kt, qt, j, grp):
        jp, c0 = grp // 2, 4 * (grp % 2)
        s = sT[cnt["g"] % 3]
        et = eT[cnt["g"] % 4]
        cnt["g"] += 1

        def fn(e):
            ins = None
            for i in range(4):
                k0 = (c0 + i) * 1024 + jp * 128
                ins = e.matmul(s[:, i * 128:(i + 1) * 128], lhsT=kt[:, k0:k0 + 128], rhs=qt[:, j * 128:(j + 1) * 128],
                               start=True, stop=True)
            return ins
        S.op("pe", fn, reads=[kt, qt], writes=[s])
        ef = ef32[cnt["g"] % 3]
        S.op("act", lambda e: e.activation(out=ef[:, :], in_=s[:, :], func=AF.Exp, scale=scale), reads=[s], writes=[ef])
        if jp == j:
            S.op("dve", lambda e: e.tensor_tensor(out=et[:, :], in0=ef[:, :], in1=dm[:, c0 * 128:(c0 + 4) * 128],
                                                  op=ALU.mult), reads=[ef, dm], writes=[et])
        elif moba:
            S.op("act", lambda e: e.activation(out=et[:, :], in_=ef[:, :], func=AF.Copy), reads=[ef], writes=[et])
        else:
            S.op("dve", lambda e: e.tensor_copy(out=et[:, :], in_=ef[:, :]), reads=[ef], writes=[et])
        return et, jp, c0

    def transpose_out(o_ap_list, chunk0, j):
        for i, (ap, sb) in enumerate(o_ap_list):
            t0 = (cnt["t"] % 8) * 128
            cnt["t"] += 1
            S.op("pe", lambda e, ap=ap, t0=t0: e.transpose(tp[:, t0:t0 + 128], ap, ident[:, :]), reads=[sb, ident], writes=[tp])
            S.op("dve", lambda e, t0=t0, i=i: e.tensor_copy(out=attnT[:, chunk0 + i, j * 128:(j + 1) * 128],
                                                            in_=tp[:, t0:t0 + 128]), reads=[tp], writes=[attnT])

    if moba:
        cb = S.sb([128, 256], F32, dma=True, name="cb")
        cand = S.sb([128, 256], F32, dma=True, name="cand")
        own = S.sb([128, 256], F32, dma=True, name="own")
        S.dma("sp", cb[:, :], C["cb"][:, :], cb, writes=[cb])
        S.dma("sp", cand[:, :], C["cand"][:, :], cand, writes=[cand])
        S.dma("sp", own[:, :], C["own"][:, :], own, writes=[own])
        tsum = S.sb([128, 64], F32, name="tsum")
        ksum = S.sb([128, 32], F32, name="ksum")
        khi = S.sb([128, 32], BF16, name="khi")
        klo = S.sb([128, 32], BF16, name="klo")
        gm = S.sb([128, 32], F32, name="gm")
        top8 = S.sb([128, 8], F32, name="top8")
        mps = [S.sb([128, 32], F32, name="mp") for _ in range(2)]
        acc = S.sb([128, 132], F32, name="acc")
        rec = S.sb([128, 1], F32, name="rec")

        def moba_pv(et, jp, c0, va, mp, first_grp):
            for ml in range(2):
                o = Ob[cnt["o"] % 2]
                cnt["o"] += 1
                b = 4 * jp + c0 // 2 + ml

                def fn(e, et=et, o=o, ml=ml, jp=jp, c0=c0, va=va):
                    e.matmul(o[:, 0:NV], lhsT=et[:, (2 * ml) * 128:(2 * ml + 1) * 128],
                             rhs=va[:, (c0 + 2 * ml) * 8 + jp, 0:NV], start=True, stop=False)
                    return e.matmul(o[:, 0:NV], lhsT=et[:, (2 * ml + 1) * 128:(2 * ml + 2) * 128],
                                    rhs=va[:, (c0 + 2 * ml + 1) * 8 + jp, 0:NV], start=False, stop=True)
                S.op("pe", fn, reads=[et, va], writes=[o])
                if first_grp and ml == 0:
                    S.op("dve", lambda e, o=o, b=b, mp=mp: e.tensor_scalar(out=acc[:, 0:NV], in0=o[:, 0:NV], scalar1=mp[:, b:b + 1],
                                                                           scalar2=None, op0=ALU.mult), reads=[o, mp], writes=[acc])
                else:
                    S.op("dve", lambda e, o=o, b=b, mp=mp: e.scalar_tensor_tensor(
                        out=acc[:, 0:NV], in0=o[:, 0:NV], scalar=mp[:, b:b + 1], in1=acc[:, 0:NV],
                        op0=ALU.mult, op1=ALU.add), reads=[o, mp, acc], writes=[acc])

        def moba_fin(h, j):
            S.op("dve", lambda e: e.reciprocal(out=rec[:, :], in_=acc[:, 128:129]), reads=[acc], writes=[rec])
            o2 = ot[j % 2]
            S.op("dve", lambda e, o2=o2: e.tensor_scalar(out=o2[:, 0:128], in0=acc[:, 0:128], scalar1=rec[:, 0:1],
                                                         scalar2=None, op0=ALU.mult), reads=[acc, rec], writes=[o2])
            transpose_out([(o2[:, 0:128], o2)], h, j)

        for h in range(24):
            kt, qt = load_kq(h, h)
            va = load_v(h, h * 128, 128)
            S.op("dve", lambda e, kt=kt: e.reduce_sum(out=tsum[:, :], in_=kt[:, :].rearrange("p (t i) -> p t i", i=128),
                                                      axis=AX.X), reads=[kt], writes=[tsum])
            tv = tsum[:, :].rearrange("p (m two j) -> p m two j", two=2, j=8)
            S.op("dve", lambda e, tv=tv: e.tensor_tensor(out=ksum[:, :].rearrange("p (j m) -> p m j", m=4),
                                                         in0=tv[:, :, 0, :], in1=tv[:, :, 1, :], op=ALU.add),
                 reads=[tsum], writes=[ksum])
            S.op("dve", lambda e: e.tensor_copy(out=khi[:, :], in_=ksum[:, :]), reads=[ksum], writes=[khi])
            S.op("dve", lambda e: e.tensor_tensor(out=klo[:, :], in0=ksum[:, :], in1=khi[:, :], op=ALU.subtract),
                 reads=[ksum, khi], writes=[klo])
            pending = []
            for j in range(8):
                js = slice(j * 32, (j + 1) * 32)
                mp = mps[j % 2]
                mm_group(S, gps[:, 0:32], [(qt[:, j * 128:(j + 1) * 128], khi[:, :]), (qt[:, j * 128:(j + 1) * 128], klo[:, :])],
                         [qt, khi, klo], gps)
                S.op("dve", lambda e, js=js: e.tensor_tensor(out=gm[:, :], in0=gps[:, 0:32], in1=cb[:, js], op=ALU.add),
                     reads=[gps, cb], writes=[gm])
                S.op("dve", lambda e: e.max(out=top8[:, :], in_=gm[:, :]), reads=[gm], writes=[top8])
                S.op("dve", lambda e, js=js, mp=mp: e.scalar_tensor_tensor(out=mp[:, :], in0=gm[:, :], scalar=top8[:, 2:3],
                                                                           in1=cand[:, js], op0=ALU.is_ge, op1=ALU.mult),
                     reads=[gm, top8, cand], writes=[mp])
                S.op("dve", lambda e, js=js, mp=mp: e.tensor_tensor(out=mp[:, :], in0=mp[:, :], in1=own[:, js], op=ALU.add),
                     reads=[mp, own], writes=[mp])
                ng = 2 * (j + 1)
                for grp in range(ng):
                    et, jp, c0 = qk_exp(kt, qt, j, grp)
                    pending.append((et, jp, c0, mp, grp == 0, (j if grp == ng - 1 else None)))
                    if len(pending) > LOOK:
                        it = pending.pop(0)
                        moba_pv(it[0], it[1], it[2], va, it[3], it[4])
                        if it[5] is not None:
                            moba_fin(h, it[5])
            while pending:
                it = pending.pop(0)
                moba_pv(it[0], it[1], it[2], va, it[3], it[4])
                if it[5] is not None:
                    moba_fin(h, it[5])
    else:
        lamv = S.sb([128, 4], F32, dma=True, name="lamv")
        gsub = S.sb([128, 256], F32, dma=True, name="gsub")
        S.dma("sp", lamv[:, :], C["lamv"][:, :], lamv, writes=[lamv])
        S.dma("sp", gsub[:, :], C["gsub"][:, :], gsub, writes=[gsub])
        prod = S.sb([128, 2], F32, name="prod")
        phi = S.sb([128, 2], BF16, name="phi")
        plo = S.sb([128, 2], BF16, name="plo")
        ex = S.sb([128, 2], F32, name="ex")
        nlam = S.sb([128, 1], F32, name="nlam")
        S.op("dve", lambda e: e.tensor_tensor(out=prod[:, :], in0=lamv[:, 0:4:2], in1=lamv[:, 1:4:2], op=ALU.mult),
             reads=[lamv], writes=[prod])
        S.op("dve", lambda e: e.tensor_copy(out=phi[:, :], in_=prod[:, :]), reads=[prod], writes=[phi])
        S.op("dve", lambda e: e.tensor_tensor(out=plo[:, :], in0=prod[:, :], in1=phi[:, :], op=ALU.subtract),
             reads=[prod, phi], writes=[plo])
        mm_group(S, gps[:, 0:2], [(ones[:, :], phi[:, :]), (ones[:, :], plo[:, :])], [ones, phi, plo], gps)
        S.op("act", lambda e: e.activation(out=ex[:, :], in_=gps[:, 0:2], func=AF.Exp), reads=[gps], writes=[ex])
        S.op("dve", lambda e: e.tensor_tensor(out=nlam[:, :], in0=ex[:, 1:2], in1=ex[:, 0:1], op=ALU.subtract),
             reads=[ex], writes=[nlam])
        S.op("dve", lambda e: e.tensor_scalar(out=nlam[:, :], in0=nlam[:, :], scalar1=-LAM_INIT1, scalar2=None, op0=ALU.add),
             reads=[nlam], writes=[nlam])
        osb = S.sb([128, 2, 8, 260], F32, name="osb")
        rr = S.sb([128, 4], F32, name="rr")
        ta = S.sb([128, 256], F32, name="ta")
        tb = S.sb([128, 256], F32, name="tb")
        i = 0
        for hd in range(12):
            va = load_v(hd, hd * 256, 256)
            for comp in range(2):
                kt, qt = load_kq(i, 2 * hd + comp)
                i += 1
                pending = []

                def diff_pv(et, jp, c0, o, grp, ng, comp, j, va=va):
                    def fn(e):
                        ins = None
                        for t in range(4):
                            ins = e.matmul(o[:, 0:NV], lhsT=et[:, t * 128:(t + 1) * 128], rhs=va[:, (c0 + t) * 8 + jp, 0:NV],
                                           start=(grp == 0 and t == 0), stop=(grp == ng - 1 and t == 3))
                        return ins
                    S.op("pe", fn, reads=[et, va], writes=[o])
                    if grp == ng - 1:
                        S.op("dve", lambda e: e.tensor_copy(out=osb[:, comp, j, 0:NV], in_=o[:, 0:NV]), reads=[o], writes=[osb])

                for j in range(8):
                    o = Ob[cnt["o"] % 2]
                    cnt["o"] += 1
                    ng = 2 * (j + 1)
                    for grp in range(ng):
                        et, jp, c0 = qk_exp(kt, qt, j, grp)
                        pending.append((et, jp, c0, o, grp, ng, comp, j))
                        if len(pending) > LOOK:
                            diff_pv(*pending.pop(0))
                while pending:
                    diff_pv(*pending.pop(0))
            for j in range(8):
                S.op("dve", lambda e, j=j: e.reciprocal(out=rr[:, 0:1], in_=osb[:, 0, j, 256:257]), reads=[osb], writes=[rr])
                S.op("dve", lambda e, j=j: e.reciprocal(out=rr[:, 1:2], in_=osb[:, 1, j, 256:257]), reads=[osb, rr], writes=[rr])
                S.op("dve", lambda e: e.tensor_tensor(out=rr[:, 2:3], in0=rr[:, 1:2], in1=nlam[:, 0:1], op=ALU.mult),
                     reads=[rr, nlam], writes=[rr])
                S.op("dve", lambda e, j=j: e.tensor_scalar(out=ta[:, :], in0=osb[:, 0, j, 0:256], scalar1=rr[:, 0:1], scalar2=None,
                                                           op0=ALU.mult), reads=[osb, rr], writes=[ta])
                S.op("dve", lambda e, j=j: e.scalar_tensor_tensor(out=ta[:, :], in0=osb[:, 1, j, 0:256], scalar=rr[:, 2:3],
                                                                  in1=ta[:, :], op0=ALU.mult, op1=ALU.add),
                     reads=[osb, rr, ta], writes=[ta])
                S.op("dve", lambda e: e.tensor_tensor(out=tb[:, :], in0=ta[:, :], in1=ta[:, :], op=ALU.mult), reads=[ta], writes=[tb])
                S.op("dve", lambda e: e.reduce_sum(out=rr[:, 3:4], in_=tb[:, :], axis=AX.X), reads=[tb, rr], writes=[rr])
                rstd_from_ss(S, rr, rr[:, 3:4], rr, rr[:, 3:4], 1.0 / 256.0, SUBLN_EPS)
                S.op("dve", lambda e: e.tensor_scalar(out=ta[:, :], in0=ta[:, :], scalar1=rr[:, 3:4], scalar2=1.0 - LAM_INIT1,
                                                      op0=ALU.mult, op1=ALU.mult), reads=[ta, rr], writes=[ta])
                o2 = ot[j % 2]
                S.op("dve", lambda e, o2=o2: e.tensor_tensor(out=o2[:, :], in0=ta[:, :], in1=gsub[:, :], op=ALU.mult),
                     reads=[ta, gsub], writes=[o2])
                transpose_out([(o2[:, 0:128], o2), (o2[:, 128:256], o2)], 2 * hd, j)

    kmT = KT[0]
    qmT = KT[1]
    mva = VA[0]
    S.dma("sp", kmT[:, 0:2048].rearrange("p (c m) -> p c m", c=8), T["kmhT_d"][lay].rearrange("c p m -> p c m"), kmT, writes=[kmT])
    S.dma("sp", qmT[:, 0:8192].rearrange("p (c n) -> p c n", c=8), T["qmT%d" % lay].rearrange("c p n -> p c n"), qmT, writes=[qmT])
    mvt = S.sb([128, 8, 260], BF16, dma=True, name="mvt")
    S.op("dve", lambda e: e.memset(mvt[:, :, 256:260], 1.0), writes=[mvt])
    for mt in range(2):
        S.dma("sp", mvt[:, mt * 4:(mt + 1) * 4, 0:256], T["mv_d"][mt * 128:(mt + 1) * 128, :].rearrange("p (h d) -> p h d", h=4),
              mvt, writes=[mvt])
    orec = S.sb([128, 1], F32, name="orec")
    for hm in range(4):
        for half in range(2):
            ets = []
            for mt in range(2):
                s = sT[cnt["g"] % 3]
                et = eT[cnt["g"] % 4]
                cnt["g"] += 1
                pairs = [(kmT[:, (2 * hm + ch) * 256 + mt * 128:(2 * hm + ch) * 256 + (mt + 1) * 128],
                          qmT[:, (2 * hm + ch) * 1024 + half * 512:(2 * hm + ch) * 1024 + (half + 1) * 512]) for ch in range(2)]
                mm_group(S, s[:, :], pairs, [kmT, qmT], s)
                ef = ef32[cnt["g"] % 3]
                S.op("act", lambda e, s=s, ef=ef: e.activation(out=ef[:, :], in_=s[:, :], func=AF.Exp, scale=1.0 / 16.0),
                     reads=[s], writes=[ef])
                S.op("dve", lambda e, ef=ef, et=et: e.tensor_copy(out=et[:, :], in_=ef[:, :]), reads=[ef], writes=[et])
                ets.append(et)
            for qt_ in range(4):
                j = half * 4 + qt_
                o = Ob[cnt["o"] % 2]
                cnt["o"] += 1
                pairs = [(ets[mt][:, qt_ * 128:(qt_ + 1) * 128], mvt[:, mt * 4 + hm, 0:257]) for mt in range(2)]
                mm_group(S, o[:, 0:257], pairs, ets + [mvt], o)
                S.op("dve", lambda e, o=o: e.reciprocal(out=orec[:, :], in_=o[:, 256:257]), reads=[o], writes=[orec])
                o2 = ot[j % 2]
                S.op("dve", lambda e, o=o, o2=o2: e.tensor_scalar(out=o2[:, :], in0=o[:, 0:256], scalar1=orec[:, 0:1], scalar2=None,
                                                                  op0=ALU.mult), reads=[o, orec], writes=[o2])
                transpose_out([(o2[:, 0:128], o2), (o2[:, 128:256], o2)], 24 + 2 * hm, j)
    for q4 in range(4):
        S.dma("sp", T["attnT_d"][q4 * 8:(q4 + 1) * 8].rearrange("c p n -> p c n"), attnT[:, q4 * 8:(q4 + 1) * 8, :], attnT,
              reads=[attnT])
    S.end()


W_SHAPES = {"w_in": [PROJ_W // 256, 128, DC, 256], "w_out": [D // 256, 128, DC, 256], "w_gate": [DFF // 256, 128, DC, 256],
            "w_up": [DFF // 256, 128, DC, 256], "w_down": [D // 256, 128, FC, 256]}


def decl_handoff(P, lay, kinds):
    P.t("qT%d" % lay, [24, 128, NLOC], BF16, kinds["qT"])
    P.t("qmT%d" % lay, [8, 128, NLOC], BF16, kinds["qmT"])
    if kinds.get("kT_loc"):
        h = P.t("kT_loc%d" % lay, [24 * 128, NLOC], BF16, kinds["kT_loc"])
        P.T["kT_loc%d" % lay] = h.ap().rearrange("(c p) n -> c p n", p=128)
        P.t("v_loc%d" % lay, [NLOC, SELF_W], BF16, kinds["v_loc"])
    if kinds.get("kT_all"):
        h = P.t("kT_all%d" % lay, [8 * 24 * 128, NLOC], BF16, kinds["kT_all"])
        P.T["kT_all%d" % lay] = h.ap().rearrange("(r c p) n -> r c p n", r=8, p=128)
        h = P.t("v_all%d" % lay, [8 * NLOC, SELF_W], BF16, kinds["v_all"])
        P.T["v_all%d" % lay] = h.ap().rearrange("(r n) f -> r n f", r=8)


def emit_tail(S, P, lay, x_in, x_out):
    phase_attn(S, P.T, P.C, lay)
    phase_outproj(S, P.T, P.C, lay, x_in, P.T["xT_mid"])
    phase_ffn1(S, P.T, P.C, lay, P.T["xT_mid"])
    phase_ffn2(S, P.T, P.C, lay, P.T["xT_mid"], x_out)


def decl_scratch(P, dbg=False):
    k = "out" if dbg else "int"
    P.t("attnT_d", [DC, 128, NLOC], BF16, k)
    P.t("hff_d", [FC, 128, NLOC], BF16, "int")
    P.t("xT_mid", [D, NLOC], F32, k)
    P.t("kmhT_d", [2, 8, 128, MEM_LEN], BF16, k)
    P.t("mv_d", [MEM_LEN, MEM_W], BF16, k)


def build_A(lay=0):
    P = Prog()
    P.consts(["ones", "perm", "pos32", "gq%d" % lay, "g_attn%d" % lay])
    P.t("xT_in%d" % lay, [D, NLOC], F32, "in")
    P.t("w_in%d" % lay, W_SHAPES["w_in"], F32, "in")
    decl_handoff(P, lay, {"qT": "out", "qmT": "out", "kT_loc": "out", "v_loc": "out"})
    with ExitStack() as st:
        S = Sched(P.nc, st)
        phase_inproj(S, P.T, P.C, lay)
    return P


def build_B(lay, with_next, dbg=False):
    P = Prog()
    cn = ["ones", "ident", "dm", "g_ffn%d" % lay, "g_memn", "gmk"]
    cn += ["cb", "cand", "own"] if lay == 0 else ["lamv", "gsub"]
    if with_next:
        cn += ["perm", "pos32", "gq%d" % (lay + 1), "g_attn%d" % (lay + 1)]
    P.consts(cn)
    P.t("xT_in%d" % lay, [D, NLOC], F32, "in")
    P.t("memT", [D, MEM_LEN], F32, "in")
    P.t("w_mem_kv", [8, 128, DC, 256], F32, "in")
    for w in ("w_out", "w_gate", "w_up", "w_down"):
        P.t("%s%d" % (w, lay), W_SHAPES[w], F32, "in")
    decl_handoff(P, lay, {"qT": "in", "qmT": "in", "kT_all": "in", "v_all": "in"})
    decl_scratch(P, dbg)
    P.t("xT_in%d" % (lay + 1), [D, NLOC], F32, "out")
    if with_next:
        P.t("w_in%d" % (lay + 1), W_SHAPES["w_in"], F32, "in")
        decl_handoff(P, lay + 1, {"qT": "out", "qmT": "out", "kT_loc": "out", "v_loc": "out"})
    with ExitStack() as st:
        S = Sched(P.nc, st)
        phase_memkv(S, P.T, P.C)
        emit_tail(S, P, lay, P.T["xT_in%d" % lay], P.T["xT_in%d" % (lay + 1)])
        if with_next:
            phase_inproj(S, P.T, P.C, lay + 1)
    return P


def phase_gather(S, P, lay):
    S.begin()
    a, b = Buf(None), Buf(None)
    S.allgather(P.T["#kT_loc%d" % lay], P.T["#kT_all%d" % lay], [], [a])
    S.allgather(P.T["#v_loc%d" % lay], P.T["#v_all%d" % lay], [], [b])
    S.end()


def build_fused():
    P = Prog()
    P.consts(list(CONST_SPECS.keys()))
    P.t("xT_in0", [D, NLOC], F32, "in")
    P.t("memT", [D, MEM_LEN], F32, "in")
    P.t("w_mem_kv", [8, 128, DC, 256], F32, "in")
    for lay in range(2):
        for w in ("w_in", "w_out", "w_gate", "w_up", "w_down"):
            P.t("%s%d" % (w, lay), W_SHAPES[w], F32, "in")
        decl_handoff(P, lay, {"qT": "int", "qmT": "int", "kT_loc": "int", "v_loc": "int", "kT_all": "int", "v_all": "int"})
    decl_scratch(P)
    P.t("xT_in1", [D, NLOC], F32, "int")
    P.t("xT_in2", [D, NLOC], F32, "out")
    with ExitStack() as st:
        S = Sched(P.nc, st)
        phase_memkv(S, P.T, P.C)
        for lay in range(2):
            phase_inproj(S, P.T, P.C, lay)
            phase_gather(S, P, lay)
            emit_tail(S, P, lay, P.T["xT_in%d" % lay], P.T["xT_in%d" % (lay + 1)])
    return P


FUSED = False


def _tile_w(w):
    K_, F_ = w.shape
    return np.ascontiguousarray(w.reshape(K_ // 128, 128, F_ // 256, 256).transpose(2, 1, 0, 3))


def _weights(inputs, P):
    m = {}
    for n in P.ins:
        for w in ("w_in", "w_out", "w_gate", "w_up", "w_down"):
            if n.startswith(w) and n[len(w):] in ("0", "1"):
                m[n] = _tile_w(np.asarray(inputs[w])[int(n[len(w):])])
    if "w_mem_kv" in P.ins:
        m["w_mem_kv"] = _tile_w(np.asarray(inputs["w_mem_kv"]))
    if "memT" in P.ins:
        m["memT"] = np.ascontiguousarray(np.asarray(inputs["mem"])[0].T)
    return m


def _run(P, maps):
    res = run_bass_kernel_spmd(P.nc, maps, core_ids=list(range(NCORES)))
    return res.results


def kernel(**inputs):
    x = np.asarray(inputs["x"])[0]
    hcs = [host_consts(c, inputs) for c in range(NCORES)]
    xT = [np.ascontiguousarray(x[core_token_index(c)].T) for c in range(NCORES)]
    if FUSED:
        P = build_fused()
        w = _weights(inputs, P)
        maps = []
        for c in range(NCORES):
            m = {n: hcs[c][n] for n in P.ins if n in hcs[c]}
            m.update(w)
            m["xT_in0"] = xT[c]
            maps.append(m)
        outs = _run(P, maps)
        fin = [np.asarray(o["xT_in2"]) for o in outs]
    else:
        PA = build_A(0)
        w = _weights(inputs, PA)
        maps = []
        for c in range(NCORES):
            m = {n: hcs[c][n] for n in PA.ins if n in hcs[c]}
            m.update(w)
            m["xT_in0"] = xT[c]
            maps.append(m)
        prev = _run(PA, maps)
        cur_x = xT
        for lay in range(2):
            PB = build_B(lay, with_next=(lay == 0))
            w = _weights(inputs, PB)
            kT_all = np.concatenate([np.asarray(prev[c]["kT_loc%d" % lay]) for c in range(NCORES)], axis=0)
            v_all = np.concatenate([np.asarray(prev[c]["v_loc%d" % lay]) for c in range(NCORES)], axis=0)
            maps = []
            for c in range(NCORES):
                m = {n: hcs[c][n] for n in PB.ins if n in hcs[c]}
                m.update(w)
                m["xT_in%d" % lay] = cur_x[c]
                m["qT%d" % lay] = np.asarray(prev[c]["qT%d" % lay])
                m["qmT%d" % lay] = np.asarray(prev[c]["qmT%d" % lay])
                m["kT_all%d" % lay] = kT_all
                m["v_all%d" % lay] = v_all
                maps.append(m)
            prev = _run(PB, maps)
            cur_x = [np.asarray(prev[c]["xT_in%d" % (lay + 1)]) for c in range(NCORES)]
        fin = cur_x
    out = np.zeros((SEQ, D), np.float32)
    for c in range(NCORES):
        out[core_token_index(c)] = fin[c].T
    return out[None]
```
